# Optimizing a Trainium2 kernel written in Bass

```python
import math
import jax, jax.numpy as jnp
from jax import lax
import numpy as np

D_MODEL = 1024
BATCH = 8
SEQ = 4096
DEPTH = 1

CHUNK = 64
SSM_WIDTH = D_MODEL // 2
SSM_GROUP = 16
SSM_GROUPS = SSM_WIDTH // SSM_GROUP
SSM_STATE = 64
CONV_WIDTH = D_MODEL // 2
CONV_K = 3
FFN_HIDDEN = ((8 * D_MODEL // 3 + 255) // 256) * 256
IN_COLS = SSM_WIDTH + 3 * CONV_WIDTH + 2 * D_MODEL
ALPHA = (2.0 * DEPTH) ** 0.25
BETA = (8.0 * DEPTH) ** -0.25
DT_MIN = 0.001
DT_MAX = 0.1
LN_EPS = 1e-5

kernel_name = "hybrid_s5_shortconv_gated_deepnorm_block"


def layer_norm(x, g, b):
    xf = x.astype(jnp.float32)
    mu = jnp.mean(xf, axis=-1, keepdims=True)
    var = jnp.mean(jnp.square(xf - mu), axis=-1, keepdims=True)
    y = (xf - mu) * lax.rsqrt(var + LN_EPS) * g.astype(jnp.float32) + b.astype(jnp.float32)
    return y.astype(x.dtype)


def _complex_affine_combine(earlier, later):
    ar_i, ai_i, br_i, bi_i = earlier
    ar_j, ai_j, br_j, bi_j = later
    ar = ar_j * ar_i - ai_j * ai_i
    ai = ar_j * ai_i + ai_j * ar_i
    br = ar_j * br_i - ai_j * bi_i + br_j
    bi = ar_j * bi_i + ai_j * br_i + bi_j
    return (ar, ai, br, bi)


def s5_ssm(u, lam_re, lam_im, log_dt, b_re, b_im, c_re, c_im, d_skip):
    f32 = jnp.float32
    bsz, slen, _ = u.shape
    uf = u.astype(f32).reshape(bsz, slen, SSM_GROUPS, SSM_GROUP).transpose(1, 0, 2, 3)
    lr = lam_re.astype(f32)
    li = lam_im.astype(f32)
    dt = jnp.exp(log_dt.astype(f32))[:, None]
    mag = jnp.exp(lr * dt)
    lb_re = mag * jnp.cos(li * dt)
    lb_im = mag * jnp.sin(li * dt)
    den = lr * lr + li * li
    num_re = lb_re - 1.0
    fr = (num_re * lr + lb_im * li) / den
    fi = (lb_im * lr - num_re * li) / den
    br = b_re.astype(f32)
    bi = b_im.astype(f32)
    bb_re = fr[..., None] * br - fi[..., None] * bi
    bb_im = fr[..., None] * bi + fi[..., None] * br
    bu_re = jnp.einsum('sbgc,gpc->sbgp', uf, bb_re)
    bu_im = jnp.einsum('sbgc,gpc->sbgp', uf, bb_im)
    a_re = jnp.broadcast_to(lb_re[None, None], (slen, 1, SSM_GROUPS, SSM_STATE))
    a_im = jnp.broadcast_to(lb_im[None, None], (slen, 1, SSM_GROUPS, SSM_STATE))
    _, _, xs_re, xs_im = lax.associative_scan(
        _complex_affine_combine, (a_re, a_im, bu_re, bu_im), axis=0)
    y = (jnp.einsum('sbgp,gcp->sbgc', xs_re, c_re.astype(f32))
         - jnp.einsum('sbgp,gcp->sbgc', xs_im, c_im.astype(f32))
         + d_skip.astype(f32).reshape(SSM_GROUPS, SSM_GROUP) * uf)
    y = y.transpose(1, 0, 2, 3).reshape(bsz, slen, SSM_WIDTH)
    return y.astype(u.dtype)


def causal_depthwise_conv(z, w):
    return lax.conv_general_dilated(
        z, w[:, None, :].astype(z.dtype), window_strides=(1,), padding=[(CONV_K - 1, 0)],
        dimension_numbers=('NWC', 'WIO', 'NWC'), feature_group_count=CONV_WIDTH)


def hybrid_layer(x, w_in, b_in, ssm_lambda_re, ssm_lambda_im, ssm_log_dt, ssm_b_re, ssm_b_im,
                 ssm_c_re, ssm_c_im, ssm_d, glu_w, glu_b, w_ssm_out, conv_w, w_conv_out, w_o,
                 ln1_g, ln1_b, w_gate, w_up, w_down, ln2_g, ln2_b):
    proj = jnp.einsum('bsd,dn->bsn', x, w_in) + b_in
    o1 = SSM_WIDTH
    o2 = o1 + CONV_WIDTH
    o3 = o2 + CONV_WIDTH
    o4 = o3 + CONV_WIDTH
    o5 = o4 + D_MODEL
    u, h, c_gate, b_gate, gate_a, gate_b = jnp.split(proj, [o1, o2, o3, o4, o5], axis=-1)

    y_a = s5_ssm(u, ssm_lambda_re, ssm_lambda_im, ssm_log_dt, ssm_b_re, ssm_b_im,
                 ssm_c_re, ssm_c_im, ssm_d)
    g = jax.nn.gelu(y_a)
    y_a = g * jax.nn.sigmoid(jnp.einsum('bsc,ce->bse', g, glu_w) + glu_b)
    y_a = jnp.einsum('bsc,cd->bsd', y_a, w_ssm_out)

    z = causal_depthwise_conv(c_gate * h, conv_w)
    y_b = jnp.einsum('bsc,cd->bsd', b_gate * z, w_conv_out)

    merged = jax.nn.sigmoid(gate_a) * y_a + jax.nn.sigmoid(gate_b) * y_b
    mix = jnp.einsum('bsd,de->bse', merged, w_o)
    x = layer_norm(ALPHA * x + mix, ln1_g, ln1_b)

    hid = jax.nn.silu(jnp.einsum('bsd,df->bsf', x, w_gate)) * jnp.einsum('bsd,df->bsf', x, w_up)
    ffn = jnp.einsum('bsf,fd->bsd', hid, w_down)
    x = layer_norm(ALPHA * x + ffn, ln2_g, ln2_b)
    return x


def setup_inputs(seed: int = 0) -> dict:
    key = jax.random.key(seed)
    ks = jax.random.split(key, 24)
    L = DEPTH
    f32 = jnp.float32
    nrm = lambda k, shape, s: jax.random.normal(k, shape, f32) * s
    x = jax.random.normal(ks[0], (BATCH, SEQ, D_MODEL), f32)
    w_in = nrm(ks[1], (L, D_MODEL, IN_COLS), D_MODEL ** -0.5)
    b_in = nrm(ks[2], (L, IN_COLS), 0.01)
    n_idx = jnp.arange(SSM_STATE, dtype=f32)
    ssm_lambda_re = -0.5 + nrm(ks[3], (L, SSM_GROUPS, SSM_STATE), 0.01)
    ssm_lambda_im = math.pi * n_idx[None, None, :] + nrm(ks[4], (L, SSM_GROUPS, SSM_STATE), 0.01)
    ssm_log_dt = jax.random.uniform(ks[5], (L, SSM_GROUPS), f32,
                                    minval=math.log(DT_MIN), maxval=math.log(DT_MAX))
    ssm_b_re = nrm(ks[6], (L, SSM_GROUPS, SSM_STATE, SSM_GROUP), (2.0 * SSM_GROUP) ** -0.5)
    ssm_b_im = nrm(ks[7], (L, SSM_GROUPS, SSM_STATE, SSM_GROUP), (2.0 * SSM_GROUP) ** -0.5)
    ssm_c_re = nrm(ks[8], (L, SSM_GROUPS, SSM_GROUP, SSM_STATE), SSM_STATE ** -0.5)
    ssm_c_im = nrm(ks[9], (L, SSM_GROUPS, SSM_GROUP, SSM_STATE), SSM_STATE ** -0.5)
    ssm_d = nrm(ks[10], (L, SSM_WIDTH), 1.0)
    glu_w = nrm(ks[11], (L, SSM_WIDTH, SSM_WIDTH), SSM_WIDTH ** -0.5)
    glu_b = nrm(ks[12], (L, SSM_WIDTH), 0.01)
    w_ssm_out = nrm(ks[13], (L, SSM_WIDTH, D_MODEL), BETA * SSM_WIDTH ** -0.5)
    conv_w = nrm(ks[14], (L, CONV_K, CONV_WIDTH), CONV_K ** -0.5)
    w_conv_out = nrm(ks[15], (L, CONV_WIDTH, D_MODEL), BETA * CONV_WIDTH ** -0.5)
    w_o = nrm(ks[16], (L, D_MODEL, D_MODEL), BETA * D_MODEL ** -0.5)
    ln1_g = 1.0 + nrm(ks[17], (L, D_MODEL), 0.01)
    ln1_b = nrm(ks[18], (L, D_MODEL), 0.01)
    w_gate = nrm(ks[19], (L, D_MODEL, FFN_HIDDEN), D_MODEL ** -0.5)
    w_up = nrm(ks[20], (L, D_MODEL, FFN_HIDDEN), D_MODEL ** -0.5)
    w_down = nrm(ks[21], (L, FFN_HIDDEN, D_MODEL), BETA * FFN_HIDDEN ** -0.5)
    ln2_g = 1.0 + nrm(ks[22], (L, D_MODEL), 0.01)
    ln2_b = nrm(ks[23], (L, D_MODEL), 0.01)
    return {"x": x, "w_in": w_in, "b_in": b_in,
            "ssm_lambda_re": ssm_lambda_re, "ssm_lambda_im": ssm_lambda_im,
            "ssm_log_dt": ssm_log_dt, "ssm_b_re": ssm_b_re, "ssm_b_im": ssm_b_im,
            "ssm_c_re": ssm_c_re, "ssm_c_im": ssm_c_im, "ssm_d": ssm_d,
            "glu_w": glu_w, "glu_b": glu_b, "w_ssm_out": w_ssm_out,
            "conv_w": conv_w, "w_conv_out": w_conv_out, "w_o": w_o,
            "ln1_g": ln1_g, "ln1_b": ln1_b, "w_gate": w_gate, "w_up": w_up,
            "w_down": w_down, "ln2_g": ln2_g, "ln2_b": ln2_b}


def reference(x, w_in, b_in, ssm_lambda_re, ssm_lambda_im, ssm_log_dt, ssm_b_re, ssm_b_im,
              ssm_c_re, ssm_c_im, ssm_d, glu_w, glu_b, w_ssm_out, conv_w, w_conv_out, w_o,
              ln1_g, ln1_b, w_gate, w_up, w_down, ln2_g, ln2_b):
    for l in range(DEPTH):
        x = hybrid_layer(x, w_in[l], b_in[l], ssm_lambda_re[l], ssm_lambda_im[l], ssm_log_dt[l],
                         ssm_b_re[l], ssm_b_im[l], ssm_c_re[l], ssm_c_im[l], ssm_d[l],
                         glu_w[l], glu_b[l], w_ssm_out[l], conv_w[l], w_conv_out[l], w_o[l],
                         ln1_g[l], ln1_b[l], w_gate[l], w_up[l], w_down[l], ln2_g[l], ln2_b[l])
    return x
```

```python
import math
import contextlib
import numpy as np
import concourse.bass as bass
import concourse.mybir as mybir
from concourse.bass_utils import run_bass_kernel_spmd

F32 = mybir.dt.float32
BF16 = mybir.dt.bfloat16
I32 = mybir.dt.int32
U8 = mybir.dt.uint8
ALU = mybir.AluOpType
AF = mybir.ActivationFunctionType

D = 1024
SEQ = 4096
NCORES = 8
FFN = 2816
NFB = FFN // 128
ALPHA = 2.0 ** 0.25
LN_EPS = 1e-5
TWO_PI = 2.0 * math.pi
MAGIC = 12582912.0
SIN_SCALE = 1.0 - 2e-6

ENGS = ("tensor", "vector", "scalar", "gpsimd", "sync")


class _Op:
    __slots__ = ("eng", "fn", "deps", "is_dma", "dma_key", "dma_target", "signal", "seq", "tag")

    def __init__(self, eng, fn, is_dma=False):
        self.eng = eng
        self.fn = fn
        self.deps = []
        self.is_dma = is_dma
        self.dma_key = None
        self.dma_target = 0
        self.signal = False
        self.seq = 0
        self.tag = ''


class Prog:
    def __init__(self, nc):
        self.nc = nc
        self.ops = {e: [] for e in ENGS}
        self.last_writer = {}
        self.readers = {}
        self.dma_counts = {}
        self.last_dma = {}
        self.all_ops = []
        self.tag = ''

    def _add_dep(self, op, dep):
        if dep is None or dep is op:
            return
        if (dep.eng == op.eng and not op.is_dma and not dep.is_dma
                and op.eng in ("tensor",)):
            return
        op.deps.append(dep)
        if not dep.is_dma:
            dep.signal = True

    def op(self, eng, fn, reads=(), writes=(), dma_key=None):
        is_dma = dma_key is not None
        o = _Op(eng, fn, is_dma)
        o.tag = self.tag
        for t in reads:
            self._add_dep(o, self.last_writer.get(t))
        for t in writes:
            self._add_dep(o, self.last_writer.get(t))
            for r in self.readers.get(t, ()):
                self._add_dep(o, r)
        for t in reads:
            self.readers.setdefault(t, []).append(o)
        for t in writes:
            self.last_writer[t] = o
            self.readers[t] = []
        if is_dma:
            c = self.dma_counts.get(dma_key, 0) + 16
            self.dma_counts[dma_key] = c
            o.dma_key = dma_key
            o.dma_target = c
            self.last_dma[dma_key] = o
        self.ops[eng].append(o)
        self.all_ops.append(o)
        return o

    def fence(self, fence_fns):
        fs = []
        for e, fn in fence_fns.items():
            o = _Op(e, fn)
            o.signal = True
            self.ops[e].append(o)
            self.all_ops.append(o)
            fs.append(o)
        dm = [o for k, o in self.last_dma.items() if not str(k).startswith('cv_')]
        for e in ENGS:
            g = _Op(e, None)
            g.deps = [f for f in fs] + dm
            self.ops[e].append(g)
            self.all_ops.append(g)

    def emit(self, final_wait_eng="sync"):
        nc = self.nc
        fin = _Op(final_wait_eng, None)
        fin.deps = list(self.last_dma.values())
        self.ops[final_wait_eng].append(fin)
        for e in ENGS:
            c = 0
            for o in self.ops[e]:
                if o.signal and not o.is_dma:
                    c += 1
                    o.seq = c
        with contextlib.ExitStack() as st:
            esem = {e: st.enter_context(nc.semaphore("es_" + e)) for e in ENGS}
            dsem = {}
            for i, k in enumerate(self.dma_counts):
                dsem[k] = st.enter_context(nc.semaphore("ds_%d" % i))
            block = st.enter_context(nc.Block())
            ops = self.ops

            def make(e):
                def body(eng):
                    waited = {}
                    for o in ops[e]:
                        for d in o.deps:
                            if d.is_dma:
                                s, v, k = dsem[d.dma_key], d.dma_target, ("d", d.dma_key)
                            else:
                                s, v, k = esem[d.eng], d.seq, ("e", d.eng)
                            if waited.get(k, 0) >= v:
                                continue
                            waited[k] = v
                            eng.wait_ge(s, v)
                        if o.fn is None:
                            continue
                        ins = o.fn(eng)
                        if o.is_dma:
                            ins.then_inc(dsem[o.dma_key], 16)
                        elif o.signal:
                            ins.then_inc(esem[e], 1)
                return body

            block.tensor(make("tensor"))
            block.vector(make("vector"))
            block.scalar(make("scalar"))
            block.gpsimd(make("gpsimd"))
            block.sync(make("sync"))


class Arena:
    def __init__(self, big, nbytes):
        self.big = big
        self.n = nbytes
        self.top = 0

    def buf(self, shape, dt, parts=128):
        esz = {F32: 4, BF16: 2, I32: 4}[dt]
        n = int(np.prod(shape)) * esz
        n_al = (n + 63) // 64 * 64
        off = self.top
        assert off + n_al <= self.n, ("arena overflow", off, n_al, self.n)
        self.top += n_al
        ap = self.big[0:parts, off:off + n].bitcast(dt)
        if len(shape) > 1:
            names = " ".join("d%d" % i for i in range(len(shape)))
            kw = {"d%d" % i: int(s) for i, s in enumerate(shape)}
            ap = ap.rearrange("p (%s) -> p %s" % (names, names), **kw)
        return ap

    def mark(self):
        return self.top

    def release(self, m):
        self.top = m


def bc(ap, shape):
    return ap.broadcast_to(list(shape))


def build_nc(debug=False):
    nc = bass.Bass("TRN2", target_bir_lowering=False)

    def din(name, shape):
        return nc.dram_tensor(name, list(shape), F32, kind="ExternalInput").ap()

    x = din("x", [SEQ, D])
    w_in = din("w_in", [D, 4096])
    b_in = din("b_in", [4096])
    lam_re = din("ssm_lambda_re", [32, 64])
    lam_im = din("ssm_lambda_im", [32, 64])
    log_dt = din("ssm_log_dt", [32])
    b_re = din("ssm_b_re", [32, 64, 16])
    b_im = din("ssm_b_im", [32, 64, 16])
    c_re = din("ssm_c_re", [32, 16, 64])
    c_im = din("ssm_c_im", [32, 16, 64])
    ssm_d = din("ssm_d", [512])
    glu_w = din("glu_w", [512, 512])
    glu_b = din("glu_b", [512])
    w_ssm_out = din("w_ssm_out", [512, D])
    conv_w = din("conv_w", [3, 512])
    w_conv_out = din("w_conv_out", [512, D])
    w_o = din("w_o", [D, D])
    ln1_g = din("ln1_g", [D])
    ln1_b = din("ln1_b", [D])
    w_gate = din("w_gate", [D, FFN])
    w_up = din("w_up", [D, FFN])
    w_down = din("w_down", [FFN, D])
    ln2_g = din("ln2_g", [D])
    ln2_b = din("ln2_b", [D])
    out = nc.dram_tensor("out", [SEQ, D], F32, kind="ExternalOutput").ap()

    win_r = nc.dram_tensor("win_r", [D, 3584], BF16, kind="Internal").ap()
    wcv_s = nc.dram_tensor("wcv_s", [4, 128, 8, 384], BF16, kind="Internal").ap()
    wgt_s = nc.dram_tensor("wgt_s", [8, 128, 8, 256], BF16, kind="Internal").ap()
    wgu_s = nc.dram_tensor("wgu_s", [NFB, 128, 2, 8, 128], BF16, kind="Internal").ap()
    wd_s = nc.dram_tensor("wd_s", [FFN, D], BF16, kind="Internal").ap()
    g_s = nc.dram_tensor("g_s", [512, SEQ], BF16, kind="ExternalOutput" if debug else "Internal").ap()

    ARENA_BYTES = 206 * 1024
    with contextlib.ExitStack() as st:
        big = st.enter_context(nc.sbuf_tensor("arena", [128, ARENA_BYTES], U8))
        psb = [st.enter_context(nc.psum_tensor("ps%d" % i, [128, 512], F32)) for i in range(8)]
        A = Arena(big, ARENA_BYTES)
        P = Prog(nc)

        ps_rr = [0]

        def nextps():
            i = ps_rr[0]
            ps_rr[0] = (i + 1) % 8
            return i, psb[i][:], "ps%d" % i

        def psbf(i):
            return psb[i][:].bitcast(BF16)

        def V(fn, reads, writes):
            return P.op("vector", fn, reads, writes)

        def S(fn, reads, writes):
            return P.op("scalar", fn, reads, writes)

        def G(fn, reads, writes):
            return P.op("gpsimd", fn, reads, writes)

        def T(fn, reads, writes):
            return P.op("tensor", fn, reads, writes)

        dma_ctr = [0]

        def DMA(fn, reads, writes, key=None, eng="sync"):
            if key is None:
                key = "dma%d" % dma_ctr[0]
                dma_ctr[0] += 1
            return P.op(eng, fn, reads, writes, dma_key=key)

        ident_f = A.buf([128], F32)
        ident_b = A.buf([128], BF16)
        iota_i = A.buf([128], I32)
        fence_v = A.buf([1], F32)
        fence_s = A.buf([1], F32)
        fence_g = A.buf([1], F32)
        bias_fm = A.buf([32], F32)
        bias_u = A.buf([512], F32)
        glub_fm = A.buf([4], F32)
        g1_fm = A.buf([8], F32)
        b1_fm = A.buf([8], F32)
        convw_fm = A.buf([12], F32)
        vbuf = A.buf([4, 514], F32)
        r8tab = A.buf([16], F32)
        C1t = A.buf([16, 2], F32)
        C2t = A.buf([16, 2], F32)
        wcar = A.buf([16, 2], F32)
        eps_t = A.buf([1], F32)
        ones_b = A.buf([128], BF16)

        fence_fns = {
            "vector": lambda e: e.memset(fence_v, 0.0),
            "gpsimd": lambda e: e.memset(fence_g, 0.0),
            "scalar": lambda e: e.activation(out=fence_s, in_=ident_f[:, 0:1], func=AF.Copy),
        }

        mA = A.mark()
        winu = A.buf([8, 512], BF16)
        DMA(lambda e: e.dma_start(out=winu, in_=w_in[:, 0:512].rearrange("(k p) n -> p k n", p=128)),
            [], ["winu"], eng="gpsimd")
        def convert_chunk(c):
            for r in (2 * c, 2 * c + 1):
                rs = slice(r * 128, (r + 1) * 128)
                DMA((lambda rs: lambda e: e.dma_start(out=win_r[rs, :].rearrange("r (c e) -> r c e", e=896),
                                                      in_=w_in[rs, 512:4096].rearrange("r (c e) -> r c e", e=896)))(rs),
                    [], ["win_r%d" % c], key="cv_win%d" % c, eng="gpsimd")
            for r in range(c * 6, min(NFB, (c + 1) * 6)):
                rs = slice(r * 128, (r + 1) * 128)
                DMA((lambda rs: lambda e: e.dma_start(out=wd_s[rs, :], in_=w_down[rs, :]))(rs),
                    [], ["wd_s"], key="cv_wd", eng="gpsimd")

        def retile_chunk(c):
            for k in (2 * c, 2 * c + 1):
                rs = slice(k * 128, (k + 1) * 128)
                for wh in range(3):
                    DMA((lambda rs, k, wh: lambda e: e.dma_start(
                        out=wcv_s[:, :, k, wh * 128:(wh + 1) * 128].rearrange("c p n -> p c n"),
                        in_=win_r[rs, wh * 512:(wh + 1) * 512].rearrange("p (c n) -> p c n", n=128)))(rs, k, wh),
                        ["win_r%d" % c], ["wcv_s"], key="cv2_win")
                for wh in range(2):
                    DMA((lambda rs, k, wh: lambda e: e.dma_start(
                        out=wgt_s[:, :, k, wh * 128:(wh + 1) * 128].rearrange("c p n -> p c n"),
                        in_=win_r[rs, 1536 + wh * 1024: 1536 + (wh + 1) * 1024].rearrange("p (c n) -> p c n", n=128)))(rs, k, wh),
                        ["win_r%d" % c], ["wgt_s"], key="cv2_wgt")

        P.tag = 'P0'
        G(lambda e: e.iota(iota_i, pattern=[[1, 128]], base=0, channel_multiplier=-1), [], ["iota_i"])
        V(lambda e: e.tensor_scalar(out=ident_f, in0=iota_i, scalar1=0.0, scalar2=None, op0=ALU.is_equal), ["iota_i"], ["ident_f"])
        V(lambda e: e.tensor_copy(out=ident_b, in_=ident_f), ["ident_f"], ["ident_b"])
        V(lambda e: e.memset(vbuf, 0.0), [], ["vbuf0", "vbuf1", "vbuf2", "vbuf3"])

        DMA(lambda e: e.dma_start(out=bias_u, in_=b_in[0:512].partition_broadcast(128)), [], ["bias_u"])

        Ec = A.buf([16, 128], F32)
        Es = A.buf([16, 128], F32)
        M0 = A.buf([32, 128], BF16)
        Hpad = A.buf([32, 2, 128], BF16)
        Gpad = A.buf([32, 2, 128], BF16)
        m0 = A.mark()
        nat = A.buf([128], F32)
        DMA(lambda e: e.dma_start(out=nat[0:32, :], in_=b_in.rearrange("(c p) -> c p", p=128)), [], ["nat_bin"])
        nat2 = A.buf([128], F32)
        DMA(lambda e: e.dma_start(out=nat2[0:4, :], in_=glu_b.rearrange("(c p) -> c p", p=128)), [], ["nat_glub"])
        nat3 = A.buf([128], F32)
        DMA(lambda e: e.dma_start(out=nat3[0:12, :], in_=conv_w.rearrange("k (c p) -> (k c) p", p=128)), [], ["nat_convw"])

        def small_T(dst, src_nat, K, rtok, wtok):
            i, ps, pt = nextps()
            T(lambda e: e.transpose(out=ps[:, 0:K], in_=src_nat[0:K, :], identity=ident_f[0:K, 0:K]), [rtok, "ident_f"], [pt])
            V(lambda e: e.tensor_copy(out=dst, in_=ps[:, 0:K]), [pt], [wtok])

        small_T(bias_fm, nat, 32, "nat_bin", "bias_fm")
        nat4 = A.buf([128], F32)
        DMA(lambda e: e.dma_start(out=nat4[0:8, :], in_=ln1_g.rearrange("(c p) -> c p", p=128)), [], ["nat_g1"])
        nat5 = A.buf([128], F32)
        DMA(lambda e: e.dma_start(out=nat5[0:8, :], in_=ln1_b.rearrange("(c p) -> c p", p=128)), [], ["nat_b1"])
        small_T(g1_fm, nat4, 8, "nat_g1", "g1_fm")
        small_T(b1_fm, nat5, 8, "nat_b1", "b1_fm")
        small_T(glub_fm, nat2, 4, "nat_glub", "glub_fm")
        small_T(convw_fm, nat3, 12, "nat_convw", "convw_fm")

        lr = A.buf([16], F32)
        li = A.buf([16], F32)
        ldt = A.buf([16], F32)
        Br = A.buf([16, 16], F32)
        Bi = A.buf([16, 16], F32)
        Cr = A.buf([16, 16], F32)
        Ci = A.buf([16, 16], F32)
        cnat_r = A.buf([4, 64], F32)
        cnat_i = A.buf([4, 64], F32)
        C64r = A.buf([512], F32)
        C64i = A.buf([512], F32)
        dT = A.buf([32], F32)
        for g2 in range(2):
            ps_ = slice(g2 * 64, (g2 + 1) * 64)
            DMA((lambda ps_, g2: lambda e: e.dma_start(out=lr[ps_, :], in_=lam_re.rearrange("(gh g2) p -> g2 p gh", g2=2)[g2],
                                                      allow_slow_non_contiguous=True))(ps_, g2), [], ["lr"], key="ld_lr")
            DMA((lambda ps_, g2: lambda e: e.dma_start(out=li[ps_, :], in_=lam_im.rearrange("(gh g2) p -> g2 p gh", g2=2)[g2],
                                                      allow_slow_non_contiguous=True))(ps_, g2), [], ["li"], key="ld_li")
            DMA((lambda ps_, g2: lambda e: e.dma_start(out=ldt[ps_, :], in_=bass.AP(tensor=log_dt.tensor, offset=g2, ap=[[0, 64], [2, 16]]),
                                                      allow_slow_non_contiguous=True))(ps_, g2), [], ["ldt"], key="ld_ldt")
            DMA((lambda ps_, g2: lambda e: e.dma_start(out=Br[ps_], in_=b_re.rearrange("(gh g2) p c -> g2 p gh c", g2=2)[g2]))(ps_, g2),
                [], ["Br"], key="ld_Br")
            DMA((lambda ps_, g2: lambda e: e.dma_start(out=Bi[ps_], in_=b_im.rearrange("(gh g2) p c -> g2 p gh c", g2=2)[g2]))(ps_, g2),
                [], ["Bi"], key="ld_Bi")
        DMA(lambda e: e.dma_start(out=cnat_r, in_=c_re.rearrange("g c p -> (g c) p").rearrange("(i r) p -> r i p", r=128)), [], ["cnat_r"])
        DMA(lambda e: e.dma_start(out=cnat_i, in_=c_im.rearrange("g c p -> (g c) p").rearrange("(i r) p -> r i p", r=128)), [], ["cnat_i"])
        DMA(lambda e: e.dma_start(out=dT[0:16, :], in_=ssm_d.rearrange("(g c) -> c g", c=16), allow_slow_non_contiguous=True), [], ["dT"])

        for cnat, C64, ctok, C128 in ((cnat_r, C64r, "cnat_r", Cr), (cnat_i, C64i, "cnat_i", Ci)):
            i, ps, pt = nextps()
            for t4 in range(4):
                T((lambda ps, cnat, t4: lambda e: e.transpose(out=ps[0:64, t4 * 128:(t4 + 1) * 128], in_=cnat[:, t4, :], identity=ident_f))(ps, cnat, t4),
                  [ctok, "ident_f"], [pt])
            S((lambda ps, C64: lambda e: e.copy(out=C64[0:64, :], in_=ps[0:64, :]))(ps, C64), [pt], [ctok + "64"])
            for g2 in range(2):
                DMA((lambda C64, C128, g2: lambda e: e.dma_start(
                    out=C128[g2 * 64:(g2 + 1) * 64],
                    in_=C64[0:64, :].rearrange("p (gh g2 c) -> p gh g2 c", g2=2, c=16)[:, :, g2, :]))(C64, C128, g2),
                    [ctok + "64"], [ctok + "128"], key="shuf_" + ctok)

        ev9 = A.buf([9], F32)
        evD = A.buf([8], F32)
        jv = A.buf([128], F32)
        ev_i = A.buf([128], I32)
        G(lambda e: e.iota(ev_i, pattern=[[1, 128]], base=0, channel_multiplier=0), [], ["ev_i"])
        V(lambda e: e.tensor_copy(out=jv, in_=ev_i), ["ev_i"], ["jv"])
        V(lambda e: e.tensor_copy(out=ev9, in_=ev_i[:, 0:9]), ["ev_i"], ["ev9"])
        V(lambda e: e.tensor_scalar(out=evD, in0=jv[:, 0:8], scalar1=-1.0, scalar2=7.0, op0=ALU.mult, op1=ALU.add), ["jv"], ["evD"])

        dt_t = A.buf([16], F32)
        a_t = A.buf([16], F32)
        th_t = A.buf([16], F32)
        S(lambda e: e.activation(out=dt_t, in_=ldt, func=AF.Exp), ["ldt"], ["dt_t"])
        V(lambda e: e.tensor_tensor(out=a_t, in0=lr, in1=dt_t, op=ALU.mult), ["lr", "dt_t"], ["a_t"])
        V(lambda e: e.tensor_tensor(out=th_t, in0=li, in1=dt_t, op=ALU.mult), ["li", "dt_t"], ["th_t"])

        def powtab(ev, n, nm, th_src=th_t, th_tok="th_t", lead=16, with_mag=True):
            ang = A.buf([lead, n], F32)
            kk = A.buf([lead, n], F32)
            rr = A.buf([lead, n], F32)
            rc = A.buf([lead, n], F32)
            sn = A.buf([lead, n], F32)
            cs = A.buf([lead, n], F32)
            thb = bc(th_src[:, :, None] if len(th_src.shape) == 2 else th_src, [128, lead, n])
            evb = bc(ev[:, None, :], [128, lead, n])
            V(lambda e: e.tensor_tensor(out=ang, in0=thb, in1=evb, op=ALU.mult), [th_tok, "ev9", "evD", "jv"], [nm + "ang"])
            V(lambda e: e.tensor_scalar(out=kk, in0=ang, scalar1=1.0 / TWO_PI, scalar2=MAGIC, op0=ALU.mult, op1=ALU.add), [nm + "ang"], [nm + "kk"])
            V(lambda e: e.tensor_scalar(out=kk, in0=kk, scalar1=-MAGIC, scalar2=None, op0=ALU.add), [nm + "kk"], [nm + "kk"])
            V(lambda e: e.scalar_tensor_tensor(out=rr, in0=kk, scalar=-TWO_PI, in1=ang, op0=ALU.mult, op1=ALU.add), [nm + "kk", nm + "ang"], [nm + "rr"])
            V(lambda e: e.tensor_scalar(out=rc, in0=rr, scalar1=math.pi / 2, scalar2=None, op0=ALU.add), [nm + "rr"], [nm + "rc"])
            V(lambda e: e.tensor_scalar(out=kk, in0=rc, scalar1=math.pi, scalar2=-TWO_PI, op0=ALU.is_gt, op1=ALU.mult), [nm + "rc", nm + "rr"], [nm + "kk"])
            V(lambda e: e.tensor_tensor(out=rc, in0=rc, in1=kk, op=ALU.add), [nm + "rc", nm + "kk"], [nm + "rc"])
            S(lambda e: e.activation(out=sn, in_=rr, func=AF.Sin, scale=SIN_SCALE), [nm + "rr"], [nm + "sn"])
            S(lambda e: e.activation(out=cs, in_=rc, func=AF.Sin, scale=SIN_SCALE), [nm + "rc"], [nm + "cs"])
            if not with_mag:
                return cs, sn, rr, None
            ea = A.buf([lead, n], F32)
            mag = A.buf([lead, n], F32)
            pre = A.buf([lead, n], F32)
            pim = A.buf([lead, n], F32)
            ab = bc(a_t[:, :, None], [128, lead, n])
            V(lambda e: e.tensor_tensor(out=ea, in0=ab, in1=evb, op=ALU.mult), ["a_t", "ev9", "evD"], [nm + "ea"])
            S(lambda e: e.activation(out=mag, in_=ea, func=AF.Exp), [nm + "ea"], [nm + "mag"])
            V(lambda e: e.tensor_tensor(out=pre, in0=mag, in1=cs, op=ALU.mult), [nm + "mag", nm + "cs"], [nm + "pre"])
            V(lambda e: e.tensor_tensor(out=pim, in0=mag, in1=sn, op=ALU.mult), [nm + "mag", nm + "sn"], [nm + "pim"])
            return pre, pim, rr, mag

        pA_re, pA_im, rA, magA = powtab(ev9, 9, "pA")
        pD_re, pD_im, _, _ = powtab(evD, 8, "pD")

        fr = A.buf([16], F32)
        fi = A.buf([16], F32)
        t1 = A.buf([16], F32)
        t2 = A.buf([16], F32)
        t3 = A.buf([16], F32)
        den = A.buf([16], F32)
        nre = A.buf([16], F32)
        lbre = pA_re[:, :, 1]
        lbim = pA_im[:, :, 1]
        V(lambda e: e.tensor_scalar(out=nre, in0=lbre, scalar1=-1.0, scalar2=None, op0=ALU.add), ["pApre"], ["nre"])
        V(lambda e: e.tensor_tensor(out=t1, in0=lr, in1=lr, op=ALU.mult), ["lr"], ["t1"])
        V(lambda e: e.tensor_tensor(out=t2, in0=li, in1=li, op=ALU.mult), ["li"], ["t2"])
        V(lambda e: e.tensor_tensor(out=den, in0=t1, in1=t2, op=ALU.add), ["t1", "t2"], ["den"])
        V(lambda e: e.reciprocal(out=den, in_=den), ["den"], ["den"])
        V(lambda e: e.tensor_tensor(out=t1, in0=nre, in1=lr, op=ALU.mult), ["nre", "lr", "den"], ["t1"])
        V(lambda e: e.tensor_tensor(out=t2, in0=lbim, in1=li, op=ALU.mult), ["pApim", "li", "den"], ["t2"])
        V(lambda e: e.tensor_tensor(out=t3, in0=t1, in1=t2, op=ALU.add), ["t1", "t2"], ["t3"])
        V(lambda e: e.tensor_tensor(out=fr, in0=t3, in1=den, op=ALU.mult), ["t3", "den"], ["fr"])
        V(lambda e: e.tensor_tensor(out=t1, in0=lbim, in1=lr, op=ALU.mult), ["pApim", "lr", "t3"], ["t1"])
        V(lambda e: e.tensor_tensor(out=t2, in0=nre, in1=li, op=ALU.mult), ["nre", "li", "t3"], ["t2"])
        V(lambda e: e.tensor_tensor(out=t3, in0=t1, in1=t2, op=ALU.subtract), ["t1", "t2", "fr"], ["t3"])
        V(lambda e: e.tensor_tensor(out=fi, in0=t3, in1=den, op=ALU.mult), ["t3", "den"], ["fi"])

        bbr = A.buf([16, 16], F32)
        bbi = A.buf([16, 16], F32)
        u1 = A.buf([16, 16], F32)
        u2 = A.buf([16, 16], F32)
        frb = bc(fr[:, :, None], [128, 16, 16])
        fib = bc(fi[:, :, None], [128, 16, 16])
        V(lambda e: e.tensor_tensor(out=u1, in0=frb, in1=Br, op=ALU.mult), ["fr", "Br"], ["u1"])
        V(lambda e: e.tensor_tensor(out=u2, in0=fib, in1=Bi, op=ALU.mult), ["fi", "Bi"], ["u2"])
        V(lambda e: e.tensor_tensor(out=bbr, in0=u1, in1=u2, op=ALU.subtract), ["u1", "u2"], ["bbr"])
        V(lambda e: e.tensor_tensor(out=u1, in0=frb, in1=Bi, op=ALU.mult), ["fr", "Bi", "bbr"], ["u1"])
        V(lambda e: e.tensor_tensor(out=u2, in0=fib, in1=Br, op=ALU.mult), ["fi", "Br", "bbr"], ["u2"])
        V(lambda e: e.tensor_tensor(out=bbi, in0=u1, in1=u2, op=ALU.add), ["u1", "u2"], ["bbi"])

        HTD_re = A.buf([16, 8, 16], F32)
        HTD_im = A.buf([16, 8, 16], F32)
        GA_re = A.buf([16, 9, 16], F32)
        GA_nim = A.buf([16, 9, 16], F32)
        w1 = A.buf([16, 9, 16], F32)
        w2 = A.buf([16, 9, 16], F32)

        def cplx_tab(eng_a, eng_b, pr, pi_, ptok, xr, xi, xtok, n, o_re, o_second, second_mode, otok):
            prb = bc(pr[:, :, :, None], [128, 16, n, 16])
            pib = bc(pi_[:, :, :, None], [128, 16, n, 16])
            xrb = bc(xr[:, :, None, :], [128, 16, n, 16])
            xib = bc(xi[:, :, None, :], [128, 16, n, 16])
            a1 = w1[:, :, 0:n, :]
            a2 = w2[:, :, 0:n, :]
            a3 = a1
            a4 = a2
            P.op(eng_a, lambda e: e.tensor_tensor(out=a1, in0=prb, in1=xrb, op=ALU.mult), [ptok + "pre", xtok[0]], ["w1"])
            P.op(eng_a, lambda e: e.tensor_tensor(out=a2, in0=pib, in1=xib, op=ALU.mult), [ptok + "pim", xtok[1]], ["w2"])
            P.op(eng_a, lambda e: e.tensor_tensor(out=o_re, in0=a1, in1=a2, op=ALU.subtract), ["w1", "w2"], [otok + "_re"])
            P.op(eng_b, lambda e: e.tensor_tensor(out=a3, in0=prb, in1=xib, op=ALU.mult), [ptok + "pre", xtok[1]], ["w1"])
            P.op(eng_b, lambda e: e.tensor_tensor(out=a4, in0=pib, in1=xrb, op=ALU.mult), [ptok + "pim", xtok[0]], ["w2"])
            if second_mode > 0:
                P.op(eng_b, lambda e: e.tensor_tensor(out=o_second, in0=a3, in1=a4, op=ALU.add), ["w1", "w2"], [otok + "_2"])
            else:
                P.op(eng_b, lambda e: e.tensor_scalar(out=a3, in0=a3, scalar1=-1.0, scalar2=None, op0=ALU.mult), ["w1"], ["w1"])
                P.op(eng_b, lambda e: e.tensor_tensor(out=o_second, in0=a3, in1=a4, op=ALU.subtract), ["w1", "w2"], [otok + "_2"])

        cplx_tab("vector", "gpsimd", pD_re, pD_im, "pD", bbr, bbi, ("bbr", "bbi"), 8, HTD_re, HTD_im, +1, "HTD")
        cplx_tab("vector", "gpsimd", pA_re, pA_im, "pA", Cr, Ci, ("cnat_r128", "cnat_i128"), 9, GA_re, GA_nim, -1, "GA")

        V(lambda e: e.tensor_copy(out=r8tab, in_=magA[:, :, 8]), ["pAmag"], ["r8tab"])
        psi = rA[:, :, 8]
        ang128 = A.buf([16], F32)
        k128 = A.buf([16], F32)
        r128 = A.buf([16], F32)
        rc128 = A.buf([16], F32)
        sn128 = A.buf([16], F32)
        cs128 = A.buf([16], F32)
        V(lambda e: e.tensor_scalar(out=ang128, in0=psi, scalar1=128.0, scalar2=None, op0=ALU.mult), ["pArr"], ["ang128"])
        V(lambda e: e.tensor_scalar(out=k128, in0=ang128, scalar1=1.0 / TWO_PI, scalar2=MAGIC, op0=ALU.mult, op1=ALU.add), ["ang128"], ["k128"])
        V(lambda e: e.tensor_scalar(out=k128, in0=k128, scalar1=-MAGIC, scalar2=None, op0=ALU.add), ["k128"], ["k128"])
        V(lambda e: e.scalar_tensor_tensor(out=r128, in0=k128, scalar=-TWO_PI, in1=ang128, op0=ALU.mult, op1=ALU.add), ["k128", "ang128"], ["r128"])
        V(lambda e: e.tensor_scalar(out=rc128, in0=r128, scalar1=math.pi / 2, scalar2=None, op0=ALU.add), ["r128"], ["rc128"])
        V(lambda e: e.tensor_scalar(out=k128, in0=rc128, scalar1=math.pi, scalar2=-TWO_PI, op0=ALU.is_gt, op1=ALU.mult), ["rc128", "r128"], ["k128"])
        V(lambda e: e.tensor_tensor(out=rc128, in0=rc128, in1=k128, op=ALU.add), ["rc128", "k128"], ["rc128"])
        S(lambda e: e.activation(out=sn128, in_=r128, func=AF.Sin, scale=SIN_SCALE), ["r128"], ["sn128"])
        S(lambda e: e.activation(out=cs128, in_=rc128, func=AF.Sin, scale=SIN_SCALE), ["rc128"], ["cs128"])
        V(lambda e: e.tensor_copy(out=C1t, in_=bc(cs128[:, :, None], [128, 16, 2])), ["cs128"], ["C1t"])
        V(lambda e: e.tensor_scalar(out=C2t[:, :, 0], in0=sn128, scalar1=-1.0, scalar2=None, op0=ALU.mult), ["sn128"], ["C2t"])
        V(lambda e: e.tensor_copy(out=C2t[:, :, 1], in_=sn128), ["sn128", "C2t"], ["C2t"])
        V(lambda e: e.memset(wcar, 0.0), [], ["wcar"])

        angj = w1.rearrange('p a b c -> p (a b c)')[:, 0:2048].rearrange('p (a b) -> p a b', a=16)
        kj = w2.rearrange('p a b c -> p (a b c)')[:, 0:2048].rearrange('p (a b) -> p a b', a=16)
        rj = angj
        rcj = kj
        psib = bc(psi[:, :, None], [128, 16, 128])
        jvb = bc(jv[:, None, :], [128, 16, 128])
        V(lambda e: e.tensor_tensor(out=angj, in0=psib, in1=jvb, op=ALU.mult), ["pArr", "jv", "GA_re", "GA_2", "HTD_re", "HTD_2"], ["w1"])
        V(lambda e: e.tensor_scalar(out=kj, in0=angj, scalar1=1.0 / TWO_PI, scalar2=MAGIC, op0=ALU.mult, op1=ALU.add), ["w1", "GA_re", "GA_2", "HTD_re", "HTD_2"], ["w2"])
        V(lambda e: e.tensor_scalar(out=kj, in0=kj, scalar1=-MAGIC, scalar2=None, op0=ALU.add), ["w2"], ["w2"])
        V(lambda e: e.scalar_tensor_tensor(out=rj, in0=kj, scalar=-TWO_PI, in1=angj, op0=ALU.mult, op1=ALU.add), ["w2", "w1"], ["w1"])
        V(lambda e: e.tensor_scalar(out=Ec, in0=rj, scalar1=math.pi / 2, scalar2=None, op0=ALU.add), ["w1"], ["Ec"])
        V(lambda e: e.tensor_scalar(out=kj, in0=Ec, scalar1=math.pi, scalar2=-TWO_PI, op0=ALU.is_gt, op1=ALU.mult), ["Ec", "w1"], ["w2"])
        V(lambda e: e.tensor_tensor(out=Ec, in0=Ec, in1=kj, op=ALU.add), ["Ec", "w2"], ["Ec"])
        S(lambda e: e.activation(out=Es, in_=rj, func=AF.Sin, scale=SIN_SCALE), ["w1"], ["Es"])
        S(lambda e: e.activation(out=Ec, in_=Ec, func=AF.Sin, scale=SIN_SCALE), ["Ec"], ["Ec"])

        V(lambda e: e.memset(Hpad, 0.0), [], ["Hpad"])
        V(lambda e: e.memset(Gpad, 0.0), [], ["Gpad"])
        V(lambda e: e.memset(M0, 0.0), [], ["M0"])
        cnt = 0
        for reim, (HT, htok) in enumerate(((HTD_re, "HTD_re"), (HTD_im, "HTD_2"))):
            for ghq in range(4):
                i, ps, pt = nextps()
                for l in range(4):
                    gh = ghq * 4 + l
                    T((lambda ps, HT, gh, l: lambda e: e.transpose(out=ps[:, l * 128:(l + 1) * 128],
                                                                   in_=HT[:, gh].rearrange("p s c -> p (s c)"), identity=ident_f))(ps, HT, gh, l),
                      [htok, "ident_f"], [pt])
                for l in range(4):
                    gh = ghq * 4 + l
                    for g2 in range(2):
                        fn = (lambda ps, gh, g2, l, reim: lambda e: e.tensor_copy(
                            out=Hpad[:, 2 * gh + g2, reim, g2 * 64:(g2 + 1) * 64],
                            in_=ps[:, l * 128 + g2 * 64: l * 128 + (g2 + 1) * 64]))(ps, gh, g2, l, reim)
                        if cnt % 2 == 0:
                            V(fn, [pt, "Hpad"], ["Hpad"])
                        else:
                            S((lambda ps, gh, g2, l, reim: lambda e: e.copy(
                                out=Hpad[:, 2 * gh + g2, reim, g2 * 64:(g2 + 1) * 64],
                                in_=ps[:, l * 128 + g2 * 64: l * 128 + (g2 + 1) * 64]))(ps, gh, g2, l, reim), [pt, "Hpad"], ["Hpad"])
                        cnt += 1
        for reim, (GT, gtok) in enumerate(((GA_re, "GA_re"), (GA_nim, "GA_2"))):
            for g2 in range(2):
                pp = slice(g2 * 64, (g2 + 1) * 64)
                V((lambda GT, pp, g2, reim: lambda e: e.tensor_copy(
                    out=Gpad[pp].rearrange("p (gh g2) r n -> p gh g2 r n", g2=2)[:, :, g2, reim, :],
                    in_=GT[pp, :, 1:9, :].rearrange("p g t c -> p g (t c)")))(GT, pp, g2, reim), [gtok, "Gpad"], ["Gpad"])
        BBr = A.buf([32, 16], F32)
        BBi = A.buf([32, 16], F32)
        Kcb = A.buf([32, 128], BF16)
        dd = A.buf([32, 16], F32)
        V(lambda e: e.memset(BBr, 0.0), [], ["BBr"])
        V(lambda e: e.memset(BBi, 0.0), [], ["BBi"])
        for BB, src, stok, btok in ((BBr, bbr, "bbr", "BBr"), (BBi, bbi, "bbi", "BBi")):
            for g2 in range(2):
                pp = slice(g2 * 64, (g2 + 1) * 64)
                V((lambda BB, src, pp, g2: lambda e: e.tensor_copy(
                    out=BB[pp].rearrange("p (gh g2) c -> p gh g2 c", g2=2)[:, :, g2, :], in_=src[pp]))(BB, src, pp, g2),
                  [stok, btok], [btok])
        V(lambda e: e.tensor_tensor(out=dd[0:16], in0=bc(ident_f[0:16, None, 0:16], [16, 32, 16]), in1=bc(dT[0:16, :, None], [16, 32, 16]), op=ALU.mult),
          ["ident_f", "dT"], ["dd"])
        for gq in range(8):
            i, ps, pt = nextps()
            for l in range(4):
                g = gq * 4 + l
                gh = g // 2
                T((lambda ps, g, gh, l: lambda e: e.matmul(ps[0:16, l * 128:(l + 1) * 128], lhsT=BBr[:, g, :],
                                                          rhs=GA_re[:, gh, 0:8, :].rearrange("p t c -> p (t c)"), start=True, stop=False))(ps, g, gh, l),
                  ["BBr", "GA_re"], [pt])
                T((lambda ps, g, gh, l: lambda e: e.matmul(ps[0:16, l * 128:(l + 1) * 128], lhsT=BBi[:, g, :],
                                                          rhs=GA_nim[:, gh, 0:8, :].rearrange("p t c -> p (t c)"), start=False, stop=True))(ps, g, gh, l),
                  ["BBi", "GA_2"], [pt])
            V((lambda ps, gq: lambda e: e.tensor_copy(out=Kcb[0:16, gq * 4:(gq + 1) * 4, :].rearrange("p g n -> p (g n)"), in_=ps[0:16, :]))(ps, gq),
              [pt], ["Kcb"])
            V((lambda ps, gq: lambda e: e.tensor_tensor(out=Kcb[0:16, gq * 4:(gq + 1) * 4, 0:16],
                                                        in0=ps[0:16, :].rearrange("p (g n) -> p g n", g=4)[:, :, 0:16],
                                                        in1=dd[0:16, gq * 4:(gq + 1) * 4, :], op=ALU.add))(ps, gq), [pt, "dd", "Kcb"], ["Kcb"])
        for s8 in range(8):
            DMA((lambda s8: lambda e: e.dma_start(out=M0[16 * s8:16 * s8 + 16, :, 16 * s8:128], in_=Kcb[0:16, :, 0:128 - 16 * s8]))(s8),
                ["Kcb", "M0"], ["M0"], key="m0asm")

        P.fence(fence_fns)
        A.release(m0)
        xa = A.buf([2, D], F32)
        xb = A.buf([8, D], BF16)
        xT = A.buf([8, 1024], BF16)
        u_oct = A.buf([32, 8, 16], BF16)
        U8s = [A.buf([32, 128], BF16) for _ in range(2)]
        wins = [A.buf([4, 2, 128], F32) for _ in range(2)]
        Wts = [A.buf([4, 2, 128], F32) for _ in range(2)]
        Xb = A.buf([16, 2, 129], BF16)
        g_oct = A.buf([8, 512], BF16)
        g_fms = [A.buf([1024], BF16) for _ in range(2)]
        rts = [[A.buf([4, 128], F32) for _ in range(4)] for _ in range(2)]
        ctas = [A.buf([4, 2], F32) for _ in range(2)]
        ctbs = [A.buf([4, 2], F32) for _ in range(2)]
        brow_f = A.buf([512], F32)
        brow_b = A.buf([512], BF16)
        V(lambda e: e.memset(Xb, 0.0), [], ["Xb0", "Xb1", "Xb2", "Xb3"])
        V(lambda e: e.memset(ones_b, 0.0), [], ["ones_b"])
        V(lambda e: e.memset(ones_b[0:1, :], 1.0), ["ones_b"], ["ones_b"])
        V(lambda e: e.memset(brow_b, 0.0), [], ["brow_b"])
        DMA(lambda e: e.dma_start(out=brow_f[0:1, :], in_=b_in[0:512].partition_broadcast(1)), [], ["brow_f"])
        V(lambda e: e.tensor_copy(out=brow_b[0:1, :], in_=brow_f[0:1, :]), ["brow_f", "brow_b"], ["brow_b"])
        cvs = [A.buf([2, 8, 128], BF16) for _ in range(2)]
        cv_state = {"n": 0, "pend": None}

        def ffn_store(fb, sl):
            DMA((lambda fb, sl: lambda e: e.dma_start(out=wgu_s[fb], in_=cvs[sl]))(fb, sl), ["fcs%d" % sl], ["wgu%d" % fb], key="fcst%d" % sl, eng="gpsimd")

        def ffn_convert(fbs):
            for fb in fbs:
                sl = cv_state["n"] % 2
                cv_state["n"] += 1
                for wh, wsrc in enumerate((w_gate, w_up)):
                    DMA((lambda sl, fb, wh, wsrc: lambda e: e.dma_start(
                        out=cvs[sl][:, wh], in_=wsrc[:, fb * 128:(fb + 1) * 128].rearrange("(k p) n -> p k n", p=128)))(sl, fb, wh, wsrc),
                        [], ["fcs%d" % sl], key="fcs%d" % sl, eng="gpsimd")
                if cv_state["pend"] is not None:
                    ffn_store(*cv_state["pend"])
                cv_state["pend"] = (fb, sl)

        print("ARENA top (phase A)", A.top, "of", A.n)
        wt_ctr = [0]
        gfm_ctr = [0]

        def load_xb(stile):
            for hh in range(4):
                r0 = stile * 1024 + hh * 256
                DMA((lambda r0: lambda e: e.dma_start(out=xa, in_=x[r0:r0 + 256, :].rearrange("(s p) d -> p s d", p=128)))(r0), [], ["xa"], key="xald")
                for sl in range(2):
                    sub = hh * 2 + sl
                    S((lambda sl, sub: lambda e: e.copy(out=xb[:, sub, :], in_=xa[:, sl, :]))(sl, sub), ["xa"], ["xb%d" % sub])

        def phaseA_front(stile):
            P.tag = 'A_fe%d' % stile
            t0 = stile * 1024
            U8t = U8s[stile % 2]
            ut = "U8_%d_" % (stile % 2)
            if stile == 0:
                load_xb(0)
            ffn_convert({0: range(0, 6), 1: range(6, 12), 2: range(12, 17), 3: range(17, 22)}[stile])
            if stile == 3:
                ffn_store(*cv_state["pend"])
            for hh in range(2):
                for k in range(8):
                    i, ps, pt = nextps()
                    pb = psbf(i)
                    for sl in range(4):
                        sub = hh * 4 + sl
                        T((lambda pb, sub, sl, k: lambda e: e.transpose(out=pb[:, sl * 128:(sl + 1) * 128], in_=xb[:, sub, k * 128:(k + 1) * 128],
                                                                        identity=ident_b))(pb, sub, sl, k), ["xb%d" % sub, "ident_b"], [pt])
                    S((lambda pb, k, hh: lambda e: e.copy(out=xT[:, k, hh * 512:(hh + 1) * 512], in_=pb[:, 0:512]))(pb, k, hh), [pt], ["xT%d" % k])
            if stile + 1 < 4:
                load_xb(stile + 1)
            if stile == 0:
                convert_chunk(0)
                convert_chunk(1)
            elif stile == 1:
                convert_chunk(2)
                convert_chunk(3)
            elif stile == 2:
                retile_chunk(0)
                retile_chunk(1)
                retile_chunk(2)
            else:
                retile_chunk(3)
            for s8 in range(8):
                i, ps, pt = nextps()
                for k in range(8):
                    T((lambda ps, s8, k: lambda e: e.matmul(ps, lhsT=xT[:, k, :].rearrange("p (j s) -> p s j", s=8)[:, s8, :],
                                                           rhs=winu[:, k, :], start=(k == 0), stop=False))(ps, s8, k),
                      ["xT%d" % k, "winu"], [pt])
                T((lambda ps: lambda e: e.matmul(ps, lhsT=ones_b, rhs=brow_b, start=False, stop=True))(ps), ["ones_b", "brow_b"], [pt])
                S((lambda ps, s8: lambda e: e.copy(out=u_oct[:, :, s8, :], in_=ps.rearrange("p (g c) -> p g c", c=16)))(ps, s8),
                  [pt], ["u_oct%d" % s8])
            for gq in range(4):
                i, ps, pt = nextps()
                pb = psbf(i)
                for l in range(8):
                    g = gq * 8 + l
                    T((lambda pb, g, l: lambda e: e.transpose(out=pb[:, l * 128:(l + 1) * 128], in_=u_oct[:, g].rearrange("p s c -> p (s c)"),
                                                              identity=ident_b))(pb, g, l), ["u_oct%d" % s for s in range(8)] + ["ident_b"], [pt])
                S((lambda pb, gq, U8t: lambda e: e.copy(out=U8t[:, gq * 8:(gq + 1) * 8, :].rearrange("p g j -> p (g j)"), in_=pb))(pb, gq, U8t),
                  [pt], [ut + "%d" % gq])

        def phaseA_scan(stile):
            P.tag = 'A_V%d' % stile
            U8t = U8s[stile % 2]
            ut = "U8_%d_" % (stile % 2)
            for pair in range(2):
                qs = (2 * pair, 2 * pair + 1)
                ctx = []
                for si, q in enumerate(qs):
                    ir, psr, ptr = nextps()
                    ii, psi_, pti = nextps()
                    for l in range(4):
                        gh = q * 4 + l
                        for reim, ps in ((0, psr), (1, psi_)):
                            for g2 in range(2):
                                g = 2 * gh + g2
                                T((lambda ps, g, reim, l, g2: lambda e: e.matmul(ps[:, l * 128:(l + 1) * 128], lhsT=Hpad[:, g, reim, :], rhs=U8t[:, g, :],
                                                                                 start=(g2 == 0), stop=(g2 == 1)))(ps, g, reim, l, g2),
                                  ["Hpad", ut + "%d" % (g // 8)], [ptr if reim == 0 else pti])
                    gsl = slice(q * 4, (q + 1) * 4)
                    ctx.append(dict(q=q, si=si, gsl=gsl, ptr=ptr, pti=pti,
                                    vr=psr.rearrange("p (g j) -> p g j", g=4), vi=psi_.rearrange("p (g j) -> p g j", g=4),
                                    ec=Ec[:, gsl, :], es=Es[:, gsl, :], win=wins[si], Wt=Wts[si], r=rts[si],
                                    wint="win%d" % si, wtok="Wt%d" % si, rt=["rt%d_%d" % (si, i_) for i_ in range(4)],
                                    cta=ctas[si], ctb=ctbs[si], ctt=["cta%d" % si, "ctb%d" % si], xtk="Xb%d" % q))
                for c in ctx:
                    V((lambda c: lambda e: e.tensor_tensor(out=c["r"][0], in0=c["vr"], in1=c["ec"], op=ALU.mult))(c), [c["ptr"], "Ec"], [c["rt"][0]])
                for c in ctx:
                    V((lambda c: lambda e: e.tensor_tensor(out=c["r"][1], in0=c["vi"], in1=c["es"], op=ALU.mult))(c), [c["pti"], "Es"], [c["rt"][1]])
                for c in ctx:
                    V((lambda c: lambda e: e.tensor_tensor(out=c["r"][2], in0=c["vi"], in1=c["ec"], op=ALU.mult))(c), [c["pti"], "Ec"], [c["rt"][2]])
                for c in ctx:
                    V((lambda c: lambda e: e.tensor_tensor(out=c["r"][3], in0=c["vr"], in1=c["es"], op=ALU.mult))(c), [c["ptr"], "Es"], [c["rt"][3]])
                for c in ctx:
                    V((lambda c: lambda e: e.tensor_tensor(out=c["win"][:, :, 0, :], in0=c["r"][0], in1=c["r"][1], op=ALU.add))(c),
                      [c["rt"][0], c["rt"][1]], [c["wint"]])
                for c in ctx:
                    V((lambda c: lambda e: e.tensor_tensor(out=c["win"][:, :, 1, :], in0=c["r"][2], in1=c["r"][3], op=ALU.subtract))(c),
                      [c["rt"][2], c["rt"][3], c["wint"]], [c["wint"]])
                for l in range(4):
                    for reim in range(2):
                        for c in ctx:
                            gh = c["q"] * 4 + l
                            V((lambda c, gh, l, reim: lambda e: e.tensor_tensor_scan(
                                out=c["Wt"][:, l, reim, :], data0=bc(r8tab[:, gh:gh + 1], [128, 128]), data1=c["win"][:, l, reim, :],
                                initial=wcar[:, gh, reim:reim + 1], op0=ALU.mult, op1=ALU.add))(c, gh, l, reim),
                              [c["wint"], "r8tab", "wcar%d" % c["q"]], [c["wtok"]])
                for c in ctx:
                    Wt = c["Wt"]
                    wl = Wt[:, :, :, 127]
                    V((lambda c, wl: lambda e: e.tensor_tensor(out=c["cta"], in0=C1t[:, c["gsl"], :], in1=wl, op=ALU.mult))(c, wl), ["C1t", c["wtok"]], [c["ctt"][0]])
                for c in ctx:
                    Wt = c["Wt"]
                    wl_sw = bass.AP(tensor=Wt.tensor, offset=Wt[:, :, 1, 127].offset, ap=[list(Wt.ap[0]), list(Wt.ap[1]), [-Wt.ap[2][0], 2]])
                    V((lambda c, wl_sw: lambda e: e.tensor_tensor(out=c["ctb"], in0=C2t[:, c["gsl"], :], in1=wl_sw, op=ALU.mult))(c, wl_sw),
                      ["C2t", c["wtok"]], [c["ctt"][1]])
                for c in ctx:
                    V((lambda c: lambda e: e.tensor_tensor(out=wcar[:, c["gsl"], :], in0=c["cta"], in1=c["ctb"], op=ALU.add))(c), c["ctt"], ["wcar%d" % c["q"]])
                for c in ctx:
                    V((lambda c: lambda e: e.tensor_copy(out=Xb[:, c["gsl"], :, 0], in_=Xb[:, c["gsl"], :, 128]))(c), [c["xtk"]], [c["xtk"]])
                for c in ctx:
                    V((lambda c: lambda e: e.tensor_tensor(out=c["r"][0], in0=c["Wt"][:, :, 0, :], in1=c["ec"], op=ALU.mult))(c), [c["wtok"], "Ec"], [c["rt"][0]])
                for c in ctx:
                    V((lambda c: lambda e: e.tensor_tensor(out=c["r"][1], in0=c["Wt"][:, :, 1, :], in1=c["es"], op=ALU.mult))(c), [c["wtok"], "Es"], [c["rt"][1]])
                for c in ctx:
                    V((lambda c: lambda e: e.tensor_tensor(out=c["r"][2], in0=c["Wt"][:, :, 0, :], in1=c["es"], op=ALU.mult))(c), [c["wtok"], "Es"], [c["rt"][2]])
                for c in ctx:
                    V((lambda c: lambda e: e.tensor_tensor(out=c["r"][3], in0=c["Wt"][:, :, 1, :], in1=c["ec"], op=ALU.mult))(c), [c["wtok"], "Ec"], [c["rt"][3]])
                for c in ctx:
                    V((lambda c: lambda e: e.tensor_tensor(out=Xb[:, c["gsl"], 0, 1:129], in0=c["r"][0], in1=c["r"][1], op=ALU.subtract))(c),
                      [c["rt"][0], c["rt"][1], c["xtk"]], [c["xtk"]])
                for c in ctx:
                    V((lambda c: lambda e: e.tensor_tensor(out=Xb[:, c["gsl"], 1, 1:129], in0=c["r"][2], in1=c["r"][3], op=ALU.add))(c),
                      [c["rt"][2], c["rt"][3], c["xtk"]], [c["xtk"]])

        def phaseA_out(stile):
            P.tag = 'A_Y%d' % stile
            t0 = stile * 1024
            U8t = U8s[stile % 2]
            ut = "U8_%d_" % (stile % 2)
            for gq in range(8):
                i, ps, pt = nextps()
                for l in range(4):
                    g = gq * 4 + l
                    gh = g // 2
                    o_ = ps[:, l * 128:(l + 1) * 128]
                    xtk = "Xb%d" % (gh // 4)
                    T((lambda o_, g, U8t: lambda e: e.matmul(o_, lhsT=U8t[:, g, :], rhs=M0[:, g, :], start=True, stop=False))(o_, g, U8t),
                      [ut + "%d" % (g // 8), "M0"], [pt])
                    T((lambda o_, g, gh: lambda e: e.matmul(o_, lhsT=Xb[:, gh, 0, 0:128], rhs=Gpad[:, g, 0, :], start=False, stop=False))(o_, g, gh),
                      [xtk, "Gpad"], [pt])
                    T((lambda o_, g, gh: lambda e: e.matmul(o_, lhsT=Xb[:, gh, 1, 0:128], rhs=Gpad[:, g, 1, :], start=False, stop=True))(o_, g, gh),
                      [xtk, "Gpad"], [pt])
                S((lambda ps, gq: lambda e: e.activation(
                    out=g_oct[:, :, gq * 64:(gq + 1) * 64].rearrange("p t (g c) -> p t g c", g=4),
                    in_=ps.rearrange("p (g t c) -> p t g c", g=4, t=8), func=AF.Gelu_apprx_tanh))(ps, gq),
                  [pt], ["g_oct"])
            P.tag = 'A_gT%d' % stile
            for cb in range(4):
                i, ps, pt = nextps()
                pb = psbf(i)
                for t8 in range(8):
                    T((lambda pb, t8, cb: lambda e: e.transpose(out=pb[:, t8 * 128:(t8 + 1) * 128], in_=g_oct[:, t8, cb * 128:(cb + 1) * 128],
                                                                identity=ident_b))(pb, t8, cb), ["g_oct", "ident_b"], [pt])
                gs_ = gfm_ctr[0] % 2
                gfm_ctr[0] += 1
                gfm = g_fms[gs_]
                S((lambda pb, gfm: lambda e: e.copy(out=gfm.rearrange("p (j t) -> p t j", t=8),
                                                    in_=pb.rearrange("p (t j) -> p t j", t=8)))(pb, gfm), [pt], ["g_fm%d" % gs_])
                DMA((lambda cb, t0, gfm: lambda e: e.dma_start(out=g_s[cb * 128:(cb + 1) * 128, t0:t0 + 1024], in_=gfm))(cb, t0, gfm),
                    ["g_fm%d" % gs_], ["g_s%d" % stile], key="gst%d" % gs_, eng="gpsimd")

        phaseA_front(0)
        for stile in range(4):
            phaseA_scan(stile)
            if stile + 1 < 4:
                phaseA_front(stile + 1)
            phaseA_out(stile)

        P.fence(fence_fns)
        A.release(mA)
        g1bc = A.buf([D], F32)
        b1bc = None
        g2bc = A.buf([D], F32)
        b2bc = A.buf([D], F32)
        glu_sb = A.buf([4, 512], BF16)
        wso_sb = A.buf([4, D], BF16)
        wco_sb = A.buf([4, D], BF16)
        wo_sb = A.buf([8, D], BF16)
        DMA(lambda e: e.dma_start(out=glu_sb, in_=glu_w.rearrange("(k p) n -> p k n", p=128)), [], ["glu_sb"], eng="gpsimd")
        DMA(lambda e: e.dma_start(out=wso_sb, in_=w_ssm_out.rearrange("(k p) n -> p k n", p=128)), [], ["wso_sb"], eng="gpsimd")
        DMA(lambda e: e.dma_start(out=wco_sb, in_=w_conv_out.rearrange("(k p) n -> p k n", p=128)), [], ["wco_sb"], eng="gpsimd")
        DMA(lambda e: e.dma_start(out=wo_sb, in_=w_o.rearrange("(k p) n -> p k n", p=128)), [], ["wo_sb"], eng="gpsimd")
        for dst, src, tok in ((g1bc, ln1_g, "g1bc"), (g2bc, ln2_g, "g2bc"), (b2bc, ln2_b, "b2bc")):
            DMA((lambda dst, src: lambda e: e.dma_start(out=dst, in_=src.partition_broadcast(128)))(dst, src), [], [tok])
        V(lambda e: e.tensor_scalar(out=g1bc, in0=g1bc, scalar1=ALPHA, scalar2=None, op0=ALU.mult), ["g1bc"], ["g1bc"])
        brow2 = A.buf([D], BF16)
        NT = 8
        g_tb = A.buf([4, 512], BF16)
        xb2 = A.buf([4, D], BF16)
        xT2 = A.buf([8, 512], BF16)
        xbx = A.buf([4, D], BF16)
        xTx = A.buf([8, 512], BF16)
        NWS = 3
        wgrp = [A.buf([8, 384], BF16) for _ in range(NWS)]
        NTMP = 8
        tmps = [A.buf([512], F32) for _ in range(NTMP)]
        bz = A.buf([4, 512], BF16)
        merged = A.buf([8, 512], BF16)
        x1 = A.buf([4, D], F32)
        stats = A.buf([4, 2, 6], F32)
        mv = A.buf([4, 2], F32)
        rstd4 = A.buf([4], F32)
        nmr4 = A.buf([4], F32)
        NFS = 3
        ffw = [A.buf([2, 8, 128], BF16) for _ in range(NFS)]
        NDS = 3
        wdb = [A.buf([D], BF16) for _ in range(NDS)]
        hid = A.buf([NFB, 512], BF16)
        brow2f = hid[:, 0:4, :].rearrange("p a b -> p (a b)").bitcast(F32)
        V(lambda e: e.memset(brow2, 0.0), [], ["brow2"])
        DMA(lambda e: e.dma_start(out=brow2f[0:1, :], in_=ln1_b.partition_broadcast(1)), [], ["hid0", "hid1", "hid2", "hid3"])
        V(lambda e: e.tensor_scalar(out=brow2[0:1, :], in0=brow2f[0:1, :], scalar1=ALPHA, scalar2=None, op0=ALU.mult),
          ["hid0", "hid1", "hid2", "hid3", "brow2"], ["brow2"])
        V(lambda e: e.memset(eps_t, LN_EPS), [], ["eps_t"])
        print("ARENA top (phase B)", A.top, "of", A.n)

        tmp_ctr = [0]

        def tmp():
            i = tmp_ctr[0] % NTMP
            tmp_ctr[0] += 1
            return tmps[i], "tmp%d" % i

        wg_ctr = [0]

        def load_wgrp(kind, idx):
            slot = wg_ctr[0] % NWS
            wg_ctr[0] += 1
            if kind == "cv":
                DMA((lambda slot, idx: lambda e: e.dma_start(out=wgrp[slot], in_=wcv_s[idx]))(slot, idx),
                    ["wcv_s"], ["wgrp%d" % slot], key="wgrp%d" % slot)
            else:
                DMA((lambda slot, idx: lambda e: e.dma_start(out=wgrp[slot][:, :, 0:256], in_=wgt_s[idx]))(slot, idx),
                    ["wgt_s"], ["wgrp%d" % slot], key="wgrp%d" % slot)
            return slot

        def ln_stage(gbc, bbc, gtok, btok):
            xt = ["x1_%d" % s_ for s_ in range(4)]
            for sub in range(4):
                for half in range(2):
                    V((lambda sub, half: lambda e: e.bn_stats(out=stats[:, sub, half, :], in_=x1[:, sub, half * 512:(half + 1) * 512]))(sub, half),
                      [xt[sub]], ["stats%d" % sub])
                V((lambda sub: lambda e: e.bn_aggr(out=mv[:, sub, :], in_=stats[:, sub].rearrange("p a b -> p (a b)")))(sub), ["stats%d" % sub], ["mv"])
            def part_b():
                S(lambda e: e.activation(out=rstd4, in_=mv[:, :, 1], func=AF.Sqrt, bias=eps_t, scale=1.0), ["mv", "eps_t"], ["rstd4"])
                V(lambda e: e.reciprocal(out=rstd4, in_=rstd4), ["rstd4"], ["rstd4"])
                for sub in range(4):
                    V((lambda sub: lambda e: e.tensor_scalar(out=x1[:, sub, :], in0=x1[:, sub, :], scalar1=mv[:, sub, 0:1], scalar2=rstd4[:, sub:sub + 1],
                                                             op0=ALU.subtract, op1=ALU.mult))(sub), [xt[sub], "mv", "rstd4"], [xt[sub]])
            deferred = []
            for sub in range(4):
                deferred.append((lambda sub: lambda: V((lambda sub: lambda e: e.tensor_tensor(out=x1[:, sub, :], in0=x1[:, sub, :], in1=gbc, op=ALU.mult))(sub),
                                                       [xt[sub], gtok], [xt[sub]]))(sub))
                deferred.append((lambda sub: lambda: V((lambda sub: lambda e: e.tensor_tensor(out=x1[:, sub, :], in0=x1[:, sub, :], in1=bbc, op=ALU.add))(sub),
                                                       [xt[sub], btok], [xt[sub]]))(sub))
            return [part_b] + deferred

        def load_x_bf16(t):
            for sub in range(4):
                r0 = t * 512 + sub * 128
                DMA((lambda sub, r0: lambda e: e.dma_start(out=xbx[:, sub, :], in_=x[r0:r0 + 128, :]))(sub, r0),
                    [], ["xbx_%d" % sub], key="xld%d" % sub, eng="gpsimd")

        def emit_xT():
            for k in range(8):
                i, ps, pt = nextps()
                pb = psbf(i)
                for sub in range(4):
                    T((lambda pb, sub, k: lambda e: e.transpose(out=pb[:, sub * 128:(sub + 1) * 128], in_=xbx[:, sub, k * 128:(k + 1) * 128],
                                                                identity=ident_b))(pb, sub, k), ["xbx_%d" % sub, "ident_b"], [pt])
                S((lambda pb, k: lambda e: e.copy(out=xTx[:, k, :], in_=pb[:, 0:512]))(pb, k), [pt], ["xTx_%d" % k])

        wq = [("cv", 0, c) for c in range(4)]
        for t_ in range(NT):
            wq += [("gt", t_, d) for d in range(8)]
            if t_ + 1 < NT:
                wq += [("cv", t_ + 1, c) for c in range(4)]
        gslot = {}

        def pump(n):
            for _ in range(n):
                if wq:
                    kd = wq.pop(0)
                    gslot[kd] = load_wgrp(kd[0], kd[2])

        def proj_block(kd, cbl):
            slot_ = gslot[kd]
            i, ps, pt = nextps()
            for k in range(8):
                T((lambda ps, slot_, cbl, k: lambda e: e.matmul(ps, lhsT=wgrp[slot_][:, k, cbl * 128:(cbl + 1) * 128], rhs=xTx[:, k, :],
                                                               start=(k == 0), stop=(k == 7)))(ps, slot_, cbl, k),
                  ["wgrp%d" % slot_, "xTx_%d" % k], [pt])
            return ps, pt

        def glu_stage(t):
            P.tag = 'B%d_glu' % t
            glu_ps = []
            for eb in range(4):
                i, ps, pt = nextps()
                for k in range(4):
                    T((lambda ps, eb, k: lambda e: e.matmul(ps, lhsT=glu_sb[:, k, eb * 128:(eb + 1) * 128], rhs=g_tb[:, k, :],
                                                           start=(k == 0), stop=(k == 3)))(ps, eb, k), ["glu_sb", "g_tb"], [pt])
                glu_ps.append((ps, pt))
            for eb in range(4):
                ps, pt = glu_ps[eb]
                sg, sgt = tmp()
                S((lambda ps, eb, sg: lambda e: e.activation(out=sg, in_=ps, func=AF.Sigmoid, bias=glub_fm[:, eb:eb + 1], scale=1.0))(ps, eb, sg),
                  [pt, "glub_fm"], [sgt])
                V((lambda eb, sg: lambda e: e.tensor_tensor(out=g_tb[:, eb, :], in0=g_tb[:, eb, :], in1=sg, op=ALU.mult))(eb, sg), [sgt, "g_tb"], ["g_tb"])

        def conv_stage(t, cbs):
            P.tag = 'B%d_conv' % t
            for cb in cbs:
                hp, hpt = proj_block(("cv", t, cb), 0)
                cp, cpt = proj_block(("cv", t, cb), 1)
                bp, bpt = proj_block(("cv", t, cb), 2)
                pump(1)
                hsb, hsbt = tmp()
                zt, ztt = tmp()
                S((lambda hp, cb, hsb: lambda e: e.activation(out=hsb, in_=hp, func=AF.Identity, bias=bias_fm[:, 4 + cb:5 + cb], scale=1.0))(hp, cb, hsb),
                  [hpt, "bias_fm"], [hsbt])
                V((lambda cp, cb, hsb: lambda e: e.scalar_tensor_tensor(out=vbuf[:, cb, 2:514], in0=cp, scalar=bias_fm[:, 8 + cb:9 + cb], in1=hsb,
                                                                        op0=ALU.add, op1=ALU.mult))(cp, cb, hsb), [cpt, "bias_fm", hsbt], ["vbuf%d" % cb])
                V((lambda cb, zt: lambda e: e.tensor_scalar(out=zt, in0=vbuf[:, cb, 0:512], scalar1=convw_fm[:, cb:cb + 1], scalar2=None, op0=ALU.mult))(cb, zt),
                  ["vbuf%d" % cb, "convw_fm"], [ztt])
                V((lambda cb, zt: lambda e: e.scalar_tensor_tensor(out=zt, in0=vbuf[:, cb, 1:513], scalar=convw_fm[:, 4 + cb:5 + cb], in1=zt,
                                                                   op0=ALU.mult, op1=ALU.add))(cb, zt), ["vbuf%d" % cb, "convw_fm", ztt], [ztt])
                V((lambda cb, zt: lambda e: e.scalar_tensor_tensor(out=zt, in0=vbuf[:, cb, 2:514], scalar=convw_fm[:, 8 + cb:9 + cb], in1=zt,
                                                                   op0=ALU.mult, op1=ALU.add))(cb, zt), ["vbuf%d" % cb, "convw_fm", ztt], [ztt])
                V((lambda bp, cb, zt: lambda e: e.scalar_tensor_tensor(out=bz[:, cb, :], in0=bp, scalar=bias_fm[:, 12 + cb:13 + cb], in1=zt,
                                                                       op0=ALU.add, op1=ALU.mult))(bp, cb, zt), [bpt, "bias_fm", ztt], ["bz%d" % cb])
                V((lambda cb: lambda e: e.tensor_copy(out=vbuf[:, cb, 0:2], in_=vbuf[:, cb, 512:514]))(cb), ["vbuf%d" % cb], ["vbuf%d" % cb])

        ff_ctr = [0]
        wd_ctr = [0]
        P.tag = 'B0_xT'
        load_x_bf16(0)
        pump(3)
        emit_xT()
        load_x_bf16(1)
        conv_stage(0, range(4))
        ln2_q = []
        def load_g(t):
            DMA((lambda t: lambda e: e.dma_start(out=g_tb, in_=g_s[:, t * 512:(t + 1) * 512].rearrange("(k p) n -> p k n", p=128)))(t),
                ["g_s%d" % (t // 2)], ["g_tb"], key="g_tb")

        load_g(0)
        glu_stage(0)
        for t in range(NT):
            P.tag = 'B%d_merged' % t
            for db in range(8):
                gap, gapt = proj_block(("gt", t, db), 0)
                gbp, gbpt = proj_block(("gt", t, db), 1)
                pump(1)
                i, yap, yapt = nextps()
                for k in range(4):
                    T((lambda yap, db, k: lambda e: e.matmul(yap, lhsT=wso_sb[:, k, db * 128:(db + 1) * 128], rhs=g_tb[:, k, :],
                                                            start=(k == 0), stop=(k == 3)))(yap, db, k), ["wso_sb", "g_tb"], [yapt])
                i, ybp, ybpt = nextps()
                for k in range(4):
                    T((lambda ybp, db, k: lambda e: e.matmul(ybp, lhsT=wco_sb[:, k, db * 128:(db + 1) * 128], rhs=bz[:, k, :],
                                                            start=(k == 0), stop=(k == 3)))(ybp, db, k), ["wco_sb", "bz%d" % k], [ybpt])
                sa, sat = tmp()
                sb_, sbt = tmp()
                S((lambda gap, db, sa: lambda e: e.activation(out=sa, in_=gap, func=AF.Sigmoid, bias=bias_fm[:, 16 + db:17 + db], scale=1.0))(gap, db, sa),
                  [gapt, "bias_fm"], [sat])
                S((lambda gbp, db, sb_: lambda e: e.activation(out=sb_, in_=gbp, func=AF.Sigmoid, bias=bias_fm[:, 24 + db:25 + db], scale=1.0))(gbp, db, sb_),
                  [gbpt, "bias_fm"], [sbt])
                V((lambda yap, sa: lambda e: e.tensor_tensor(out=sa, in0=yap, in1=sa, op=ALU.mult))(yap, sa), [yapt, sat], [sat])
                V((lambda ybp, sb_: lambda e: e.tensor_tensor(out=sb_, in0=ybp, in1=sb_, op=ALU.mult))(ybp, sb_), [ybpt, sbt], [sbt])
                V((lambda db, sa, sb_: lambda e: e.tensor_tensor(out=merged[:, db, :], in0=sa, in1=sb_, op=ALU.add))(db, sa, sb_), [sat, sbt], ["merged%d" % db])
                for _ in range({1: 1, 2: 2, 3: 2, 4: 2, 5: 2}.get(db, 0)):
                    if ln2_q:
                        ln2_q.pop(0)()
            ffq = list(range(NFB))
            ffslot = {}

            def ffpump(n):
                for _ in range(n):
                    if ffq:
                        fb = ffq.pop(0)
                        fs = ff_ctr[0] % NFS
                        ff_ctr[0] += 1
                        ffslot[fb] = fs
                        DMA((lambda fs, fb: lambda e: e.dma_start(out=ffw[fs], in_=wgu_s[fb]))(fs, fb), ["wgu%d" % fb], ["ffw%d" % fs],
                            key="ffw%d" % fs)

            ffpump(NFS)
            P.tag = 'B%d_wo' % t
            for sub in range(4):
                r0 = t * 512 + sub * 128
                DMA((lambda sub, r0: lambda e: e.dma_start(out=x1[:, sub, :], in_=x[r0:r0 + 128, :]))(sub, r0), [], ["x1_%d" % sub], key="xres%d" % sub)
            for sub in range(4):
                for half in range(2):
                    i, ps, pt = nextps()
                    for k in range(8):
                        T((lambda ps, sub, half, k: lambda e: e.matmul(ps, lhsT=merged[:, k, sub * 128:(sub + 1) * 128],
                                                                      rhs=wo_sb[:, k, half * 512:(half + 1) * 512],
                                                                      start=(k == 0), stop=(k == 7)))(ps, sub, half, k), ["merged%d" % k, "wo_sb"], [pt])
                    V((lambda ps, sub, half: lambda e: e.scalar_tensor_tensor(
                        out=x1[:, sub, half * 512:(half + 1) * 512], in0=x1[:, sub, half * 512:(half + 1) * 512], scalar=ALPHA, in1=ps,
                        op0=ALU.mult, op1=ALU.add))(ps, sub, half), [pt, "x1_%d" % sub], ["x1_%d" % sub])
            xt_ = ["x1_%d" % s_ for s_ in range(4)]
            for sub in range(4):
                for half in range(2):
                    V((lambda sub, half: lambda e: e.bn_stats(out=stats[:, sub, half, :], in_=x1[:, sub, half * 512:(half + 1) * 512]))(sub, half),
                      [xt_[sub]], ["stats%d" % sub])
                V((lambda sub: lambda e: e.bn_aggr(out=mv[:, sub, :], in_=stats[:, sub].rearrange("p a b -> p (a b)")))(sub), ["stats%d" % sub], ["mv"])
            S(lambda e: e.activation(out=rstd4, in_=mv[:, :, 1], func=AF.Sqrt, bias=eps_t, scale=1.0), ["mv", "eps_t"], ["rstd4"])
            V(lambda e: e.reciprocal(out=rstd4, in_=rstd4), ["rstd4"], ["rstd4"])
            V(lambda e: e.scalar_tensor_tensor(out=nmr4, in0=mv[:, :, 0], scalar=-1.0, in1=rstd4, op0=ALU.mult, op1=ALU.mult), ["mv", "rstd4"], ["nmr4"])
            for sub in range(4):
                S((lambda sub: lambda e: e.activation(out=xb2[:, sub, :], in_=x1[:, sub, :], func=AF.Identity,
                                                      bias=nmr4[:, sub:sub + 1], scale=rstd4[:, sub:sub + 1]))(sub), [xt_[sub], "rstd4", "nmr4"], ["xb2_%d" % sub])
            if t + 1 < NT:
                P.tag = 'B%d_xT' % (t + 1)
                emit_xT()
                if t + 2 < NT:
                    load_x_bf16(t + 2)
                conv_stage(t + 1, (0, 1))
            P.tag = 'B%d_x1T' % t
            for k in range(8):
                i, ps, pt = nextps()
                pb = psbf(i)
                for sub in range(4):
                    T((lambda pb, sub, k: lambda e: e.transpose(out=pb[:, sub * 128:(sub + 1) * 128], in_=xb2[:, sub, k * 128:(k + 1) * 128],
                                                                identity=ident_b))(pb, sub, k), ["xb2_%d" % sub, "ident_b"], [pt])
                S((lambda pb, k: lambda e: e.activation(out=xT2[:, k, :], in_=pb[:, 0:512], func=AF.Identity,
                                                        bias=b1_fm[:, k:k + 1], scale=g1_fm[:, k:k + 1]))(pb, k), [pt, "g1_fm", "b1_fm"], ["xT2_%d" % k])
            if t + 1 < NT:
                conv_stage(t + 1, (2, 3))
            ln1_q = []
            for sub in range(4):
                ln1_q.append((lambda sub: lambda: V((lambda sub: lambda e: e.tensor_scalar(
                    out=x1[:, sub, :], in0=x1[:, sub, :], scalar1=mv[:, sub, 0:1], scalar2=rstd4[:, sub:sub + 1],
                    op0=ALU.subtract, op1=ALU.mult))(sub), [xt_[sub], "mv", "rstd4"], [xt_[sub]]))(sub))
            for sub in range(4):
                ln1_q.append((lambda sub: lambda: V((lambda sub: lambda e: e.tensor_tensor(
                    out=x1[:, sub, :], in0=x1[:, sub, :], in1=g1bc, op=ALU.mult))(sub), [xt_[sub], "g1bc"], [xt_[sub]]))(sub))
            if t + 1 < NT:
                load_g(t + 1)
            P.tag = 'B%d_ffn' % t
            wdq = list(range(NFB))
            wdslot = {}

            def wdpump(n):
                for _ in range(n):
                    if wdq:
                        fb = wdq.pop(0)
                        ds_ = wd_ctr[0] % NDS
                        wd_ctr[0] += 1
                        wdslot[fb] = ds_
                        DMA((lambda ds_, fb: lambda e: e.dma_start(out=wdb[ds_], in_=wd_s[fb * 128:(fb + 1) * 128, :]))(ds_, fb),
                            ["wd_s"], ["wdb%d" % ds_], key="wdb%d" % ds_)

            for fb in range(NFB):
                fs = ffslot[fb]
                i, gp, gpt = nextps()
                for k in range(8):
                    T((lambda gp, fs, k: lambda e: e.matmul(gp, lhsT=ffw[fs][:, 0, k, :], rhs=xT2[:, k, :],
                                                           start=(k == 0), stop=(k == 7)))(gp, fs, k), ["ffw%d" % fs, "xT2_%d" % k], [gpt])
                i, up, upt = nextps()
                for k in range(8):
                    T((lambda up, fs, k: lambda e: e.matmul(up, lhsT=ffw[fs][:, 1, k, :], rhs=xT2[:, k, :],
                                                           start=(k == 0), stop=(k == 7)))(up, fs, k), ["ffw%d" % fs, "xT2_%d" % k], [upt])
                ffpump(1)
                if fb == NFB - 3:
                    wdpump(NDS)
                sgl, sglt = tmp()
                S((lambda gp, sgl: lambda e: e.activation(out=sgl, in_=gp, func=AF.Silu))(gp, sgl), [gpt], [sglt])
                V((lambda up, fb, sgl: lambda e: e.tensor_tensor(out=hid[:, fb, :], in0=up, in1=sgl, op=ALU.mult))(up, fb, sgl), [upt, sglt], ["hid%d" % fb])
                if ln1_q:
                    ln1_q.pop(0)()
            if t + 1 < NT:
                glu_stage(t + 1)
            P.tag = 'B%d_down' % t
            banks = {}
            for sub in range(4):
                for half in range(2):
                    banks[(sub, half)] = nextps()
            for sub in range(4):
                for half in range(2):
                    i, ps, pt = banks[(sub, half)]
                    T((lambda ps, half: lambda e: e.matmul(ps, lhsT=ones_b, rhs=brow2[:, half * 512:(half + 1) * 512], start=True, stop=False))(ps, half),
                      ["ones_b", "brow2"], [pt])
            for fb in range(NFB):
                ds_ = wdslot[fb]
                for sub in range(4):
                    for half in range(2):
                        i, ps, pt = banks[(sub, half)]
                        T((lambda ps, ds_, fb, sub, half: lambda e: e.matmul(ps, lhsT=hid[:, fb, sub * 128:(sub + 1) * 128],
                                                                            rhs=wdb[ds_][:, half * 512:(half + 1) * 512],
                                                                            start=False, stop=(fb == NFB - 1)))(ps, ds_, fb, sub, half),
                          ["hid%d" % fb, "wdb%d" % ds_], [pt])
                wdpump(1)
            for sub in range(4):
                for half in range(2):
                    i, ps, pt = banks[(sub, half)]
                    V((lambda ps, sub, half: lambda e: e.tensor_tensor(
                        out=x1[:, sub, half * 512:(half + 1) * 512], in0=ps, in1=x1[:, sub, half * 512:(half + 1) * 512],
                        op=ALU.add))(ps, sub, half), [pt, "x1_%d" % sub], ["x1_%d" % sub])
            dfr = ln_stage(g2bc, b2bc, "g2bc", "b2bc")
            ln2_q = [dfr[0]]
            for sub in range(4):
                r0 = t * 512 + sub * 128
                ln2_q.append(dfr[1 + 2 * sub])
                ln2_q.append((lambda sub, r0, f: lambda: (f(), DMA((lambda sub, r0: lambda e: e.dma_start(out=out[r0:r0 + 128, :], in_=x1[:, sub, :]))(sub, r0),
                                                                     ["x1_%d" % sub], [], key="ost%d" % sub, eng="gpsimd")))(sub, r0, dfr[2 + 2 * sub]))
            if t == NT - 1:
                for f in ln2_q:
                    f()
                ln2_q = []

        P.emit()
        import os
        if os.environ.get('KDUMP_TAGS'):
            import json
            json.dump({e: [o.tag for o in P.ops[e] if o.fn is not None] for e in ENGS}, open(os.environ['KDUMP_TAGS'], 'w'))
    return nc


_NC_CACHE = {}


def kernel(**inputs):
    if "nc" not in _NC_CACHE:
        _NC_CACHE["nc"] = build_nc()
    nc = _NC_CACHE["nc"]
    x = np.ascontiguousarray(inputs["x"], dtype=np.float32)
    shared = {}
    for k, v in inputs.items():
        if k == "x":
            continue
        shared[k] = np.ascontiguousarray(np.asarray(v, dtype=np.float32)[0])
    in_maps = []
    for c in range(NCORES):
        m = dict(shared)
        m["x"] = x[c]
        in_maps.append(m)
    res = run_bass_kernel_spmd(nc, in_maps, core_ids=list(range(NCORES)))
    outs = [np.asarray(res.results[c]["out"], dtype=np.float32) for c in range(NCORES)]
    return np.stack(outs, axis=0)
```

```python
import math
import contextlib
import numpy as np
import concourse.bass as bass
import concourse.mybir as mybir
from concourse.bass_utils import run_bass_kernel_spmd

F32 = mybir.dt.float32
BF16 = mybir.dt.bfloat16
I32 = mybir.dt.int32
U8 = mybir.dt.uint8
ALU = mybir.AluOpType
AF = mybir.ActivationFunctionType

D = 1024
SEQ = 4096
NCORES = 8
FFN = 2816
NFB = FFN // 128
ALPHA = 2.0 ** 0.25
LN_EPS = 1e-5
TWO_PI = 2.0 * math.pi
MAGIC = 12582912.0
SIN_SCALE = 1.0 - 2e-6

ENGS = ("tensor", "vector", "scalar", "gpsimd", "sync")


class _Op:
    __slots__ = ("eng", "fn", "deps", "is_dma", "dma_key", "dma_target", "signal", "seq", "tag")

    def __init__(self, eng, fn, is_dma=False):
        self.eng = eng
        self.fn = fn
        self.deps = []
        self.is_dma = is_dma
        self.dma_key = None
        self.dma_target = 0
        self.signal = False
        self.seq = 0
        self.tag = ''


class Prog:
    def __init__(self, nc):
        self.nc = nc
        self.ops = {e: [] for e in ENGS}
        self.last_writer = {}
        self.readers = {}
        self.dma_counts = {}
        self.last_dma = {}
        self.all_ops = []
        self.tag = ''

    def _add_dep(self, op, dep):
        if dep is None or dep is op:
            return
        if (dep.eng == op.eng and not op.is_dma and not dep.is_dma
                and op.eng in ("tensor",)):
            return
        op.deps.append(dep)
        if not dep.is_dma:
            dep.signal = True

    def op(self, eng, fn, reads=(), writes=(), dma_key=None):
        is_dma = dma_key is not None
        o = _Op(eng, fn, is_dma)
        o.tag = self.tag
        for t in reads:
            self._add_dep(o, self.last_writer.get(t))
        for t in writes:
            self._add_dep(o, self.last_writer.get(t))
            for r in self.readers.get(t, ()):
                self._add_dep(o, r)
        for t in reads:
            self.readers.setdefault(t, []).append(o)
        for t in writes:
            self.last_writer[t] = o
            self.readers[t] = []
        if is_dma:
            c = self.dma_counts.get(dma_key, 0) + 16
            self.dma_counts[dma_key] = c
            o.dma_key = dma_key
            o.dma_target = c
            self.last_dma[dma_key] = o
        self.ops[eng].append(o)
        self.all_ops.append(o)
        return o

    def fence(self, fence_fns):
        fs = []
        for e, fn in fence_fns.items():
            o = _Op(e, fn)
            o.signal = True
            self.ops[e].append(o)
            self.all_ops.append(o)
            fs.append(o)
        dm = [o for k, o in self.last_dma.items() if not str(k).startswith('cv_')]
        for e in ENGS:
            g = _Op(e, None)
            g.deps = [f for f in fs] + dm
            self.ops[e].append(g)
            self.all_ops.append(g)

    def emit(self, final_wait_eng="sync"):
        nc = self.nc
        fin = _Op(final_wait_eng, None)
        fin.deps = list(self.last_dma.values())
        self.ops[final_wait_eng].append(fin)
        for e in ENGS:
            c = 0
            for o in self.ops[e]:
                if o.signal and not o.is_dma:
                    c += 1
                    o.seq = c
        with contextlib.ExitStack() as st:
            esem = {e: st.enter_context(nc.semaphore("es_" + e)) for e in ENGS}
            dsem = {}
            for i, k in enumerate(self.dma_counts):
                dsem[k] = st.enter_context(nc.semaphore("ds_%d" % i))
            block = st.enter_context(nc.Block())
            ops = self.ops

            def make(e):
                def body(eng):
                    waited = {}
                    for o in ops[e]:
                        for d in o.deps:
                            if d.is_dma:
                                s, v, k = dsem[d.dma_key], d.dma_target, ("d", d.dma_key)
                            else:
                                s, v, k = esem[d.eng], d.seq, ("e", d.eng)
                            if waited.get(k, 0) >= v:
                                continue
                            waited[k] = v
                            eng.wait_ge(s, v)
                        if o.fn is None:
                            continue
                        ins = o.fn(eng)
                        if o.is_dma:
                            ins.then_inc(dsem[o.dma_key], 16)
                        elif o.signal:
                            ins.then_inc(esem[e], 1)
                return body

            block.tensor(make("tensor"))
            block.vector(make("vector"))
            block.scalar(make("scalar"))
            block.gpsimd(make("gpsimd"))
            block.sync(make("sync"))


class Arena:
    def __init__(self, big, nbytes):
        self.big = big
        self.n = nbytes
        self.top = 0

    def buf(self, shape, dt, parts=128):
        esz = {F32: 4, BF16: 2, I32: 4}[dt]
        n = int(np.prod(shape)) * esz
        n_al = (n + 63) // 64 * 64
        off = self.top
        assert off + n_al <= self.n, ("arena overflow", off, n_al, self.n)
        self.top += n_al
        ap = self.big[0:parts, off:off + n].bitcast(dt)
        if len(shape) > 1:
            names = " ".join("d%d" % i for i in range(len(shape)))
            kw = {"d%d" % i: int(s) for i, s in enumerate(shape)}
            ap = ap.rearrange("p (%s) -> p %s" % (names, names), **kw)
        return ap

    def mark(self):
        return self.top

    def release(self, m):
        self.top = m


def bc(ap, shape):
    return ap.broadcast_to(list(shape))


def build_nc(debug=False):
    nc = bass.Bass("TRN2", target_bir_lowering=False)

    def din(name, shape):
        return nc.dram_tensor(name, list(shape), F32, kind="ExternalInput").ap()

    x = din("x", [SEQ, D])
    w_in = din("w_in", [D, 4096])
    b_in = din("b_in", [4096])
    lam_re = din("ssm_lambda_re", [32, 64])
    lam_im = din("ssm_lambda_im", [32, 64])
    log_dt = din("ssm_log_dt", [32])
    b_re = din("ssm_b_re", [32, 64, 16])
    b_im = din("ssm_b_im", [32, 64, 16])
    c_re = din("ssm_c_re", [32, 16, 64])
    c_im = din("ssm_c_im", [32, 16, 64])
    ssm_d = din("ssm_d", [512])
    glu_w = din("glu_w", [512, 512])
    glu_b = din("glu_b", [512])
    w_ssm_out = din("w_ssm_out", [512, D])
    conv_w = din("conv_w", [3, 512])
    w_conv_out = din("w_conv_out", [512, D])
    w_o = din("w_o", [D, D])
    ln1_g = din("ln1_g", [D])
    ln1_b = din("ln1_b", [D])
    w_gate = din("w_gate", [D, FFN])
    w_up = din("w_up", [D, FFN])
    w_down = din("w_down", [FFN, D])
    ln2_g = din("ln2_g", [D])
    ln2_b = din("ln2_b", [D])
    out = nc.dram_tensor("out", [SEQ, D], F32, kind="ExternalOutput").ap()

    win_r = nc.dram_tensor("win_r", [D, 3584], BF16, kind="Internal").ap()
    wcv_s = nc.dram_tensor("wcv_s", [4, 128, 8, 384], BF16, kind="Internal").ap()
    wgt_s = nc.dram_tensor("wgt_s", [8, 128, 8, 256], BF16, kind="Internal").ap()
    wgu_s = nc.dram_tensor("wgu_s", [NFB, 128, 2, 8, 128], BF16, kind="Internal").ap()
    wd_s = nc.dram_tensor("wd_s", [FFN, D], BF16, kind="Internal").ap()
    g_s = nc.dram_tensor("g_s", [512, SEQ], BF16, kind="ExternalOutput" if debug else "Internal").ap()

    ARENA_BYTES = 206 * 1024
    with contextlib.ExitStack() as st:
        big = st.enter_context(nc.sbuf_tensor("arena", [128, ARENA_BYTES], U8))
        psb = [st.enter_context(nc.psum_tensor("ps%d" % i, [128, 512], F32)) for i in range(8)]
        A = Arena(big, ARENA_BYTES)
        P = Prog(nc)

        ps_rr = [0]

        def nextps():
            i = ps_rr[0]
            ps_rr[0] = (i + 1) % 8
            return i, psb[i][:], "ps%d" % i

        def psbf(i):
            return psb[i][:].bitcast(BF16)

        def V(fn, reads, writes):
            return P.op("vector", fn, reads, writes)

        def S(fn, reads, writes):
            return P.op("scalar", fn, reads, writes)

        def G(fn, reads, writes):
            return P.op("gpsimd", fn, reads, writes)

        def T(fn, reads, writes):
            return P.op("tensor", fn, reads, writes)

        dma_ctr = [0]

        def DMA(fn, reads, writes, key=None, eng="sync"):
            if key is None:
                key = "dma%d" % dma_ctr[0]
                dma_ctr[0] += 1
            return P.op(eng, fn, reads, writes, dma_key=key)

        ident_f = A.buf([128], F32)
        ident_b = A.buf([128], BF16)
        iota_i = A.buf([128], I32)
        fence_v = A.buf([1], F32)
        fence_s = A.buf([1], F32)
        fence_g = A.buf([1], F32)
        bias_fm = A.buf([32], F32)
        bias_u = A.buf([512], F32)
        glub_fm = A.buf([4], F32)
        g1_fm = A.buf([8], F32)
        b1_fm = A.buf([8], F32)
        convw_fm = A.buf([12], F32)
        vbuf = A.buf([4, 514], F32)
        r8tab = A.buf([16], F32)
        C1t = A.buf([16, 2], F32)
        C2t = A.buf([16, 2], F32)
        wcar = A.buf([16, 2], F32)
        eps_t = A.buf([1], F32)
        ones_b = A.buf([128], BF16)

        fence_fns = {
            "vector": lambda e: e.memset(fence_v, 0.0),
            "gpsimd": lambda e: e.memset(fence_g, 0.0),
            "scalar": lambda e: e.activation(out=fence_s, in_=ident_f[:, 0:1], func=AF.Copy),
        }

        mA = A.mark()
        winu = A.buf([8, 512], BF16)
        DMA(lambda e: e.dma_start(out=winu, in_=w_in[:, 0:512].rearrange("(k p) n -> p k n", p=128)),
            [], ["winu"], eng="gpsimd")
        def convert_chunk(c):
            for r in (2 * c, 2 * c + 1):
                rs = slice(r * 128, (r + 1) * 128)
                DMA((lambda rs: lambda e: e.dma_start(out=win_r[rs, :].rearrange("r (c e) -> r c e", e=896),
                                                      in_=w_in[rs, 512:4096].rearrange("r (c e) -> r c e", e=896)))(rs),
                    [], ["win_r%d" % c], key="cv_win%d" % c, eng="gpsimd")
            for r in range(c * 6, min(NFB, (c + 1) * 6)):
                rs = slice(r * 128, (r + 1) * 128)
                DMA((lambda rs: lambda e: e.dma_start(out=wd_s[rs, :], in_=w_down[rs, :]))(rs),
                    [], ["wd_s"], key="cv_wd", eng="gpsimd")

        def retile_chunk(c):
            for k in (2 * c, 2 * c + 1):
                rs = slice(k * 128, (k + 1) * 128)
                for wh in range(3):
                    DMA((lambda rs, k, wh: lambda e: e.dma_start(
                        out=wcv_s[:, :, k, wh * 128:(wh + 1) * 128].rearrange("c p n -> p c n"),
                        in_=win_r[rs, wh * 512:(wh + 1) * 512].rearrange("p (c n) -> p c n", n=128)))(rs, k, wh),
                        ["win_r%d" % c], ["wcv_s"], key="cv2_win")
                for wh in range(2):
                    DMA((lambda rs, k, wh: lambda e: e.dma_start(
                        out=wgt_s[:, :, k, wh * 128:(wh + 1) * 128].rearrange("c p n -> p c n"),
                        in_=win_r[rs, 1536 + wh * 1024: 1536 + (wh + 1) * 1024].rearrange("p (c n) -> p c n", n=128)))(rs, k, wh),
                        ["win_r%d" % c], ["wgt_s"], key="cv2_wgt")

        P.tag = 'P0'
        G(lambda e: e.iota(iota_i, pattern=[[1, 128]], base=0, channel_multiplier=-1), [], ["iota_i"])
        V(lambda e: e.tensor_scalar(out=ident_f, in0=iota_i, scalar1=0.0, scalar2=None, op0=ALU.is_equal), ["iota_i"], ["ident_f"])
        V(lambda e: e.tensor_copy(out=ident_b, in_=ident_f), ["ident_f"], ["ident_b"])
        V(lambda e: e.memset(vbuf, 0.0), [], ["vbuf0", "vbuf1", "vbuf2", "vbuf3"])

        DMA(lambda e: e.dma_start(out=bias_u, in_=b_in[0:512].partition_broadcast(128)), [], ["bias_u"])

        Ec = A.buf([16, 128], F32)
        Es = A.buf([16, 128], F32)
        M0 = A.buf([32, 128], BF16)
        Hpad = A.buf([32, 2, 128], BF16)
        Gpad = A.buf([32, 2, 128], BF16)
        m0 = A.mark()
        nat = A.buf([128], F32)
        DMA(lambda e: e.dma_start(out=nat[0:32, :], in_=b_in.rearrange("(c p) -> c p", p=128)), [], ["nat_bin"])
        nat2 = A.buf([128], F32)
        DMA(lambda e: e.dma_start(out=nat2[0:4, :], in_=glu_b.rearrange("(c p) -> c p", p=128)), [], ["nat_glub"])
        nat3 = A.buf([128], F32)
        DMA(lambda e: e.dma_start(out=nat3[0:12, :], in_=conv_w.rearrange("k (c p) -> (k c) p", p=128)), [], ["nat_convw"])

        def small_T(dst, src_nat, K, rtok, wtok):
            i, ps, pt = nextps()
            T(lambda e: e.transpose(out=ps[:, 0:K], in_=src_nat[0:K, :], identity=ident_f[0:K, 0:K]), [rtok, "ident_f"], [pt])
            V(lambda e: e.tensor_copy(out=dst, in_=ps[:, 0:K]), [pt], [wtok])

        small_T(bias_fm, nat, 32, "nat_bin", "bias_fm")
        nat4 = A.buf([128], F32)
        DMA(lambda e: e.dma_start(out=nat4[0:8, :], in_=ln1_g.rearrange("(c p) -> c p", p=128)), [], ["nat_g1"])
        nat5 = A.buf([128], F32)
        DMA(lambda e: e.dma_start(out=nat5[0:8, :], in_=ln1_b.rearrange("(c p) -> c p", p=128)), [], ["nat_b1"])
        small_T(g1_fm, nat4, 8, "nat_g1", "g1_fm")
        small_T(b1_fm, nat5, 8, "nat_b1", "b1_fm")
        small_T(glub_fm, nat2, 4, "nat_glub", "glub_fm")
        small_T(convw_fm, nat3, 12, "nat_convw", "convw_fm")

        lr = A.buf([16], F32)
        li = A.buf([16], F32)
        ldt = A.buf([16], F32)
        Br = A.buf([16, 16], F32)
        Bi = A.buf([16, 16], F32)
        Cr = A.buf([16, 16], F32)
        Ci = A.buf([16, 16], F32)
        cnat_r = A.buf([4, 64], F32)
        cnat_i = A.buf([4, 64], F32)
        C64r = A.buf([512], F32)
        C64i = A.buf([512], F32)
        dT = A.buf([32], F32)
        for g2 in range(2):
            ps_ = slice(g2 * 64, (g2 + 1) * 64)
            DMA((lambda ps_, g2: lambda e: e.dma_start(out=lr[ps_, :], in_=lam_re.rearrange("(gh g2) p -> g2 p gh", g2=2)[g2],
                                                      allow_slow_non_contiguous=True))(ps_, g2), [], ["lr"], key="ld_lr")
            DMA((lambda ps_, g2: lambda e: e.dma_start(out=li[ps_, :], in_=lam_im.rearrange("(gh g2) p -> g2 p gh", g2=2)[g2],
                                                      allow_slow_non_contiguous=True))(ps_, g2), [], ["li"], key="ld_li")
            DMA((lambda ps_, g2: lambda e: e.dma_start(out=ldt[ps_, :], in_=bass.AP(tensor=log_dt.tensor, offset=g2, ap=[[0, 64], [2, 16]]),
                                                      allow_slow_non_contiguous=True))(ps_, g2), [], ["ldt"], key="ld_ldt")
            DMA((lambda ps_, g2: lambda e: e.dma_start(out=Br[ps_], in_=b_re.rearrange("(gh g2) p c -> g2 p gh c", g2=2)[g2]))(ps_, g2),
                [], ["Br"], key="ld_Br")
            DMA((lambda ps_, g2: lambda e: e.dma_start(out=Bi[ps_], in_=b_im.rearrange("(gh g2) p c -> g2 p gh c", g2=2)[g2]))(ps_, g2),
                [], ["Bi"], key="ld_Bi")
        DMA(lambda e: e.dma_start(out=cnat_r, in_=c_re.rearrange("g c p -> (g c) p").rearrange("(i r) p -> r i p", r=128)), [], ["cnat_r"])
        DMA(lambda e: e.dma_start(out=cnat_i, in_=c_im.rearrange("g c p -> (g c) p").rearrange("(i r) p -> r i p", r=128)), [], ["cnat_i"])
        DMA(lambda e: e.dma_start(out=dT[0:16, :], in_=ssm_d.rearrange("(g c) -> c g", c=16), allow_slow_non_contiguous=True), [], ["dT"])

        for cnat, C64, ctok, C128 in ((cnat_r, C64r, "cnat_r", Cr), (cnat_i, C64i, "cnat_i", Ci)):
            i, ps, pt = nextps()
            for t4 in range(4):
                T((lambda ps, cnat, t4: lambda e: e.transpose(out=ps[0:64, t4 * 128:(t4 + 1) * 128], in_=cnat[:, t4, :], identity=ident_f))(ps, cnat, t4),
                  [ctok, "ident_f"], [pt])
            S((lambda ps, C64: lambda e: e.copy(out=C64[0:64, :], in_=ps[0:64, :]))(ps, C64), [pt], [ctok + "64"])
            for g2 in range(2):
                DMA((lambda C64, C128, g2: lambda e: e.dma_start(
                    out=C128[g2 * 64:(g2 + 1) * 64],
                    in_=C64[0:64, :].rearrange("p (gh g2 c) -> p gh g2 c", g2=2, c=16)[:, :, g2, :]))(C64, C128, g2),
                    [ctok + "64"], [ctok + "128"], key="shuf_" + ctok)

        ev9 = A.buf([9], F32)
        evD = A.buf([8], F32)
        jv = A.buf([128], F32)
        ev_i = A.buf([128], I32)
        G(lambda e: e.iota(ev_i, pattern=[[1, 128]], base=0, channel_multiplier=0), [], ["ev_i"])
        V(lambda e: e.tensor_copy(out=jv, in_=ev_i), ["ev_i"], ["jv"])
        V(lambda e: e.tensor_copy(out=ev9, in_=ev_i[:, 0:9]), ["ev_i"], ["ev9"])
        V(lambda e: e.tensor_scalar(out=evD, in0=jv[:, 0:8], scalar1=-1.0, scalar2=7.0, op0=ALU.mult, op1=ALU.add), ["jv"], ["evD"])

        dt_t = A.buf([16], F32)
        a_t = A.buf([16], F32)
        th_t = A.buf([16], F32)
        S(lambda e: e.activation(out=dt_t, in_=ldt, func=AF.Exp), ["ldt"], ["dt_t"])
        V(lambda e: e.tensor_tensor(out=a_t, in0=lr, in1=dt_t, op=ALU.mult), ["lr", "dt_t"], ["a_t"])
        V(lambda e: e.tensor_tensor(out=th_t, in0=li, in1=dt_t, op=ALU.mult), ["li", "dt_t"], ["th_t"])

        def powtab(ev, n, nm, th_src=th_t, th_tok="th_t", lead=16, with_mag=True):
            ang = A.buf([lead, n], F32)
            kk = A.buf([lead, n], F32)
            rr = A.buf([lead, n], F32)
            rc = A.buf([lead, n], F32)
            sn = A.buf([lead, n], F32)
            cs = A.buf([lead, n], F32)
            thb = bc(th_src[:, :, None] if len(th_src.shape) == 2 else th_src, [128, lead, n])
            evb = bc(ev[:, None, :], [128, lead, n])
            V(lambda e: e.tensor_tensor(out=ang, in0=thb, in1=evb, op=ALU.mult), [th_tok, "ev9", "evD", "jv"], [nm + "ang"])
            V(lambda e: e.tensor_scalar(out=kk, in0=ang, scalar1=1.0 / TWO_PI, scalar2=MAGIC, op0=ALU.mult, op1=ALU.add), [nm + "ang"], [nm + "kk"])
            V(lambda e: e.tensor_scalar(out=kk, in0=kk, scalar1=-MAGIC, scalar2=None, op0=ALU.add), [nm + "kk"], [nm + "kk"])
            V(lambda e: e.scalar_tensor_tensor(out=rr, in0=kk, scalar=-TWO_PI, in1=ang, op0=ALU.mult, op1=ALU.add), [nm + "kk", nm + "ang"], [nm + "rr"])
            V(lambda e: e.tensor_scalar(out=rc, in0=rr, scalar1=math.pi / 2, scalar2=None, op0=ALU.add), [nm + "rr"], [nm + "rc"])
            V(lambda e: e.tensor_scalar(out=kk, in0=rc, scalar1=math.pi, scalar2=-TWO_PI, op0=ALU.is_gt, op1=ALU.mult), [nm + "rc", nm + "rr"], [nm + "kk"])
            V(lambda e: e.tensor_tensor(out=rc, in0=rc, in1=kk, op=ALU.add), [nm + "rc", nm + "kk"], [nm + "rc"])
            S(lambda e: e.activation(out=sn, in_=rr, func=AF.Sin, scale=SIN_SCALE), [nm + "rr"], [nm + "sn"])
            S(lambda e: e.activation(out=cs, in_=rc, func=AF.Sin, scale=SIN_SCALE), [nm + "rc"], [nm + "cs"])
            if not with_mag:
                return cs, sn, rr, None
            ea = A.buf([lead, n], F32)
            mag = A.buf([lead, n], F32)
            pre = A.buf([lead, n], F32)
            pim = A.buf([lead, n], F32)
            ab = bc(a_t[:, :, None], [128, lead, n])
            V(lambda e: e.tensor_tensor(out=ea, in0=ab, in1=evb, op=ALU.mult), ["a_t", "ev9", "evD"], [nm + "ea"])
            S(lambda e: e.activation(out=mag, in_=ea, func=AF.Exp), [nm + "ea"], [nm + "mag"])
            V(lambda e: e.tensor_tensor(out=pre, in0=mag, in1=cs, op=ALU.mult), [nm + "mag", nm + "cs"], [nm + "pre"])
            V(lambda e: e.tensor_tensor(out=pim, in0=mag, in1=sn, op=ALU.mult), [nm + "mag", nm + "sn"], [nm + "pim"])
            return pre, pim, rr, mag

        pA_re, pA_im, rA, magA = powtab(ev9, 9, "pA")
        pD_re, pD_im, _, _ = powtab(evD, 8, "pD")

        fr = A.buf([16], F32)
        fi = A.buf([16], F32)
        t1 = A.buf([16], F32)
        t2 = A.buf([16], F32)
        t3 = A.buf([16], F32)
        den = A.buf([16], F32)
        nre = A.buf([16], F32)
        lbre = pA_re[:, :, 1]
        lbim = pA_im[:, :, 1]
        V(lambda e: e.tensor_scalar(out=nre, in0=lbre, scalar1=-1.0, scalar2=None, op0=ALU.add), ["pApre"], ["nre"])
        V(lambda e: e.tensor_tensor(out=t1, in0=lr, in1=lr, op=ALU.mult), ["lr"], ["t1"])
        V(lambda e: e.tensor_tensor(out=t2, in0=li, in1=li, op=ALU.mult), ["li"], ["t2"])
        V(lambda e: e.tensor_tensor(out=den, in0=t1, in1=t2, op=ALU.add), ["t1", "t2"], ["den"])
        V(lambda e: e.reciprocal(out=den, in_=den), ["den"], ["den"])
        V(lambda e: e.tensor_tensor(out=t1, in0=nre, in1=lr, op=ALU.mult), ["nre", "lr", "den"], ["t1"])
        V(lambda e: e.tensor_tensor(out=t2, in0=lbim, in1=li, op=ALU.mult), ["pApim", "li", "den"], ["t2"])
        V(lambda e: e.tensor_tensor(out=t3, in0=t1, in1=t2, op=ALU.add), ["t1", "t2"], ["t3"])
        V(lambda e: e.tensor_tensor(out=fr, in0=t3, in1=den, op=ALU.mult), ["t3", "den"], ["fr"])
        V(lambda e: e.tensor_tensor(out=t1, in0=lbim, in1=lr, op=ALU.mult), ["pApim", "lr", "t3"], ["t1"])
        V(lambda e: e.tensor_tensor(out=t2, in0=nre, in1=li, op=ALU.mult), ["nre", "li", "t3"], ["t2"])
        V(lambda e: e.tensor_tensor(out=t3, in0=t1, in1=t2, op=ALU.subtract), ["t1", "t2", "fr"], ["t3"])
        V(lambda e: e.tensor_tensor(out=fi, in0=t3, in1=den, op=ALU.mult), ["t3", "den"], ["fi"])

        bbr = A.buf([16, 16], F32)
        bbi = A.buf([16, 16], F32)
        u1 = A.buf([16, 16], F32)
        u2 = A.buf([16, 16], F32)
        frb = bc(fr[:, :, None], [128, 16, 16])
        fib = bc(fi[:, :, None], [128, 16, 16])
        V(lambda e: e.tensor_tensor(out=u1, in0=frb, in1=Br, op=ALU.mult), ["fr", "Br"], ["u1"])
        V(lambda e: e.tensor_tensor(out=u2, in0=fib, in1=Bi, op=ALU.mult), ["fi", "Bi"], ["u2"])
        V(lambda e: e.tensor_tensor(out=bbr, in0=u1, in1=u2, op=ALU.subtract), ["u1", "u2"], ["bbr"])
        V(lambda e: e.tensor_tensor(out=u1, in0=frb, in1=Bi, op=ALU.mult), ["fr", "Bi", "bbr"], ["u1"])
        V(lambda e: e.tensor_tensor(out=u2, in0=fib, in1=Br, op=ALU.mult), ["fi", "Br", "bbr"], ["u2"])
        V(lambda e: e.tensor_tensor(out=bbi, in0=u1, in1=u2, op=ALU.add), ["u1", "u2"], ["bbi"])

        HTD_re = A.buf([16, 8, 16], F32)
        HTD_im = A.buf([16, 8, 16], F32)
        GA_re = A.buf([16, 9, 16], F32)
        GA_nim = A.buf([16, 9, 16], F32)
        w1 = A.buf([16, 9, 16], F32)
        w2 = A.buf([16, 9, 16], F32)

        def cplx_tab(eng_a, eng_b, pr, pi_, ptok, xr, xi, xtok, n, o_re, o_second, second_mode, otok):
            prb = bc(pr[:, :, :, None], [128, 16, n, 16])
            pib = bc(pi_[:, :, :, None], [128, 16, n, 16])
            xrb = bc(xr[:, :, None, :], [128, 16, n, 16])
            xib = bc(xi[:, :, None, :], [128, 16, n, 16])
            a1 = w1[:, :, 0:n, :]
            a2 = w2[:, :, 0:n, :]
            a3 = a1
            a4 = a2
            P.op(eng_a, lambda e: e.tensor_tensor(out=a1, in0=prb, in1=xrb, op=ALU.mult), [ptok + "pre", xtok[0]], ["w1"])
            P.op(eng_a, lambda e: e.tensor_tensor(out=a2, in0=pib, in1=xib, op=ALU.mult), [ptok + "pim", xtok[1]], ["w2"])
            P.op(eng_a, lambda e: e.tensor_tensor(out=o_re, in0=a1, in1=a2, op=ALU.subtract), ["w1", "w2"], [otok + "_re"])
            P.op(eng_b, lambda e: e.tensor_tensor(out=a3, in0=prb, in1=xib, op=ALU.mult), [ptok + "pre", xtok[1]], ["w1"])
            P.op(eng_b, lambda e: e.tensor_tensor(out=a4, in0=pib, in1=xrb, op=ALU.mult), [ptok + "pim", xtok[0]], ["w2"])
            if second_mode > 0:
                P.op(eng_b, lambda e: e.tensor_tensor(out=o_second, in0=a3, in1=a4, op=ALU.add), ["w1", "w2"], [otok + "_2"])
            else:
                P.op(eng_b, lambda e: e.tensor_scalar(out=a3, in0=a3, scalar1=-1.0, scalar2=None, op0=ALU.mult), ["w1"], ["w1"])
                P.op(eng_b, lambda e: e.tensor_tensor(out=o_second, in0=a3, in1=a4, op=ALU.subtract), ["w1", "w2"], [otok + "_2"])

        cplx_tab("vector", "gpsimd", pD_re, pD_im, "pD", bbr, bbi, ("bbr", "bbi"), 8, HTD_re, HTD_im, +1, "HTD")
        cplx_tab("vector", "gpsimd", pA_re, pA_im, "pA", Cr, Ci, ("cnat_r128", "cnat_i128"), 9, GA_re, GA_nim, -1, "GA")

        V(lambda e: e.tensor_copy(out=r8tab, in_=magA[:, :, 8]), ["pAmag"], ["r8tab"])
        psi = rA[:, :, 8]
        ang128 = A.buf([16], F32)
        k128 = A.buf([16], F32)
        r128 = A.buf([16], F32)
        rc128 = A.buf([16], F32)
        sn128 = A.buf([16], F32)
        cs128 = A.buf([16], F32)
        V(lambda e: e.tensor_scalar(out=ang128, in0=psi, scalar1=128.0, scalar2=None, op0=ALU.mult), ["pArr"], ["ang128"])
        V(lambda e: e.tensor_scalar(out=k128, in0=ang128, scalar1=1.0 / TWO_PI, scalar2=MAGIC, op0=ALU.mult, op1=ALU.add), ["ang128"], ["k128"])
        V(lambda e: e.tensor_scalar(out=k128, in0=k128, scalar1=-MAGIC, scalar2=None, op0=ALU.add), ["k128"], ["k128"])
        V(lambda e: e.scalar_tensor_tensor(out=r128, in0=k128, scalar=-TWO_PI, in1=ang128, op0=ALU.mult, op1=ALU.add), ["k128", "ang128"], ["r128"])
        V(lambda e: e.tensor_scalar(out=rc128, in0=r128, scalar1=math.pi / 2, scalar2=None, op0=ALU.add), ["r128"], ["rc128"])
        V(lambda e: e.tensor_scalar(out=k128, in0=rc128, scalar1=math.pi, scalar2=-TWO_PI, op0=ALU.is_gt, op1=ALU.mult), ["rc128", "r128"], ["k128"])
        V(lambda e: e.tensor_tensor(out=rc128, in0=rc128, in1=k128, op=ALU.add), ["rc128", "k128"], ["rc128"])
        S(lambda e: e.activation(out=sn128, in_=r128, func=AF.Sin, scale=SIN_SCALE), ["r128"], ["sn128"])
        S(lambda e: e.activation(out=cs128, in_=rc128, func=AF.Sin, scale=SIN_SCALE), ["rc128"], ["cs128"])
        V(lambda e: e.tensor_copy(out=C1t, in_=bc(cs128[:, :, None], [128, 16, 2])), ["cs128"], ["C1t"])
        V(lambda e: e.tensor_scalar(out=C2t[:, :, 0], in0=sn128, scalar1=-1.0, scalar2=None, op0=ALU.mult), ["sn128"], ["C2t"])
        V(lambda e: e.tensor_copy(out=C2t[:, :, 1], in_=sn128), ["sn128", "C2t"], ["C2t"])
        V(lambda e: e.memset(wcar, 0.0), [], ["wcar"])

        angj = w1.rearrange('p a b c -> p (a b c)')[:, 0:2048].rearrange('p (a b) -> p a b', a=16)
        kj = w2.rearrange('p a b c -> p (a b c)')[:, 0:2048].rearrange('p (a b) -> p a b', a=16)
        rj = angj
        rcj = kj
        psib = bc(psi[:, :, None], [128, 16, 128])
        jvb = bc(jv[:, None, :], [128, 16, 128])
        V(lambda e: e.tensor_tensor(out=angj, in0=psib, in1=jvb, op=ALU.mult), ["pArr", "jv", "GA_re", "GA_2", "HTD_re", "HTD_2"], ["w1"])
        V(lambda e: e.tensor_scalar(out=kj, in0=angj, scalar1=1.0 / TWO_PI, scalar2=MAGIC, op0=ALU.mult, op1=ALU.add), ["w1", "GA_re", "GA_2", "HTD_re", "HTD_2"], ["w2"])
        V(lambda e: e.tensor_scalar(out=kj, in0=kj, scalar1=-MAGIC, scalar2=None, op0=ALU.add), ["w2"], ["w2"])
        V(lambda e: e.scalar_tensor_tensor(out=rj, in0=kj, scalar=-TWO_PI, in1=angj, op0=ALU.mult, op1=ALU.add), ["w2", "w1"], ["w1"])
        V(lambda e: e.tensor_scalar(out=Ec, in0=rj, scalar1=math.pi / 2, scalar2=None, op0=ALU.add), ["w1"], ["Ec"])
        V(lambda e: e.tensor_scalar(out=kj, in0=Ec, scalar1=math.pi, scalar2=-TWO_PI, op0=ALU.is_gt, op1=ALU.mult), ["Ec", "w1"], ["w2"])
        V(lambda e: e.tensor_tensor(out=Ec, in0=Ec, in1=kj, op=ALU.add), ["Ec", "w2"], ["Ec"])
        S(lambda e: e.activation(out=Es, in_=rj, func=AF.Sin, scale=SIN_SCALE), ["w1"], ["Es"])
        S(lambda e: e.activation(out=Ec, in_=Ec, func=AF.Sin, scale=SIN_SCALE), ["Ec"], ["Ec"])

        V(lambda e: e.memset(Hpad, 0.0), [], ["Hpad"])
        V(lambda e: e.memset(Gpad, 0.0), [], ["Gpad"])
        V(lambda e: e.memset(M0, 0.0), [], ["M0"])
        cnt = 0
        for reim, (HT, htok) in enumerate(((HTD_re, "HTD_re"), (HTD_im, "HTD_2"))):
            for ghq in range(4):
                i, ps, pt = nextps()
                for l in range(4):
                    gh = ghq * 4 + l
                    T((lambda ps, HT, gh, l: lambda e: e.transpose(out=ps[:, l * 128:(l + 1) * 128],
                                                                   in_=HT[:, gh].rearrange("p s c -> p (s c)"), identity=ident_f))(ps, HT, gh, l),
                      [htok, "ident_f"], [pt])
                for l in range(4):
                    gh = ghq * 4 + l
                    for g2 in range(2):
                        fn = (lambda ps, gh, g2, l, reim: lambda e: e.tensor_copy(
                            out=Hpad[:, 2 * gh + g2, reim, g2 * 64:(g2 + 1) * 64],
                            in_=ps[:, l * 128 + g2 * 64: l * 128 + (g2 + 1) * 64]))(ps, gh, g2, l, reim)
                        if cnt % 2 == 0:
                            V(fn, [pt, "Hpad"], ["Hpad"])
                        else:
                            S((lambda ps, gh, g2, l, reim: lambda e: e.copy(
                                out=Hpad[:, 2 * gh + g2, reim, g2 * 64:(g2 + 1) * 64],
                                in_=ps[:, l * 128 + g2 * 64: l * 128 + (g2 + 1) * 64]))(ps, gh, g2, l, reim), [pt, "Hpad"], ["Hpad"])
                        cnt += 1
        for reim, (GT, gtok) in enumerate(((GA_re, "GA_re"), (GA_nim, "GA_2"))):
            for g2 in range(2):
                pp = slice(g2 * 64, (g2 + 1) * 64)
                V((lambda GT, pp, g2, reim: lambda e: e.tensor_copy(
                    out=Gpad[pp].rearrange("p (gh g2) r n -> p gh g2 r n", g2=2)[:, :, g2, reim, :],
                    in_=GT[pp, :, 1:9, :].rearrange("p g t c -> p g (t c)")))(GT, pp, g2, reim), [gtok, "Gpad"], ["Gpad"])
        BBr = A.buf([32, 16], F32)
        BBi = A.buf([32, 16], F32)
        Kcb = A.buf([32, 128], BF16)
        dd = A.buf([32, 16], F32)
        V(lambda e: e.memset(BBr, 0.0), [], ["BBr"])
        V(lambda e: e.memset(BBi, 0.0), [], ["BBi"])
        for BB, src, stok, btok in ((BBr, bbr, "bbr", "BBr"), (BBi, bbi, "bbi", "BBi")):
            for g2 in range(2):
                pp = slice(g2 * 64, (g2 + 1) * 64)
                V((lambda BB, src, pp, g2: lambda e: e.tensor_copy(
                    out=BB[pp].rearrange("p (gh g2) c -> p gh g2 c", g2=2)[:, :, g2, :], in_=src[pp]))(BB, src, pp, g2),
                  [stok, btok], [btok])
        V(lambda e: e.tensor_tensor(out=dd[0:16], in0=bc(ident_f[0:16, None, 0:16], [16, 32, 16]), in1=bc(dT[0:16, :, None], [16, 32, 16]), op=ALU.mult),
          ["ident_f", "dT"], ["dd"])
        for gq in range(8):
            i, ps, pt = nextps()
            for l in range(4):
                g = gq * 4 + l
                gh = g // 2
                T((lambda ps, g, gh, l: lambda e: e.matmul(ps[0:16, l * 128:(l + 1) * 128], lhsT=BBr[:, g, :],
                                                          rhs=GA_re[:, gh, 0:8, :].rearrange("p t c -> p (t c)"), start=True, stop=False))(ps, g, gh, l),
                  ["BBr", "GA_re"], [pt])
                T((lambda ps, g, gh, l: lambda e: e.matmul(ps[0:16, l * 128:(l + 1) * 128], lhsT=BBi[:, g, :],
                                                          rhs=GA_nim[:, gh, 0:8, :].rearrange("p t c -> p (t c)"), start=False, stop=True))(ps, g, gh, l),
                  ["BBi", "GA_2"], [pt])
            V((lambda ps, gq: lambda e: e.tensor_copy(out=Kcb[0:16, gq * 4:(gq + 1) * 4, :].rearrange("p g n -> p (g n)"), in_=ps[0:16, :]))(ps, gq),
              [pt], ["Kcb"])
            V((lambda ps, gq: lambda e: e.tensor_tensor(out=Kcb[0:16, gq * 4:(gq + 1) * 4, 0:16],
                                                        in0=ps[0:16, :].rearrange("p (g n) -> p g n", g=4)[:, :, 0:16],
                                                        in1=dd[0:16, gq * 4:(gq + 1) * 4, :], op=ALU.add))(ps, gq), [pt, "dd", "Kcb"], ["Kcb"])
        for s8 in range(8):
            DMA((lambda s8: lambda e: e.dma_start(out=M0[16 * s8:16 * s8 + 16, :, 16 * s8:128], in_=Kcb[0:16, :, 0:128 - 16 * s8]))(s8),
                ["Kcb", "M0"], ["M0"], key="m0asm")

        P.fence(fence_fns)
        A.release(m0)
        xa = A.buf([2, D], F32)
        xb = A.buf([8, D], BF16)
        xT = A.buf([8, 1024], BF16)
        u_oct = A.buf([32, 8, 16], BF16)
        U8s = [A.buf([32, 128], BF16) for _ in range(2)]
        wins = [A.buf([4, 2, 128], F32) for _ in range(2)]
        Wts = [A.buf([4, 2, 128], F32) for _ in range(2)]
        Xb = A.buf([16, 2, 129], BF16)
        g_oct = A.buf([8, 512], BF16)
        g_fms = [A.buf([1024], BF16) for _ in range(2)]
        rts = [[A.buf([4, 128], F32) for _ in range(4)] for _ in range(2)]
        ctas = [A.buf([4, 2], F32) for _ in range(2)]
        ctbs = [A.buf([4, 2], F32) for _ in range(2)]
        brow_f = A.buf([512], F32)
        brow_b = A.buf([512], BF16)
        V(lambda e: e.memset(Xb, 0.0), [], ["Xb0", "Xb1", "Xb2", "Xb3"])
        V(lambda e: e.memset(ones_b, 0.0), [], ["ones_b"])
        V(lambda e: e.memset(ones_b[0:1, :], 1.0), ["ones_b"], ["ones_b"])
        V(lambda e: e.memset(brow_b, 0.0), [], ["brow_b"])
        DMA(lambda e: e.dma_start(out=brow_f[0:1, :], in_=b_in[0:512].partition_broadcast(1)), [], ["brow_f"])
        V(lambda e: e.tensor_copy(out=brow_b[0:1, :], in_=brow_f[0:1, :]), ["brow_f", "brow_b"], ["brow_b"])
        print("ARENA top (phase A)", A.top, "of", A.n)
        wt_ctr = [0]
        gfm_ctr = [0]

        def load_xb(stile):
            for hh in range(4):
                r0 = stile * 1024 + hh * 256
                DMA((lambda r0: lambda e: e.dma_start(out=xa, in_=x[r0:r0 + 256, :].rearrange("(s p) d -> p s d", p=128)))(r0), [], ["xa"], key="xald")
                for sl in range(2):
                    sub = hh * 2 + sl
                    S((lambda sl, sub: lambda e: e.copy(out=xb[:, sub, :], in_=xa[:, sl, :]))(sl, sub), ["xa"], ["xb%d" % sub])

        def phaseA_front(stile):
            P.tag = 'A_fe%d' % stile
            t0 = stile * 1024
            U8t = U8s[stile % 2]
            ut = "U8_%d_" % (stile % 2)
            if stile == 0:
                load_xb(0)
            for hh in range(2):
                for k in range(8):
                    i, ps, pt = nextps()
                    pb = psbf(i)
                    for sl in range(4):
                        sub = hh * 4 + sl
                        T((lambda pb, sub, sl, k: lambda e: e.transpose(out=pb[:, sl * 128:(sl + 1) * 128], in_=xb[:, sub, k * 128:(k + 1) * 128],
                                                                        identity=ident_b))(pb, sub, sl, k), ["xb%d" % sub, "ident_b"], [pt])
                    S((lambda pb, k, hh: lambda e: e.copy(out=xT[:, k, hh * 512:(hh + 1) * 512], in_=pb[:, 0:512]))(pb, k, hh), [pt], ["xT%d" % k])
            if stile + 1 < 4:
                load_xb(stile + 1)
            if stile == 0:
                convert_chunk(0)
                convert_chunk(1)
            elif stile == 1:
                convert_chunk(2)
                convert_chunk(3)
            elif stile == 2:
                retile_chunk(0)
                retile_chunk(1)
                retile_chunk(2)
            else:
                retile_chunk(3)
            for s8 in range(8):
                i, ps, pt = nextps()
                for k in range(8):
                    T((lambda ps, s8, k: lambda e: e.matmul(ps, lhsT=xT[:, k, :].rearrange("p (j s) -> p s j", s=8)[:, s8, :],
                                                           rhs=winu[:, k, :], start=(k == 0), stop=False))(ps, s8, k),
                      ["xT%d" % k, "winu"], [pt])
                T((lambda ps: lambda e: e.matmul(ps, lhsT=ones_b, rhs=brow_b, start=False, stop=True))(ps), ["ones_b", "brow_b"], [pt])
                S((lambda ps, s8: lambda e: e.copy(out=u_oct[:, :, s8, :], in_=ps.rearrange("p (g c) -> p g c", c=16)))(ps, s8),
                  [pt], ["u_oct%d" % s8])
            for gq in range(4):
                i, ps, pt = nextps()
                pb = psbf(i)
                for l in range(8):
                    g = gq * 8 + l
                    T((lambda pb, g, l: lambda e: e.transpose(out=pb[:, l * 128:(l + 1) * 128], in_=u_oct[:, g].rearrange("p s c -> p (s c)"),
                                                              identity=ident_b))(pb, g, l), ["u_oct%d" % s for s in range(8)] + ["ident_b"], [pt])
                S((lambda pb, gq, U8t: lambda e: e.copy(out=U8t[:, gq * 8:(gq + 1) * 8, :].rearrange("p g j -> p (g j)"), in_=pb))(pb, gq, U8t),
                  [pt], [ut + "%d" % gq])

        def phaseA_scan(stile):
            P.tag = 'A_V%d' % stile
            U8t = U8s[stile % 2]
            ut = "U8_%d_" % (stile % 2)
            for pair in range(2):
                qs = (2 * pair, 2 * pair + 1)
                ctx = []
                for si, q in enumerate(qs):
                    ir, psr, ptr = nextps()
                    ii, psi_, pti = nextps()
                    for l in range(4):
                        gh = q * 4 + l
                        for reim, ps in ((0, psr), (1, psi_)):
                            for g2 in range(2):
                                g = 2 * gh + g2
                                T((lambda ps, g, reim, l, g2: lambda e: e.matmul(ps[:, l * 128:(l + 1) * 128], lhsT=Hpad[:, g, reim, :], rhs=U8t[:, g, :],
                                                                                 start=(g2 == 0), stop=(g2 == 1)))(ps, g, reim, l, g2),
                                  ["Hpad", ut + "%d" % (g // 8)], [ptr if reim == 0 else pti])
                    gsl = slice(q * 4, (q + 1) * 4)
                    ctx.append(dict(q=q, si=si, gsl=gsl, ptr=ptr, pti=pti,
                                    vr=psr.rearrange("p (g j) -> p g j", g=4), vi=psi_.rearrange("p (g j) -> p g j", g=4),
                                    ec=Ec[:, gsl, :], es=Es[:, gsl, :], win=wins[si], Wt=Wts[si], r=rts[si],
                                    wint="win%d" % si, wtok="Wt%d" % si, rt=["rt%d_%d" % (si, i_) for i_ in range(4)],
                                    cta=ctas[si], ctb=ctbs[si], ctt=["cta%d" % si, "ctb%d" % si], xtk="Xb%d" % q))
                for c in ctx:
                    V((lambda c: lambda e: e.tensor_tensor(out=c["r"][0], in0=c["vr"], in1=c["ec"], op=ALU.mult))(c), [c["ptr"], "Ec"], [c["rt"][0]])
                for c in ctx:
                    V((lambda c: lambda e: e.tensor_tensor(out=c["r"][1], in0=c["vi"], in1=c["es"], op=ALU.mult))(c), [c["pti"], "Es"], [c["rt"][1]])
                for c in ctx:
                    V((lambda c: lambda e: e.tensor_tensor(out=c["r"][2], in0=c["vi"], in1=c["ec"], op=ALU.mult))(c), [c["pti"], "Ec"], [c["rt"][2]])
                for c in ctx:
                    V((lambda c: lambda e: e.tensor_tensor(out=c["r"][3], in0=c["vr"], in1=c["es"], op=ALU.mult))(c), [c["ptr"], "Es"], [c["rt"][3]])
                for c in ctx:
                    V((lambda c: lambda e: e.tensor_tensor(out=c["win"][:, :, 0, :], in0=c["r"][0], in1=c["r"][1], op=ALU.add))(c),
                      [c["rt"][0], c["rt"][1]], [c["wint"]])
                for c in ctx:
                    V((lambda c: lambda e: e.tensor_tensor(out=c["win"][:, :, 1, :], in0=c["r"][2], in1=c["r"][3], op=ALU.subtract))(c),
                      [c["rt"][2], c["rt"][3], c["wint"]], [c["wint"]])
                for l in range(4):
                    for reim in range(2):
                        for c in ctx:
                            gh = c["q"] * 4 + l
                            V((lambda c, gh, l, reim: lambda e: e.tensor_tensor_scan(
                                out=c["Wt"][:, l, reim, :], data0=bc(r8tab[:, gh:gh + 1], [128, 128]), data1=c["win"][:, l, reim, :],
                                initial=wcar[:, gh, reim:reim + 1], op0=ALU.mult, op1=ALU.add))(c, gh, l, reim),
                              [c["wint"], "r8tab", "wcar%d" % c["q"]], [c["wtok"]])
                for c in ctx:
                    Wt = c["Wt"]
                    wl = Wt[:, :, :, 127]
                    V((lambda c, wl: lambda e: e.tensor_tensor(out=c["cta"], in0=C1t[:, c["gsl"], :], in1=wl, op=ALU.mult))(c, wl), ["C1t", c["wtok"]], [c["ctt"][0]])
                for c in ctx:
                    Wt = c["Wt"]
                    wl_sw = bass.AP(tensor=Wt.tensor, offset=Wt[:, :, 1, 127].offset, ap=[list(Wt.ap[0]), list(Wt.ap[1]), [-Wt.ap[2][0], 2]])
                    V((lambda c, wl_sw: lambda e: e.tensor_tensor(out=c["ctb"], in0=C2t[:, c["gsl"], :], in1=wl_sw, op=ALU.mult))(c, wl_sw),
                      ["C2t", c["wtok"]], [c["ctt"][1]])
                for c in ctx:
                    V((lambda c: lambda e: e.tensor_tensor(out=wcar[:, c["gsl"], :], in0=c["cta"], in1=c["ctb"], op=ALU.add))(c), c["ctt"], ["wcar%d" % c["q"]])
                for c in ctx:
                    V((lambda c: lambda e: e.tensor_copy(out=Xb[:, c["gsl"], :, 0], in_=Xb[:, c["gsl"], :, 128]))(c), [c["xtk"]], [c["xtk"]])
                for c in ctx:
                    V((lambda c: lambda e: e.tensor_tensor(out=c["r"][0], in0=c["Wt"][:, :, 0, :], in1=c["ec"], op=ALU.mult))(c), [c["wtok"], "Ec"], [c["rt"][0]])
                for c in ctx:
                    V((lambda c: lambda e: e.tensor_tensor(out=c["r"][1], in0=c["Wt"][:, :, 1, :], in1=c["es"], op=ALU.mult))(c), [c["wtok"], "Es"], [c["rt"][1]])
                for c in ctx:
                    V((lambda c: lambda e: e.tensor_tensor(out=c["r"][2], in0=c["Wt"][:, :, 0, :], in1=c["es"], op=ALU.mult))(c), [c["wtok"], "Es"], [c["rt"][2]])
                for c in ctx:
                    V((lambda c: lambda e: e.tensor_tensor(out=c["r"][3], in0=c["Wt"][:, :, 1, :], in1=c["ec"], op=ALU.mult))(c), [c["wtok"], "Ec"], [c["rt"][3]])
                for c in ctx:
                    V((lambda c: lambda e: e.tensor_tensor(out=Xb[:, c["gsl"], 0, 1:129], in0=c["r"][0], in1=c["r"][1], op=ALU.subtract))(c),
                      [c["rt"][0], c["rt"][1], c["xtk"]], [c["xtk"]])
                for c in ctx:
                    V((lambda c: lambda e: e.tensor_tensor(out=Xb[:, c["gsl"], 1, 1:129], in0=c["r"][2], in1=c["r"][3], op=ALU.add))(c),
                      [c["rt"][2], c["rt"][3], c["xtk"]], [c["xtk"]])

        def phaseA_out(stile):
            P.tag = 'A_Y%d' % stile
            t0 = stile * 1024
            U8t = U8s[stile % 2]
            ut = "U8_%d_" % (stile % 2)
            for gq in range(8):
                i, ps, pt = nextps()
                for l in range(4):
                    g = gq * 4 + l
                    gh = g // 2
                    o_ = ps[:, l * 128:(l + 1) * 128]
                    xtk = "Xb%d" % (gh // 4)
                    T((lambda o_, g, U8t: lambda e: e.matmul(o_, lhsT=U8t[:, g, :], rhs=M0[:, g, :], start=True, stop=False))(o_, g, U8t),
                      [ut + "%d" % (g // 8), "M0"], [pt])
                    T((lambda o_, g, gh: lambda e: e.matmul(o_, lhsT=Xb[:, gh, 0, 0:128], rhs=Gpad[:, g, 0, :], start=False, stop=False))(o_, g, gh),
                      [xtk, "Gpad"], [pt])
                    T((lambda o_, g, gh: lambda e: e.matmul(o_, lhsT=Xb[:, gh, 1, 0:128], rhs=Gpad[:, g, 1, :], start=False, stop=True))(o_, g, gh),
                      [xtk, "Gpad"], [pt])
                S((lambda ps, gq: lambda e: e.activation(
                    out=g_oct[:, :, gq * 64:(gq + 1) * 64].rearrange("p t (g c) -> p t g c", g=4),
                    in_=ps.rearrange("p (g t c) -> p t g c", g=4, t=8), func=AF.Gelu_apprx_tanh))(ps, gq),
                  [pt], ["g_oct"])
            P.tag = 'A_gT%d' % stile
            for cb in range(4):
                i, ps, pt = nextps()
                pb = psbf(i)
                for t8 in range(8):
                    T((lambda pb, t8, cb: lambda e: e.transpose(out=pb[:, t8 * 128:(t8 + 1) * 128], in_=g_oct[:, t8, cb * 128:(cb + 1) * 128],
                                                                identity=ident_b))(pb, t8, cb), ["g_oct", "ident_b"], [pt])
                gs_ = gfm_ctr[0] % 2
                gfm_ctr[0] += 1
                gfm = g_fms[gs_]
                S((lambda pb, gfm: lambda e: e.copy(out=gfm.rearrange("p (j t) -> p t j", t=8),
                                                    in_=pb.rearrange("p (t j) -> p t j", t=8)))(pb, gfm), [pt], ["g_fm%d" % gs_])
                DMA((lambda cb, t0, gfm: lambda e: e.dma_start(out=g_s[cb * 128:(cb + 1) * 128, t0:t0 + 1024], in_=gfm))(cb, t0, gfm),
                    ["g_fm%d" % gs_], ["g_s%d" % stile], key="gst%d" % gs_, eng="gpsimd")

        phaseA_front(0)
        for stile in range(4):
            phaseA_scan(stile)
            if stile + 1 < 4:
                phaseA_front(stile + 1)
            phaseA_out(stile)

        P.fence(fence_fns)
        A.release(mA)
        g1bc = A.buf([D], F32)
        b1bc = None
        g2bc = A.buf([D], F32)
        b2bc = A.buf([D], F32)
        glu_sb = A.buf([4, 512], BF16)
        wso_sb = A.buf([4, D], BF16)
        wco_sb = A.buf([4, D], BF16)
        wo_sb = A.buf([8, D], BF16)
        for dst, src, tok in ((g1bc, ln1_g, "g1bc"), (g2bc, ln2_g, "g2bc"), (b2bc, ln2_b, "b2bc")):
            DMA((lambda dst, src: lambda e: e.dma_start(out=dst, in_=src.partition_broadcast(128)))(dst, src), [], [tok])
        V(lambda e: e.tensor_scalar(out=g1bc, in0=g1bc, scalar1=ALPHA, scalar2=None, op0=ALU.mult), ["g1bc"], ["g1bc"])
        brow2 = A.buf([D], BF16)
        NT = 8
        g_tb = A.buf([4, 512], BF16)
        xb2 = A.buf([4, D], BF16)
        xT2 = A.buf([8, 512], BF16)
        xbx = A.buf([4, D], BF16)
        xTx = A.buf([8, 512], BF16)
        NWS = 3
        wgrp = [A.buf([8, 384], BF16) for _ in range(NWS)]
        NTMP = 8
        tmps = [A.buf([512], F32) for _ in range(NTMP)]
        bz = A.buf([4, 512], BF16)
        merged = A.buf([8, 512], BF16)
        x1 = A.buf([4, D], F32)
        stats = A.buf([4, 2, 6], F32)
        mv = A.buf([4, 2], F32)
        rstd4 = A.buf([4], F32)
        nmr4 = A.buf([4], F32)
        NFS = 3
        ffw = [A.buf([2, 8, 128], BF16) for _ in range(NFS)]
        NDS = 3
        wdb = [A.buf([D], BF16) for _ in range(NDS)]
        hid = A.buf([NFB, 512], BF16)
        brow2f = hid[:, 0:4, :].rearrange("p a b -> p (a b)").bitcast(F32)
        V(lambda e: e.memset(brow2, 0.0), [], ["brow2"])
        DMA(lambda e: e.dma_start(out=brow2f[0:1, :], in_=ln1_b.partition_broadcast(1)), [], ["hid0", "hid1", "hid2", "hid3"])
        V(lambda e: e.tensor_scalar(out=brow2[0:1, :], in0=brow2f[0:1, :], scalar1=ALPHA, scalar2=None, op0=ALU.mult),
          ["hid0", "hid1", "hid2", "hid3", "brow2"], ["brow2"])
        V(lambda e: e.memset(eps_t, LN_EPS), [], ["eps_t"])
        print("ARENA top (phase B)", A.top, "of", A.n)

        tmp_ctr = [0]

        def tmp():
            i = tmp_ctr[0] % NTMP
            tmp_ctr[0] += 1
            return tmps[i], "tmp%d" % i

        wg_ctr = [0]

        def load_wgrp(kind, idx):
            slot = wg_ctr[0] % NWS
            wg_ctr[0] += 1
            if kind == "cv":
                DMA((lambda slot, idx: lambda e: e.dma_start(out=wgrp[slot], in_=wcv_s[idx]))(slot, idx),
                    ["wcv_s"], ["wgrp%d" % slot], key="wgrp%d" % slot)
            else:
                DMA((lambda slot, idx: lambda e: e.dma_start(out=wgrp[slot][:, :, 0:256], in_=wgt_s[idx]))(slot, idx),
                    ["wgt_s"], ["wgrp%d" % slot], key="wgrp%d" % slot)
            return slot

        def ln_stage(gbc, bbc, gtok, btok):
            xt = ["x1_%d" % s_ for s_ in range(4)]
            for sub in range(4):
                for half in range(2):
                    V((lambda sub, half: lambda e: e.bn_stats(out=stats[:, sub, half, :], in_=x1[:, sub, half * 512:(half + 1) * 512]))(sub, half),
                      [xt[sub]], ["stats%d" % sub])
                V((lambda sub: lambda e: e.bn_aggr(out=mv[:, sub, :], in_=stats[:, sub].rearrange("p a b -> p (a b)")))(sub), ["stats%d" % sub], ["mv"])
            def part_b():
                S(lambda e: e.activation(out=rstd4, in_=mv[:, :, 1], func=AF.Sqrt, bias=eps_t, scale=1.0), ["mv", "eps_t"], ["rstd4"])
                V(lambda e: e.reciprocal(out=rstd4, in_=rstd4), ["rstd4"], ["rstd4"])
                for sub in range(4):
                    V((lambda sub: lambda e: e.tensor_scalar(out=x1[:, sub, :], in0=x1[:, sub, :], scalar1=mv[:, sub, 0:1], scalar2=rstd4[:, sub:sub + 1],
                                                             op0=ALU.subtract, op1=ALU.mult))(sub), [xt[sub], "mv", "rstd4"], [xt[sub]])
            deferred = []
            for sub in range(4):
                deferred.append((lambda sub: lambda: V((lambda sub: lambda e: e.tensor_tensor(out=x1[:, sub, :], in0=x1[:, sub, :], in1=gbc, op=ALU.mult))(sub),
                                                       [xt[sub], gtok], [xt[sub]]))(sub))
                deferred.append((lambda sub: lambda: V((lambda sub: lambda e: e.tensor_tensor(out=x1[:, sub, :], in0=x1[:, sub, :], in1=bbc, op=ALU.add))(sub),
                                                       [xt[sub], btok], [xt[sub]]))(sub))
            return [part_b] + deferred

        def load_x_bf16(t):
            for sub in range(4):
                r0 = t * 512 + sub * 128
                DMA((lambda sub, r0: lambda e: e.dma_start(out=xbx[:, sub, :], in_=x[r0:r0 + 128, :]))(sub, r0),
                    [], ["xbx_%d" % sub], key="xld%d" % sub, eng="gpsimd")

        def emit_xT():
            for k in range(8):
                i, ps, pt = nextps()
                pb = psbf(i)
                for sub in range(4):
                    T((lambda pb, sub, k: lambda e: e.transpose(out=pb[:, sub * 128:(sub + 1) * 128], in_=xbx[:, sub, k * 128:(k + 1) * 128],
                                                                identity=ident_b))(pb, sub, k), ["xbx_%d" % sub, "ident_b"], [pt])
                S((lambda pb, k: lambda e: e.copy(out=xTx[:, k, :], in_=pb[:, 0:512]))(pb, k), [pt], ["xTx_%d" % k])

        wq = [("cv", 0, c) for c in range(4)]
        for t_ in range(NT):
            wq += [("gt", t_, d) for d in range(8)]
            if t_ + 1 < NT:
                wq += [("cv", t_ + 1, c) for c in range(4)]
        gslot = {}

        def pump(n):
            for _ in range(n):
                if wq:
                    kd = wq.pop(0)
                    gslot[kd] = load_wgrp(kd[0], kd[2])

        def proj_block(kd, cbl):
            slot_ = gslot[kd]
            i, ps, pt = nextps()
            for k in range(8):
                T((lambda ps, slot_, cbl, k: lambda e: e.matmul(ps, lhsT=wgrp[slot_][:, k, cbl * 128:(cbl + 1) * 128], rhs=xTx[:, k, :],
                                                               start=(k == 0), stop=(k == 7)))(ps, slot_, cbl, k),
                  ["wgrp%d" % slot_, "xTx_%d" % k], [pt])
            return ps, pt

        def glu_stage(t):
            P.tag = 'B%d_glu' % t
            glu_ps = []
            for eb in range(4):
                i, ps, pt = nextps()
                for k in range(4):
                    T((lambda ps, eb, k: lambda e: e.matmul(ps, lhsT=glu_sb[:, k, eb * 128:(eb + 1) * 128], rhs=g_tb[:, k, :],
                                                           start=(k == 0), stop=(k == 3)))(ps, eb, k), ["glu_sb", "g_tb"], [pt])
                glu_ps.append((ps, pt))
            for eb in range(4):
                ps, pt = glu_ps[eb]
                sg, sgt = tmp()
                S((lambda ps, eb, sg: lambda e: e.activation(out=sg, in_=ps, func=AF.Sigmoid, bias=glub_fm[:, eb:eb + 1], scale=1.0))(ps, eb, sg),
                  [pt, "glub_fm"], [sgt])
                V((lambda eb, sg: lambda e: e.tensor_tensor(out=g_tb[:, eb, :], in0=g_tb[:, eb, :], in1=sg, op=ALU.mult))(eb, sg), [sgt, "g_tb"], ["g_tb"])

        def conv_stage(t, cbs):
            P.tag = 'B%d_conv' % t
            for cb in cbs:
                hp, hpt = proj_block(("cv", t, cb), 0)
                cp, cpt = proj_block(("cv", t, cb), 1)
                bp, bpt = proj_block(("cv", t, cb), 2)
                pump(1)
                hsb, hsbt = tmp()
                zt, ztt = tmp()
                S((lambda hp, cb, hsb: lambda e: e.activation(out=hsb, in_=hp, func=AF.Identity, bias=bias_fm[:, 4 + cb:5 + cb], scale=1.0))(hp, cb, hsb),
                  [hpt, "bias_fm"], [hsbt])
                V((lambda cp, cb, hsb: lambda e: e.scalar_tensor_tensor(out=vbuf[:, cb, 2:514], in0=cp, scalar=bias_fm[:, 8 + cb:9 + cb], in1=hsb,
                                                                        op0=ALU.add, op1=ALU.mult))(cp, cb, hsb), [cpt, "bias_fm", hsbt], ["vbuf%d" % cb])
                V((lambda cb, zt: lambda e: e.tensor_scalar(out=zt, in0=vbuf[:, cb, 0:512], scalar1=convw_fm[:, cb:cb + 1], scalar2=None, op0=ALU.mult))(cb, zt),
                  ["vbuf%d" % cb, "convw_fm"], [ztt])
                V((lambda cb, zt: lambda e: e.scalar_tensor_tensor(out=zt, in0=vbuf[:, cb, 1:513], scalar=convw_fm[:, 4 + cb:5 + cb], in1=zt,
                                                                   op0=ALU.mult, op1=ALU.add))(cb, zt), ["vbuf%d" % cb, "convw_fm", ztt], [ztt])
                V((lambda cb, zt: lambda e: e.scalar_tensor_tensor(out=zt, in0=vbuf[:, cb, 2:514], scalar=convw_fm[:, 8 + cb:9 + cb], in1=zt,
                                                                   op0=ALU.mult, op1=ALU.add))(cb, zt), ["vbuf%d" % cb, "convw_fm", ztt], [ztt])
                V((lambda bp, cb, zt: lambda e: e.scalar_tensor_tensor(out=bz[:, cb, :], in0=bp, scalar=bias_fm[:, 12 + cb:13 + cb], in1=zt,
                                                                       op0=ALU.add, op1=ALU.mult))(bp, cb, zt), [bpt, "bias_fm", ztt], ["bz%d" % cb])
                V((lambda cb: lambda e: e.tensor_copy(out=vbuf[:, cb, 0:2], in_=vbuf[:, cb, 512:514]))(cb), ["vbuf%d" % cb], ["vbuf%d" % cb])

        ff_ctr = [0]
        wd_ctr = [0]
        P.tag = 'B0_xT'
        load_x_bf16(0)
        DMA(lambda e: e.dma_start(out=glu_sb, in_=glu_w.rearrange("(k p) n -> p k n", p=128)), [], ["glu_sb"], eng="gpsimd")
        DMA(lambda e: e.dma_start(out=wso_sb, in_=w_ssm_out.rearrange("(k p) n -> p k n", p=128)), [], ["wso_sb"], eng="gpsimd")
        DMA(lambda e: e.dma_start(out=wco_sb, in_=w_conv_out.rearrange("(k p) n -> p k n", p=128)), [], ["wco_sb"], eng="gpsimd")
        DMA(lambda e: e.dma_start(out=wo_sb, in_=w_o.rearrange("(k p) n -> p k n", p=128)), [], ["wo_sb"], eng="gpsimd")
        pump(3)
        emit_xT()
        load_x_bf16(1)
        conv_stage(0, range(4))
        ln2_q = []
        def load_g(t):
            DMA((lambda t: lambda e: e.dma_start(out=g_tb, in_=g_s[:, t * 512:(t + 1) * 512].rearrange("(k p) n -> p k n", p=128)))(t),
                ["g_s%d" % (t // 2)], ["g_tb"], key="g_tb")

        load_g(0)
        glu_stage(0)
        for t in range(NT):
            P.tag = 'B%d_merged' % t
            for db in range(8):
                gap, gapt = proj_block(("gt", t, db), 0)
                gbp, gbpt = proj_block(("gt", t, db), 1)
                pump(1)
                i, yap, yapt = nextps()
                for k in range(4):
                    T((lambda yap, db, k: lambda e: e.matmul(yap, lhsT=wso_sb[:, k, db * 128:(db + 1) * 128], rhs=g_tb[:, k, :],
                                                            start=(k == 0), stop=(k == 3)))(yap, db, k), ["wso_sb", "g_tb"], [yapt])
                i, ybp, ybpt = nextps()
                for k in range(4):
                    T((lambda ybp, db, k: lambda e: e.matmul(ybp, lhsT=wco_sb[:, k, db * 128:(db + 1) * 128], rhs=bz[:, k, :],
                                                            start=(k == 0), stop=(k == 3)))(ybp, db, k), ["wco_sb", "bz%d" % k], [ybpt])
                sa, sat = tmp()
                sb_, sbt = tmp()
                S((lambda gap, db, sa: lambda e: e.activation(out=sa, in_=gap, func=AF.Sigmoid, bias=bias_fm[:, 16 + db:17 + db], scale=1.0))(gap, db, sa),
                  [gapt, "bias_fm"], [sat])
                S((lambda gbp, db, sb_: lambda e: e.activation(out=sb_, in_=gbp, func=AF.Sigmoid, bias=bias_fm[:, 24 + db:25 + db], scale=1.0))(gbp, db, sb_),
                  [gbpt, "bias_fm"], [sbt])
                V((lambda yap, sa: lambda e: e.tensor_tensor(out=sa, in0=yap, in1=sa, op=ALU.mult))(yap, sa), [yapt, sat], [sat])
                V((lambda ybp, sb_: lambda e: e.tensor_tensor(out=sb_, in0=ybp, in1=sb_, op=ALU.mult))(ybp, sb_), [ybpt, sbt], [sbt])
                V((lambda db, sa, sb_: lambda e: e.tensor_tensor(out=merged[:, db, :], in0=sa, in1=sb_, op=ALU.add))(db, sa, sb_), [sat, sbt], ["merged%d" % db])
                for _ in range({1: 1, 2: 2, 3: 2, 4: 2, 5: 2}.get(db, 0)):
                    if ln2_q:
                        ln2_q.pop(0)()
            ffq = list(range(NFB))
            ffslot = {}

            def ff_store(fb):
                fs_ = ffslot[fb]
                DMA((lambda fs_, fb: lambda e: e.dma_start(out=wgu_s[fb], in_=ffw[fs_]))(fs_, fb), ["ffw%d" % fs_], ["wgu%d" % fb], key="ffst%d" % fs_)

            def ffpump(n):
                for _ in range(n):
                    if ffq:
                        fb = ffq.pop(0)
                        fs = ff_ctr[0] % NFS
                        ff_ctr[0] += 1
                        ffslot[fb] = fs
                        if t == 0:
                            if fb >= 1:
                                ff_store(fb - 1)
                            for wh, wsrc in enumerate((w_gate, w_up)):
                                DMA((lambda fs, fb, wh, wsrc: lambda e: e.dma_start(
                                    out=ffw[fs][:, wh], in_=wsrc[:, fb * 128:(fb + 1) * 128].rearrange("(k p) n -> p k n", p=128)))(fs, fb, wh, wsrc),
                                    [], ["ffw%d" % fs], key="ffw%d" % fs, eng="gpsimd")
                        else:
                            DMA((lambda fs, fb: lambda e: e.dma_start(out=ffw[fs], in_=wgu_s[fb]))(fs, fb), ["wgu%d" % fb], ["ffw%d" % fs],
                                key="ffw%d" % fs)

            ffpump(NFS)
            P.tag = 'B%d_wo' % t
            for sub in range(4):
                r0 = t * 512 + sub * 128
                DMA((lambda sub, r0: lambda e: e.dma_start(out=x1[:, sub, :], in_=x[r0:r0 + 128, :]))(sub, r0), [], ["x1_%d" % sub], key="xres%d" % sub)
            for sub in range(4):
                for half in range(2):
                    i, ps, pt = nextps()
                    for k in range(8):
                        T((lambda ps, sub, half, k: lambda e: e.matmul(ps, lhsT=merged[:, k, sub * 128:(sub + 1) * 128],
                                                                      rhs=wo_sb[:, k, half * 512:(half + 1) * 512],
                                                                      start=(k == 0), stop=(k == 7)))(ps, sub, half, k), ["merged%d" % k, "wo_sb"], [pt])
                    V((lambda ps, sub, half: lambda e: e.scalar_tensor_tensor(
                        out=x1[:, sub, half * 512:(half + 1) * 512], in0=x1[:, sub, half * 512:(half + 1) * 512], scalar=ALPHA, in1=ps,
                        op0=ALU.mult, op1=ALU.add))(ps, sub, half), [pt, "x1_%d" % sub], ["x1_%d" % sub])
            xt_ = ["x1_%d" % s_ for s_ in range(4)]
            for sub in range(4):
                for half in range(2):
                    V((lambda sub, half: lambda e: e.bn_stats(out=stats[:, sub, half, :], in_=x1[:, sub, half * 512:(half + 1) * 512]))(sub, half),
                      [xt_[sub]], ["stats%d" % sub])
                V((lambda sub: lambda e: e.bn_aggr(out=mv[:, sub, :], in_=stats[:, sub].rearrange("p a b -> p (a b)")))(sub), ["stats%d" % sub], ["mv"])
            S(lambda e: e.activation(out=rstd4, in_=mv[:, :, 1], func=AF.Sqrt, bias=eps_t, scale=1.0), ["mv", "eps_t"], ["rstd4"])
            V(lambda e: e.reciprocal(out=rstd4, in_=rstd4), ["rstd4"], ["rstd4"])
            V(lambda e: e.scalar_tensor_tensor(out=nmr4, in0=mv[:, :, 0], scalar=-1.0, in1=rstd4, op0=ALU.mult, op1=ALU.mult), ["mv", "rstd4"], ["nmr4"])
            for sub in range(4):
                S((lambda sub: lambda e: e.activation(out=xb2[:, sub, :], in_=x1[:, sub, :], func=AF.Identity,
                                                      bias=nmr4[:, sub:sub + 1], scale=rstd4[:, sub:sub + 1]))(sub), [xt_[sub], "rstd4", "nmr4"], ["xb2_%d" % sub])
            if t + 1 < NT:
                P.tag = 'B%d_xT' % (t + 1)
                emit_xT()
                if t + 2 < NT:
                    load_x_bf16(t + 2)
                conv_stage(t + 1, (0, 1))
            P.tag = 'B%d_x1T' % t
            for k in range(8):
                i, ps, pt = nextps()
                pb = psbf(i)
                for sub in range(4):
                    T((lambda pb, sub, k: lambda e: e.transpose(out=pb[:, sub * 128:(sub + 1) * 128], in_=xb2[:, sub, k * 128:(k + 1) * 128],
                                                                identity=ident_b))(pb, sub, k), ["xb2_%d" % sub, "ident_b"], [pt])
                S((lambda pb, k: lambda e: e.activation(out=xT2[:, k, :], in_=pb[:, 0:512], func=AF.Identity,
                                                        bias=b1_fm[:, k:k + 1], scale=g1_fm[:, k:k + 1]))(pb, k), [pt, "g1_fm", "b1_fm"], ["xT2_%d" % k])
            if t + 1 < NT:
                conv_stage(t + 1, (2, 3))
            ln1_q = []
            for sub in range(4):
                ln1_q.append((lambda sub: lambda: V((lambda sub: lambda e: e.tensor_scalar(
                    out=x1[:, sub, :], in0=x1[:, sub, :], scalar1=mv[:, sub, 0:1], scalar2=rstd4[:, sub:sub + 1],
                    op0=ALU.subtract, op1=ALU.mult))(sub), [xt_[sub], "mv", "rstd4"], [xt_[sub]]))(sub))
            for sub in range(4):
                ln1_q.append((lambda sub: lambda: V((lambda sub: lambda e: e.tensor_tensor(
                    out=x1[:, sub, :], in0=x1[:, sub, :], in1=g1bc, op=ALU.mult))(sub), [xt_[sub], "g1bc"], [xt_[sub]]))(sub))
            if t + 1 < NT:
                load_g(t + 1)
            P.tag = 'B%d_ffn' % t
            wdq = list(range(NFB))
            wdslot = {}

            def wdpump(n):
                for _ in range(n):
                    if wdq:
                        fb = wdq.pop(0)
                        ds_ = wd_ctr[0] % NDS
                        wd_ctr[0] += 1
                        wdslot[fb] = ds_
                        DMA((lambda ds_, fb: lambda e: e.dma_start(out=wdb[ds_], in_=wd_s[fb * 128:(fb + 1) * 128, :]))(ds_, fb),
                            ["wd_s"], ["wdb%d" % ds_], key="wdb%d" % ds_)

            for fb in range(NFB):
                fs = ffslot[fb]
                i, gp, gpt = nextps()
                for k in range(8):
                    T((lambda gp, fs, k: lambda e: e.matmul(gp, lhsT=ffw[fs][:, 0, k, :], rhs=xT2[:, k, :],
                                                           start=(k == 0), stop=(k == 7)))(gp, fs, k), ["ffw%d" % fs, "xT2_%d" % k], [gpt])
                i, up, upt = nextps()
                for k in range(8):
                    T((lambda up, fs, k: lambda e: e.matmul(up, lhsT=ffw[fs][:, 1, k, :], rhs=xT2[:, k, :],
                                                           start=(k == 0), stop=(k == 7)))(up, fs, k), ["ffw%d" % fs, "xT2_%d" % k], [upt])
                ffpump(1)
                if fb == NFB - 3:
                    wdpump(NDS)
                sgl, sglt = tmp()
                S((lambda gp, sgl: lambda e: e.activation(out=sgl, in_=gp, func=AF.Silu))(gp, sgl), [gpt], [sglt])
                V((lambda up, fb, sgl: lambda e: e.tensor_tensor(out=hid[:, fb, :], in0=up, in1=sgl, op=ALU.mult))(up, fb, sgl), [upt, sglt], ["hid%d" % fb])
                if ln1_q:
                    ln1_q.pop(0)()
            if t == 0:
                ff_store(NFB - 1)
            if t + 1 < NT:
                glu_stage(t + 1)
            P.tag = 'B%d_down' % t
            banks = {}
            for sub in range(4):
                for half in range(2):
                    banks[(sub, half)] = nextps()
            for sub in range(4):
                for half in range(2):
                    i, ps, pt = banks[(sub, half)]
                    T((lambda ps, half: lambda e: e.matmul(ps, lhsT=ones_b, rhs=brow2[:, half * 512:(half + 1) * 512], start=True, stop=False))(ps, half),
                      ["ones_b", "brow2"], [pt])
            for fb in range(NFB):
                ds_ = wdslot[fb]
                for sub in range(4):
                    for half in range(2):
                        i, ps, pt = banks[(sub, half)]
                        T((lambda ps, ds_, fb, sub, half: lambda e: e.matmul(ps, lhsT=hid[:, fb, sub * 128:(sub + 1) * 128],
                                                                            rhs=wdb[ds_][:, half * 512:(half + 1) * 512],
                                                                            start=False, stop=(fb == NFB - 1)))(ps, ds_, fb, sub, half),
                          ["hid%d" % fb, "wdb%d" % ds_], [pt])
                wdpump(1)
            for sub in range(4):
                for half in range(2):
                    i, ps, pt = banks[(sub, half)]
                    V((lambda ps, sub, half: lambda e: e.tensor_tensor(
                        out=x1[:, sub, half * 512:(half + 1) * 512], in0=ps, in1=x1[:, sub, half * 512:(half + 1) * 512],
                        op=ALU.add))(ps, sub, half), [pt, "x1_%d" % sub], ["x1_%d" % sub])
            dfr = ln_stage(g2bc, b2bc, "g2bc", "b2bc")
            ln2_q = [dfr[0]]
            for sub in range(4):
                r0 = t * 512 + sub * 128
                ln2_q.append(dfr[1 + 2 * sub])
                ln2_q.append((lambda sub, r0, f: lambda: (f(), DMA((lambda sub, r0: lambda e: e.dma_start(out=out[r0:r0 + 128, :], in_=x1[:, sub, :]))(sub, r0),
                                                                     ["x1_%d" % sub], [], key="ost%d" % sub, eng="gpsimd")))(sub, r0, dfr[2 + 2 * sub]))
            if t == NT - 1:
                for f in ln2_q:
                    f()
                ln2_q = []

        P.emit()
        import os
        if os.environ.get('KDUMP_TAGS'):
            import json
            json.dump({e: [o.tag for o in P.ops[e] if o.fn is not None] for e in ENGS}, open(os.environ['KDUMP_TAGS'], 'w'))
    return nc


_NC_CACHE = {}


def kernel(**inputs):
    if "nc" not in _NC_CACHE:
        _NC_CACHE["nc"] = build_nc()
    nc = _NC_CACHE["nc"]
    x = np.ascontiguousarray(inputs["x"], dtype=np.float32)
    shared = {}
    for k, v in inputs.items():
        if k == "x":
            continue
        shared[k] = np.ascontiguousarray(np.asarray(v, dtype=np.float32)[0])
    in_maps = []
    for c in range(NCORES):
        m = dict(shared)
        m["x"] = x[c]
        in_maps.append(m)
    res = run_bass_kernel_spmd(nc, in_maps, core_ids=list(range(NCORES)))
    outs = [np.asarray(res.results[c]["out"], dtype=np.float32) for c in range(NCORES)]
    return np.stack(outs, axis=0)
```

```python
import math
import contextlib
import numpy as np
import concourse.bass as bass
import concourse.mybir as mybir
from concourse.bass_utils import run_bass_kernel_spmd

F32 = mybir.dt.float32
BF16 = mybir.dt.bfloat16
I32 = mybir.dt.int32
U8 = mybir.dt.uint8
ALU = mybir.AluOpType
AF = mybir.ActivationFunctionType

D = 1024
SEQ = 4096
NCORES = 8
FFN = 2816
NFB = FFN // 128
ALPHA = 2.0 ** 0.25
LN_EPS = 1e-5
TWO_PI = 2.0 * math.pi
MAGIC = 12582912.0
SIN_SCALE = 1.0 - 2e-6

ENGS = ("tensor", "vector", "scalar", "gpsimd", "sync")


class _Op:
    __slots__ = ("eng", "fn", "deps", "is_dma", "dma_key", "dma_target", "signal", "seq", "tag")

    def __init__(self, eng, fn, is_dma=False):
        self.eng = eng
        self.fn = fn
        self.deps = []
        self.is_dma = is_dma
        self.dma_key = None
        self.dma_target = 0
        self.signal = False
        self.seq = 0
        self.tag = ''


class Prog:
    def __init__(self, nc):
        self.nc = nc
        self.ops = {e: [] for e in ENGS}
        self.last_writer = {}
        self.readers = {}
        self.dma_counts = {}
        self.last_dma = {}
        self.all_ops = []
        self.tag = ''

    def _add_dep(self, op, dep):
        if dep is None or dep is op:
            return
        if (dep.eng == op.eng and not op.is_dma and not dep.is_dma
                and op.eng in ("tensor",)):
            return
        op.deps.append(dep)
        if not dep.is_dma:
            dep.signal = True

    def op(self, eng, fn, reads=(), writes=(), dma_key=None):
        is_dma = dma_key is not None
        o = _Op(eng, fn, is_dma)
        o.tag = self.tag
        for t in reads:
            self._add_dep(o, self.last_writer.get(t))
        for t in writes:
            self._add_dep(o, self.last_writer.get(t))
            for r in self.readers.get(t, ()):
                self._add_dep(o, r)
        for t in reads:
            self.readers.setdefault(t, []).append(o)
        for t in writes:
            self.last_writer[t] = o
            self.readers[t] = []
        if is_dma:
            c = self.dma_counts.get(dma_key, 0) + 16
            self.dma_counts[dma_key] = c
            o.dma_key = dma_key
            o.dma_target = c
            self.last_dma[dma_key] = o
        self.ops[eng].append(o)
        self.all_ops.append(o)
        return o

    def fence(self, fence_fns):
        fs = []
        for e, fn in fence_fns.items():
            o = _Op(e, fn)
            o.signal = True
            self.ops[e].append(o)
            self.all_ops.append(o)
            fs.append(o)
        dm = [o for k, o in self.last_dma.items() if not str(k).startswith('cv_')]
        for e in ENGS:
            g = _Op(e, None)
            g.deps = [f for f in fs] + dm
            self.ops[e].append(g)
            self.all_ops.append(g)

    def emit(self, final_wait_eng="sync"):
        nc = self.nc
        fin = _Op(final_wait_eng, None)
        fin.deps = list(self.last_dma.values())
        self.ops[final_wait_eng].append(fin)
        for e in ENGS:
            c = 0
            for o in self.ops[e]:
                if o.signal and not o.is_dma:
                    c += 1
                    o.seq = c
        with contextlib.ExitStack() as st:
            esem = {e: st.enter_context(nc.semaphore("es_" + e)) for e in ENGS}
            dsem = {}
            for i, k in enumerate(self.dma_counts):
                dsem[k] = st.enter_context(nc.semaphore("ds_%d" % i))
            block = st.enter_context(nc.Block())
            ops = self.ops

            def make(e):
                def body(eng):
                    waited = {}
                    for o in ops[e]:
                        for d in o.deps:
                            if d.is_dma:
                                s, v, k = dsem[d.dma_key], d.dma_target, ("d", d.dma_key)
                            else:
                                s, v, k = esem[d.eng], d.seq, ("e", d.eng)
                            if waited.get(k, 0) >= v:
                                continue
                            waited[k] = v
                            eng.wait_ge(s, v)
                        if o.fn is None:
                            continue
                        ins = o.fn(eng)
                        if o.is_dma:
                            ins.then_inc(dsem[o.dma_key], 16)
                        elif o.signal:
                            ins.then_inc(esem[e], 1)
                return body

            block.tensor(make("tensor"))
            block.vector(make("vector"))
            block.scalar(make("scalar"))
            block.gpsimd(make("gpsimd"))
            block.sync(make("sync"))


class Arena:
    def __init__(self, big, nbytes):
        self.big = big
        self.n = nbytes
        self.top = 0

    def buf(self, shape, dt, parts=128):
        esz = {F32: 4, BF16: 2, I32: 4}[dt]
        n = int(np.prod(shape)) * esz
        n_al = (n + 63) // 64 * 64
        off = self.top
        assert off + n_al <= self.n, ("arena overflow", off, n_al, self.n)
        self.top += n_al
        ap = self.big[0:parts, off:off + n].bitcast(dt)
        if len(shape) > 1:
            names = " ".join("d%d" % i for i in range(len(shape)))
            kw = {"d%d" % i: int(s) for i, s in enumerate(shape)}
            ap = ap.rearrange("p (%s) -> p %s" % (names, names), **kw)
        return ap

    def mark(self):
        return self.top

    def release(self, m):
        self.top = m


def bc(ap, shape):
    return ap.broadcast_to(list(shape))


def build_nc(debug=False):
    nc = bass.Bass("TRN2", target_bir_lowering=False)

    def din(name, shape):
        return nc.dram_tensor(name, list(shape), F32, kind="ExternalInput").ap()

    x = din("x", [SEQ, D])
    w_in = din("w_in", [D, 4096])
    b_in = din("b_in", [4096])
    lam_re = din("ssm_lambda_re", [32, 64])
    lam_im = din("ssm_lambda_im", [32, 64])
    log_dt = din("ssm_log_dt", [32])
    b_re = din("ssm_b_re", [32, 64, 16])
    b_im = din("ssm_b_im", [32, 64, 16])
    c_re = din("ssm_c_re", [32, 16, 64])
    c_im = din("ssm_c_im", [32, 16, 64])
    ssm_d = din("ssm_d", [512])
    glu_w = din("glu_w", [512, 512])
    glu_b = din("glu_b", [512])
    w_ssm_out = din("w_ssm_out", [512, D])
    conv_w = din("conv_w", [3, 512])
    w_conv_out = din("w_conv_out", [512, D])
    w_o = din("w_o", [D, D])
    ln1_g = din("ln1_g", [D])
    ln1_b = din("ln1_b", [D])
    w_gate = din("w_gate", [D, FFN])
    w_up = din("w_up", [D, FFN])
    w_down = din("w_down", [FFN, D])
    ln2_g = din("ln2_g", [D])
    ln2_b = din("ln2_b", [D])
    out = nc.dram_tensor("out", [SEQ, D], F32, kind="ExternalOutput").ap()

    win_r = nc.dram_tensor("win_r", [D, 3584], BF16, kind="Internal").ap()
    wcv_s = nc.dram_tensor("wcv_s", [4, 128, 8, 384], BF16, kind="Internal").ap()
    wgt_s = nc.dram_tensor("wgt_s", [8, 128, 8, 256], BF16, kind="Internal").ap()
    wgu_s = nc.dram_tensor("wgu_s", [NFB, 128, 2, 8, 128], BF16, kind="Internal").ap()
    wd_s = nc.dram_tensor("wd_s", [FFN, D], BF16, kind="Internal").ap()
    g_s = nc.dram_tensor("g_s", [512, SEQ], BF16, kind="ExternalOutput" if debug else "Internal").ap()

    ARENA_BYTES = 206 * 1024
    with contextlib.ExitStack() as st:
        big = st.enter_context(nc.sbuf_tensor("arena", [128, ARENA_BYTES], U8))
        psb = [st.enter_context(nc.psum_tensor("ps%d" % i, [128, 512], F32)) for i in range(8)]
        A = Arena(big, ARENA_BYTES)
        P = Prog(nc)

        ps_rr = [0]

        def nextps():
            i = ps_rr[0]
            ps_rr[0] = (i + 1) % 8
            return i, psb[i][:], "ps%d" % i

        def psbf(i):
            return psb[i][:].bitcast(BF16)

        def V(fn, reads, writes):
            return P.op("vector", fn, reads, writes)

        def S(fn, reads, writes):
            return P.op("scalar", fn, reads, writes)

        def G(fn, reads, writes):
            return P.op("gpsimd", fn, reads, writes)

        def T(fn, reads, writes):
            return P.op("tensor", fn, reads, writes)

        dma_ctr = [0]

        def DMA(fn, reads, writes, key=None, eng="sync"):
            if key is None:
                key = "dma%d" % dma_ctr[0]
                dma_ctr[0] += 1
            return P.op(eng, fn, reads, writes, dma_key=key)

        ident_f = A.buf([128], F32)
        ident_b = A.buf([128], BF16)
        iota_i = A.buf([128], I32)
        fence_v = A.buf([1], F32)
        fence_s = A.buf([1], F32)
        fence_g = A.buf([1], F32)
        bias_fm = A.buf([32], F32)
        bias_u = A.buf([512], F32)
        glub_fm = A.buf([4], F32)
        g1_fm = A.buf([8], F32)
        b1_fm = A.buf([8], F32)
        convw_fm = A.buf([12], F32)
        vbuf = A.buf([4, 514], F32)
        r8tab = A.buf([16], F32)
        C1t = A.buf([16, 2], F32)
        C2t = A.buf([16, 2], F32)
        wcar = A.buf([16, 2], F32)
        eps_t = A.buf([1], F32)
        ones_b = A.buf([128], BF16)

        fence_fns = {
            "vector": lambda e: e.memset(fence_v, 0.0),
            "gpsimd": lambda e: e.memset(fence_g, 0.0),
            "scalar": lambda e: e.activation(out=fence_s, in_=ident_f[:, 0:1], func=AF.Copy),
        }

        mA = A.mark()
        winu = A.buf([8, 512], BF16)
        DMA(lambda e: e.dma_start(out=winu, in_=w_in[:, 0:512].rearrange("(k p) n -> p k n", p=128)),
            [], ["winu"], eng="gpsimd")
        def convert_chunk(c):
            for r in (2 * c, 2 * c + 1):
                rs = slice(r * 128, (r + 1) * 128)
                DMA((lambda rs: lambda e: e.dma_start(out=win_r[rs, :].rearrange("r (c e) -> r c e", e=896),
                                                      in_=w_in[rs, 512:4096].rearrange("r (c e) -> r c e", e=896)))(rs),
                    [], ["win_r%d" % c], key="cv_win%d" % c, eng="gpsimd")
            for r in range(c * 6, min(NFB, (c + 1) * 6)):
                rs = slice(r * 128, (r + 1) * 128)
                DMA((lambda rs: lambda e: e.dma_start(out=wd_s[rs, :], in_=w_down[rs, :]))(rs),
                    [], ["wd_s"], key="cv_wd", eng="gpsimd")

        def retile_chunk(c):
            for k in (2 * c, 2 * c + 1):
                rs = slice(k * 128, (k + 1) * 128)
                for wh in range(3):
                    DMA((lambda rs, k, wh: lambda e: e.dma_start(
                        out=wcv_s[:, :, k, wh * 128:(wh + 1) * 128].rearrange("c p n -> p c n"),
                        in_=win_r[rs, wh * 512:(wh + 1) * 512].rearrange("p (c n) -> p c n", n=128)))(rs, k, wh),
                        ["win_r%d" % c], ["wcv_s"], key="cv2_win")
                for wh in range(2):
                    DMA((lambda rs, k, wh: lambda e: e.dma_start(
                        out=wgt_s[:, :, k, wh * 128:(wh + 1) * 128].rearrange("c p n -> p c n"),
                        in_=win_r[rs, 1536 + wh * 1024: 1536 + (wh + 1) * 1024].rearrange("p (c n) -> p c n", n=128)))(rs, k, wh),
                        ["win_r%d" % c], ["wgt_s"], key="cv2_wgt")

        P.tag = 'P0'
        G(lambda e: e.iota(iota_i, pattern=[[1, 128]], base=0, channel_multiplier=-1), [], ["iota_i"])
        V(lambda e: e.tensor_scalar(out=ident_f, in0=iota_i, scalar1=0.0, scalar2=None, op0=ALU.is_equal), ["iota_i"], ["ident_f"])
        V(lambda e: e.tensor_copy(out=ident_b, in_=ident_f), ["ident_f"], ["ident_b"])
        V(lambda e: e.memset(vbuf, 0.0), [], ["vbuf0", "vbuf1", "vbuf2", "vbuf3"])

        DMA(lambda e: e.dma_start(out=bias_u, in_=b_in[0:512].partition_broadcast(128)), [], ["bias_u"])

        Ec = A.buf([16, 128], F32)
        Es = A.buf([16, 128], F32)
        M0 = A.buf([32, 128], BF16)
        Hpad = A.buf([32, 2, 128], BF16)
        Gpad = A.buf([32, 2, 128], BF16)
        m0 = A.mark()
        nat = A.buf([128], F32)
        DMA(lambda e: e.dma_start(out=nat[0:32, :], in_=b_in.rearrange("(c p) -> c p", p=128)), [], ["nat_bin"])
        nat2 = A.buf([128], F32)
        DMA(lambda e: e.dma_start(out=nat2[0:4, :], in_=glu_b.rearrange("(c p) -> c p", p=128)), [], ["nat_glub"])
        nat3 = A.buf([128], F32)
        DMA(lambda e: e.dma_start(out=nat3[0:12, :], in_=conv_w.rearrange("k (c p) -> (k c) p", p=128)), [], ["nat_convw"])

        def small_T(dst, src_nat, K, rtok, wtok):
            i, ps, pt = nextps()
            T(lambda e: e.transpose(out=ps[:, 0:K], in_=src_nat[0:K, :], identity=ident_f[0:K, 0:K]), [rtok, "ident_f"], [pt])
            V(lambda e: e.tensor_copy(out=dst, in_=ps[:, 0:K]), [pt], [wtok])

        small_T(bias_fm, nat, 32, "nat_bin", "bias_fm")
        nat4 = A.buf([128], F32)
        DMA(lambda e: e.dma_start(out=nat4[0:8, :], in_=ln1_g.rearrange("(c p) -> c p", p=128)), [], ["nat_g1"])
        nat5 = A.buf([128], F32)
        DMA(lambda e: e.dma_start(out=nat5[0:8, :], in_=ln1_b.rearrange("(c p) -> c p", p=128)), [], ["nat_b1"])
        small_T(g1_fm, nat4, 8, "nat_g1", "g1_fm")
        small_T(b1_fm, nat5, 8, "nat_b1", "b1_fm")
        small_T(glub_fm, nat2, 4, "nat_glub", "glub_fm")
        small_T(convw_fm, nat3, 12, "nat_convw", "convw_fm")

        lr = A.buf([16], F32)
        li = A.buf([16], F32)
        ldt = A.buf([16], F32)
        Br = A.buf([16, 16], F32)
        Bi = A.buf([16, 16], F32)
        Cr = A.buf([16, 16], F32)
        Ci = A.buf([16, 16], F32)
        cnat_r = A.buf([4, 64], F32)
        cnat_i = A.buf([4, 64], F32)
        C64r = A.buf([512], F32)
        C64i = A.buf([512], F32)
        dT = A.buf([32], F32)
        for g2 in range(2):
            ps_ = slice(g2 * 64, (g2 + 1) * 64)
            DMA((lambda ps_, g2: lambda e: e.dma_start(out=lr[ps_, :], in_=lam_re.rearrange("(gh g2) p -> g2 p gh", g2=2)[g2],
                                                      allow_slow_non_contiguous=True))(ps_, g2), [], ["lr"], key="ld_lr")
            DMA((lambda ps_, g2: lambda e: e.dma_start(out=li[ps_, :], in_=lam_im.rearrange("(gh g2) p -> g2 p gh", g2=2)[g2],
                                                      allow_slow_non_contiguous=True))(ps_, g2), [], ["li"], key="ld_li")
            DMA((lambda ps_, g2: lambda e: e.dma_start(out=ldt[ps_, :], in_=bass.AP(tensor=log_dt.tensor, offset=g2, ap=[[0, 64], [2, 16]]),
                                                      allow_slow_non_contiguous=True))(ps_, g2), [], ["ldt"], key="ld_ldt")
            DMA((lambda ps_, g2: lambda e: e.dma_start(out=Br[ps_], in_=b_re.rearrange("(gh g2) p c -> g2 p gh c", g2=2)[g2]))(ps_, g2),
                [], ["Br"], key="ld_Br")
            DMA((lambda ps_, g2: lambda e: e.dma_start(out=Bi[ps_], in_=b_im.rearrange("(gh g2) p c -> g2 p gh c", g2=2)[g2]))(ps_, g2),
                [], ["Bi"], key="ld_Bi")
        DMA(lambda e: e.dma_start(out=cnat_r, in_=c_re.rearrange("g c p -> (g c) p").rearrange("(i r) p -> r i p", r=128)), [], ["cnat_r"])
        DMA(lambda e: e.dma_start(out=cnat_i, in_=c_im.rearrange("g c p -> (g c) p").rearrange("(i r) p -> r i p", r=128)), [], ["cnat_i"])
        DMA(lambda e: e.dma_start(out=dT[0:16, :], in_=ssm_d.rearrange("(g c) -> c g", c=16), allow_slow_non_contiguous=True), [], ["dT"])

        for cnat, C64, ctok, C128 in ((cnat_r, C64r, "cnat_r", Cr), (cnat_i, C64i, "cnat_i", Ci)):
            i, ps, pt = nextps()
            for t4 in range(4):
                T((lambda ps, cnat, t4: lambda e: e.transpose(out=ps[0:64, t4 * 128:(t4 + 1) * 128], in_=cnat[:, t4, :], identity=ident_f))(ps, cnat, t4),
                  [ctok, "ident_f"], [pt])
            S((lambda ps, C64: lambda e: e.copy(out=C64[0:64, :], in_=ps[0:64, :]))(ps, C64), [pt], [ctok + "64"])
            for g2 in range(2):
                DMA((lambda C64, C128, g2: lambda e: e.dma_start(
                    out=C128[g2 * 64:(g2 + 1) * 64],
                    in_=C64[0:64, :].rearrange("p (gh g2 c) -> p gh g2 c", g2=2, c=16)[:, :, g2, :]))(C64, C128, g2),
                    [ctok + "64"], [ctok + "128"], key="shuf_" + ctok)

        ev9 = A.buf([9], F32)
        evD = A.buf([8], F32)
        jv = A.buf([128], F32)
        ev_i = A.buf([128], I32)
        G(lambda e: e.iota(ev_i, pattern=[[1, 128]], base=0, channel_multiplier=0), [], ["ev_i"])
        V(lambda e: e.tensor_copy(out=jv, in_=ev_i), ["ev_i"], ["jv"])
        V(lambda e: e.tensor_copy(out=ev9, in_=ev_i[:, 0:9]), ["ev_i"], ["ev9"])
        V(lambda e: e.tensor_scalar(out=evD, in0=jv[:, 0:8], scalar1=-1.0, scalar2=7.0, op0=ALU.mult, op1=ALU.add), ["jv"], ["evD"])

        dt_t = A.buf([16], F32)
        a_t = A.buf([16], F32)
        th_t = A.buf([16], F32)
        S(lambda e: e.activation(out=dt_t, in_=ldt, func=AF.Exp), ["ldt"], ["dt_t"])
        V(lambda e: e.tensor_tensor(out=a_t, in0=lr, in1=dt_t, op=ALU.mult), ["lr", "dt_t"], ["a_t"])
        V(lambda e: e.tensor_tensor(out=th_t, in0=li, in1=dt_t, op=ALU.mult), ["li", "dt_t"], ["th_t"])

        def powtab(ev, n, nm, th_src=th_t, th_tok="th_t", lead=16, with_mag=True):
            ang = A.buf([lead, n], F32)
            kk = A.buf([lead, n], F32)
            rr = A.buf([lead, n], F32)
            rc = A.buf([lead, n], F32)
            sn = A.buf([lead, n], F32)
            cs = A.buf([lead, n], F32)
            thb = bc(th_src[:, :, None] if len(th_src.shape) == 2 else th_src, [128, lead, n])
            evb = bc(ev[:, None, :], [128, lead, n])
            V(lambda e: e.tensor_tensor(out=ang, in0=thb, in1=evb, op=ALU.mult), [th_tok, "ev9", "evD", "jv"], [nm + "ang"])
            V(lambda e: e.tensor_scalar(out=kk, in0=ang, scalar1=1.0 / TWO_PI, scalar2=MAGIC, op0=ALU.mult, op1=ALU.add), [nm + "ang"], [nm + "kk"])
            V(lambda e: e.tensor_scalar(out=kk, in0=kk, scalar1=-MAGIC, scalar2=None, op0=ALU.add), [nm + "kk"], [nm + "kk"])
            V(lambda e: e.scalar_tensor_tensor(out=rr, in0=kk, scalar=-TWO_PI, in1=ang, op0=ALU.mult, op1=ALU.add), [nm + "kk", nm + "ang"], [nm + "rr"])
            V(lambda e: e.tensor_scalar(out=rc, in0=rr, scalar1=math.pi / 2, scalar2=None, op0=ALU.add), [nm + "rr"], [nm + "rc"])
            V(lambda e: e.tensor_scalar(out=kk, in0=rc, scalar1=math.pi, scalar2=-TWO_PI, op0=ALU.is_gt, op1=ALU.mult), [nm + "rc", nm + "rr"], [nm + "kk"])
            V(lambda e: e.tensor_tensor(out=rc, in0=rc, in1=kk, op=ALU.add), [nm + "rc", nm + "kk"], [nm + "rc"])
            S(lambda e: e.activation(out=sn, in_=rr, func=AF.Sin, scale=SIN_SCALE), [nm + "rr"], [nm + "sn"])
            S(lambda e: e.activation(out=cs, in_=rc, func=AF.Sin, scale=SIN_SCALE), [nm + "rc"], [nm + "cs"])
            if not with_mag:
                return cs, sn, rr, None
            ea = A.buf([lead, n], F32)
            mag = A.buf([lead, n], F32)
            pre = A.buf([lead, n], F32)
            pim = A.buf([lead, n], F32)
            ab = bc(a_t[:, :, None], [128, lead, n])
            V(lambda e: e.tensor_tensor(out=ea, in0=ab, in1=evb, op=ALU.mult), ["a_t", "ev9", "evD"], [nm + "ea"])
            S(lambda e: e.activation(out=mag, in_=ea, func=AF.Exp), [nm + "ea"], [nm + "mag"])
            V(lambda e: e.tensor_tensor(out=pre, in0=mag, in1=cs, op=ALU.mult), [nm + "mag", nm + "cs"], [nm + "pre"])
            V(lambda e: e.tensor_tensor(out=pim, in0=mag, in1=sn, op=ALU.mult), [nm + "mag", nm + "sn"], [nm + "pim"])
            return pre, pim, rr, mag

        pA_re, pA_im, rA, magA = powtab(ev9, 9, "pA")
        pD_re, pD_im, _, _ = powtab(evD, 8, "pD")

        fr = A.buf([16], F32)
        fi = A.buf([16], F32)
        t1 = A.buf([16], F32)
        t2 = A.buf([16], F32)
        t3 = A.buf([16], F32)
        den = A.buf([16], F32)
        nre = A.buf([16], F32)
        lbre = pA_re[:, :, 1]
        lbim = pA_im[:, :, 1]
        V(lambda e: e.tensor_scalar(out=nre, in0=lbre, scalar1=-1.0, scalar2=None, op0=ALU.add), ["pApre"], ["nre"])
        V(lambda e: e.tensor_tensor(out=t1, in0=lr, in1=lr, op=ALU.mult), ["lr"], ["t1"])
        V(lambda e: e.tensor_tensor(out=t2, in0=li, in1=li, op=ALU.mult), ["li"], ["t2"])
        V(lambda e: e.tensor_tensor(out=den, in0=t1, in1=t2, op=ALU.add), ["t1", "t2"], ["den"])
        V(lambda e: e.reciprocal(out=den, in_=den), ["den"], ["den"])
        V(lambda e: e.tensor_tensor(out=t1, in0=nre, in1=lr, op=ALU.mult), ["nre", "lr", "den"], ["t1"])
        V(lambda e: e.tensor_tensor(out=t2, in0=lbim, in1=li, op=ALU.mult), ["pApim", "li", "den"], ["t2"])
        V(lambda e: e.tensor_tensor(out=t3, in0=t1, in1=t2, op=ALU.add), ["t1", "t2"], ["t3"])
        V(lambda e: e.tensor_tensor(out=fr, in0=t3, in1=den, op=ALU.mult), ["t3", "den"], ["fr"])
        V(lambda e: e.tensor_tensor(out=t1, in0=lbim, in1=lr, op=ALU.mult), ["pApim", "lr", "t3"], ["t1"])
        V(lambda e: e.tensor_tensor(out=t2, in0=nre, in1=li, op=ALU.mult), ["nre", "li", "t3"], ["t2"])
        V(lambda e: e.tensor_tensor(out=t3, in0=t1, in1=t2, op=ALU.subtract), ["t1", "t2", "fr"], ["t3"])
        V(lambda e: e.tensor_tensor(out=fi, in0=t3, in1=den, op=ALU.mult), ["t3", "den"], ["fi"])

        bbr = A.buf([16, 16], F32)
        bbi = A.buf([16, 16], F32)
        u1 = A.buf([16, 16], F32)
        u2 = A.buf([16, 16], F32)
        frb = bc(fr[:, :, None], [128, 16, 16])
        fib = bc(fi[:, :, None], [128, 16, 16])
        V(lambda e: e.tensor_tensor(out=u1, in0=frb, in1=Br, op=ALU.mult), ["fr", "Br"], ["u1"])
        V(lambda e: e.tensor_tensor(out=u2, in0=fib, in1=Bi, op=ALU.mult), ["fi", "Bi"], ["u2"])
        V(lambda e: e.tensor_tensor(out=bbr, in0=u1, in1=u2, op=ALU.subtract), ["u1", "u2"], ["bbr"])
        V(lambda e: e.tensor_tensor(out=u1, in0=frb, in1=Bi, op=ALU.mult), ["fr", "Bi", "bbr"], ["u1"])
        V(lambda e: e.tensor_tensor(out=u2, in0=fib, in1=Br, op=ALU.mult), ["fi", "Br", "bbr"], ["u2"])
        V(lambda e: e.tensor_tensor(out=bbi, in0=u1, in1=u2, op=ALU.add), ["u1", "u2"], ["bbi"])

        HTD_re = A.buf([16, 8, 16], F32)
        HTD_im = A.buf([16, 8, 16], F32)
        GA_re = A.buf([16, 9, 16], F32)
        GA_nim = A.buf([16, 9, 16], F32)
        w1 = A.buf([16, 9, 16], F32)
        w2 = A.buf([16, 9, 16], F32)

        def cplx_tab(eng_a, eng_b, pr, pi_, ptok, xr, xi, xtok, n, o_re, o_second, second_mode, otok):
            prb = bc(pr[:, :, :, None], [128, 16, n, 16])
            pib = bc(pi_[:, :, :, None], [128, 16, n, 16])
            xrb = bc(xr[:, :, None, :], [128, 16, n, 16])
            xib = bc(xi[:, :, None, :], [128, 16, n, 16])
            a1 = w1[:, :, 0:n, :]
            a2 = w2[:, :, 0:n, :]
            a3 = a1
            a4 = a2
            P.op(eng_a, lambda e: e.tensor_tensor(out=a1, in0=prb, in1=xrb, op=ALU.mult), [ptok + "pre", xtok[0]], ["w1"])
            P.op(eng_a, lambda e: e.tensor_tensor(out=a2, in0=pib, in1=xib, op=ALU.mult), [ptok + "pim", xtok[1]], ["w2"])
            P.op(eng_a, lambda e: e.tensor_tensor(out=o_re, in0=a1, in1=a2, op=ALU.subtract), ["w1", "w2"], [otok + "_re"])
            P.op(eng_b, lambda e: e.tensor_tensor(out=a3, in0=prb, in1=xib, op=ALU.mult), [ptok + "pre", xtok[1]], ["w1"])
            P.op(eng_b, lambda e: e.tensor_tensor(out=a4, in0=pib, in1=xrb, op=ALU.mult), [ptok + "pim", xtok[0]], ["w2"])
            if second_mode > 0:
                P.op(eng_b, lambda e: e.tensor_tensor(out=o_second, in0=a3, in1=a4, op=ALU.add), ["w1", "w2"], [otok + "_2"])
            else:
                P.op(eng_b, lambda e: e.tensor_scalar(out=a3, in0=a3, scalar1=-1.0, scalar2=None, op0=ALU.mult), ["w1"], ["w1"])
                P.op(eng_b, lambda e: e.tensor_tensor(out=o_second, in0=a3, in1=a4, op=ALU.subtract), ["w1", "w2"], [otok + "_2"])

        cplx_tab("vector", "gpsimd", pD_re, pD_im, "pD", bbr, bbi, ("bbr", "bbi"), 8, HTD_re, HTD_im, +1, "HTD")
        cplx_tab("vector", "gpsimd", pA_re, pA_im, "pA", Cr, Ci, ("cnat_r128", "cnat_i128"), 9, GA_re, GA_nim, -1, "GA")

        V(lambda e: e.tensor_copy(out=r8tab, in_=magA[:, :, 8]), ["pAmag"], ["r8tab"])
        psi = rA[:, :, 8]
        ang128 = A.buf([16], F32)
        k128 = A.buf([16], F32)
        r128 = A.buf([16], F32)
        rc128 = A.buf([16], F32)
        sn128 = A.buf([16], F32)
        cs128 = A.buf([16], F32)
        V(lambda e: e.tensor_scalar(out=ang128, in0=psi, scalar1=128.0, scalar2=None, op0=ALU.mult), ["pArr"], ["ang128"])
        V(lambda e: e.tensor_scalar(out=k128, in0=ang128, scalar1=1.0 / TWO_PI, scalar2=MAGIC, op0=ALU.mult, op1=ALU.add), ["ang128"], ["k128"])
        V(lambda e: e.tensor_scalar(out=k128, in0=k128, scalar1=-MAGIC, scalar2=None, op0=ALU.add), ["k128"], ["k128"])
        V(lambda e: e.scalar_tensor_tensor(out=r128, in0=k128, scalar=-TWO_PI, in1=ang128, op0=ALU.mult, op1=ALU.add), ["k128", "ang128"], ["r128"])
        V(lambda e: e.tensor_scalar(out=rc128, in0=r128, scalar1=math.pi / 2, scalar2=None, op0=ALU.add), ["r128"], ["rc128"])
        V(lambda e: e.tensor_scalar(out=k128, in0=rc128, scalar1=math.pi, scalar2=-TWO_PI, op0=ALU.is_gt, op1=ALU.mult), ["rc128", "r128"], ["k128"])
        V(lambda e: e.tensor_tensor(out=rc128, in0=rc128, in1=k128, op=ALU.add), ["rc128", "k128"], ["rc128"])
        S(lambda e: e.activation(out=sn128, in_=r128, func=AF.Sin, scale=SIN_SCALE), ["r128"], ["sn128"])
        S(lambda e: e.activation(out=cs128, in_=rc128, func=AF.Sin, scale=SIN_SCALE), ["rc128"], ["cs128"])
        V(lambda e: e.tensor_copy(out=C1t, in_=bc(cs128[:, :, None], [128, 16, 2])), ["cs128"], ["C1t"])
        V(lambda e: e.tensor_scalar(out=C2t[:, :, 0], in0=sn128, scalar1=-1.0, scalar2=None, op0=ALU.mult), ["sn128"], ["C2t"])
        V(lambda e: e.tensor_copy(out=C2t[:, :, 1], in_=sn128), ["sn128", "C2t"], ["C2t"])
        V(lambda e: e.memset(wcar, 0.0), [], ["wcar"])

        angj = w1.rearrange('p a b c -> p (a b c)')[:, 0:2048].rearrange('p (a b) -> p a b', a=16)
        kj = w2.rearrange('p a b c -> p (a b c)')[:, 0:2048].rearrange('p (a b) -> p a b', a=16)
        rj = angj
        rcj = kj
        psib = bc(psi[:, :, None], [128, 16, 128])
        jvb = bc(jv[:, None, :], [128, 16, 128])
        V(lambda e: e.tensor_tensor(out=angj, in0=psib, in1=jvb, op=ALU.mult), ["pArr", "jv", "GA_re", "GA_2", "HTD_re", "HTD_2"], ["w1"])
        V(lambda e: e.tensor_scalar(out=kj, in0=angj, scalar1=1.0 / TWO_PI, scalar2=MAGIC, op0=ALU.mult, op1=ALU.add), ["w1", "GA_re", "GA_2", "HTD_re", "HTD_2"], ["w2"])
        V(lambda e: e.tensor_scalar(out=kj, in0=kj, scalar1=-MAGIC, scalar2=None, op0=ALU.add), ["w2"], ["w2"])
        V(lambda e: e.scalar_tensor_tensor(out=rj, in0=kj, scalar=-TWO_PI, in1=angj, op0=ALU.mult, op1=ALU.add), ["w2", "w1"], ["w1"])
        V(lambda e: e.tensor_scalar(out=Ec, in0=rj, scalar1=math.pi / 2, scalar2=None, op0=ALU.add), ["w1"], ["Ec"])
        V(lambda e: e.tensor_scalar(out=kj, in0=Ec, scalar1=math.pi, scalar2=-TWO_PI, op0=ALU.is_gt, op1=ALU.mult), ["Ec", "w1"], ["w2"])
        V(lambda e: e.tensor_tensor(out=Ec, in0=Ec, in1=kj, op=ALU.add), ["Ec", "w2"], ["Ec"])
        S(lambda e: e.activation(out=Es, in_=rj, func=AF.Sin, scale=SIN_SCALE), ["w1"], ["Es"])
        S(lambda e: e.activation(out=Ec, in_=Ec, func=AF.Sin, scale=SIN_SCALE), ["Ec"], ["Ec"])

        V(lambda e: e.memset(Hpad, 0.0), [], ["Hpad"])
        V(lambda e: e.memset(Gpad, 0.0), [], ["Gpad"])
        V(lambda e: e.memset(M0, 0.0), [], ["M0"])
        cnt = 0
        for reim, (HT, htok) in enumerate(((HTD_re, "HTD_re"), (HTD_im, "HTD_2"))):
            for ghq in range(4):
                i, ps, pt = nextps()
                for l in range(4):
                    gh = ghq * 4 + l
                    T((lambda ps, HT, gh, l: lambda e: e.transpose(out=ps[:, l * 128:(l + 1) * 128],
                                                                   in_=HT[:, gh].rearrange("p s c -> p (s c)"), identity=ident_f))(ps, HT, gh, l),
                      [htok, "ident_f"], [pt])
                for l in range(4):
                    gh = ghq * 4 + l
                    for g2 in range(2):
                        fn = (lambda ps, gh, g2, l, reim: lambda e: e.tensor_copy(
                            out=Hpad[:, 2 * gh + g2, reim, g2 * 64:(g2 + 1) * 64],
                            in_=ps[:, l * 128 + g2 * 64: l * 128 + (g2 + 1) * 64]))(ps, gh, g2, l, reim)
                        if cnt % 2 == 0:
                            V(fn, [pt, "Hpad"], ["Hpad"])
                        else:
                            S((lambda ps, gh, g2, l, reim: lambda e: e.copy(
                                out=Hpad[:, 2 * gh + g2, reim, g2 * 64:(g2 + 1) * 64],
                                in_=ps[:, l * 128 + g2 * 64: l * 128 + (g2 + 1) * 64]))(ps, gh, g2, l, reim), [pt, "Hpad"], ["Hpad"])
                        cnt += 1
        for reim, (GT, gtok) in enumerate(((GA_re, "GA_re"), (GA_nim, "GA_2"))):
            for g2 in range(2):
                pp = slice(g2 * 64, (g2 + 1) * 64)
                V((lambda GT, pp, g2, reim: lambda e: e.tensor_copy(
                    out=Gpad[pp].rearrange("p (gh g2) r n -> p gh g2 r n", g2=2)[:, :, g2, reim, :],
                    in_=GT[pp, :, 1:9, :].rearrange("p g t c -> p g (t c)")))(GT, pp, g2, reim), [gtok, "Gpad"], ["Gpad"])
        BBr = A.buf([32, 16], F32)
        BBi = A.buf([32, 16], F32)
        Kcb = A.buf([32, 128], BF16)
        dd = A.buf([32, 16], F32)
        V(lambda e: e.memset(BBr, 0.0), [], ["BBr"])
        V(lambda e: e.memset(BBi, 0.0), [], ["BBi"])
        for BB, src, stok, btok in ((BBr, bbr, "bbr", "BBr"), (BBi, bbi, "bbi", "BBi")):
            for g2 in range(2):
                pp = slice(g2 * 64, (g2 + 1) * 64)
                V((lambda BB, src, pp, g2: lambda e: e.tensor_copy(
                    out=BB[pp].rearrange("p (gh g2) c -> p gh g2 c", g2=2)[:, :, g2, :], in_=src[pp]))(BB, src, pp, g2),
                  [stok, btok], [btok])
        V(lambda e: e.tensor_tensor(out=dd[0:16], in0=bc(ident_f[0:16, None, 0:16], [16, 32, 16]), in1=bc(dT[0:16, :, None], [16, 32, 16]), op=ALU.mult),
          ["ident_f", "dT"], ["dd"])
        for gq in range(8):
            i, ps, pt = nextps()
            for l in range(4):
                g = gq * 4 + l
                gh = g // 2
                T((lambda ps, g, gh, l: lambda e: e.matmul(ps[0:16, l * 128:(l + 1) * 128], lhsT=BBr[:, g, :],
                                                          rhs=GA_re[:, gh, 0:8, :].rearrange("p t c -> p (t c)"), start=True, stop=False))(ps, g, gh, l),
                  ["BBr", "GA_re"], [pt])
                T((lambda ps, g, gh, l: lambda e: e.matmul(ps[0:16, l * 128:(l + 1) * 128], lhsT=BBi[:, g, :],
                                                          rhs=GA_nim[:, gh, 0:8, :].rearrange("p t c -> p (t c)"), start=False, stop=True))(ps, g, gh, l),
                  ["BBi", "GA_2"], [pt])
            V((lambda ps, gq: lambda e: e.tensor_copy(out=Kcb[0:16, gq * 4:(gq + 1) * 4, :].rearrange("p g n -> p (g n)"), in_=ps[0:16, :]))(ps, gq),
              [pt], ["Kcb"])
            V((lambda ps, gq: lambda e: e.tensor_tensor(out=Kcb[0:16, gq * 4:(gq + 1) * 4, 0:16],
                                                        in0=ps[0:16, :].rearrange("p (g n) -> p g n", g=4)[:, :, 0:16],
                                                        in1=dd[0:16, gq * 4:(gq + 1) * 4, :], op=ALU.add))(ps, gq), [pt, "dd", "Kcb"], ["Kcb"])
        for s8 in range(8):
            DMA((lambda s8: lambda e: e.dma_start(out=M0[16 * s8:16 * s8 + 16, :, 16 * s8:128], in_=Kcb[0:16, :, 0:128 - 16 * s8]))(s8),
                ["Kcb", "M0"], ["M0"], key="m0asm")

        P.fence(fence_fns)
        A.release(m0)
        xa = A.buf([2, D], F32)
        xb = A.buf([8, D], BF16)
        xT = A.buf([8, 1024], BF16)
        u_oct = A.buf([32, 8, 16], BF16)
        U8s = [A.buf([32, 128], BF16) for _ in range(2)]
        wins = [A.buf([4, 2, 128], F32) for _ in range(2)]
        Wts = [A.buf([4, 2, 128], F32) for _ in range(2)]
        Xb = A.buf([16, 2, 129], BF16)
        g_oct = A.buf([8, 512], BF16)
        g_fms = [A.buf([1024], BF16) for _ in range(2)]
        rts = [[A.buf([4, 128], F32) for _ in range(4)] for _ in range(2)]
        ctas = [A.buf([4, 2], F32) for _ in range(2)]
        ctbs = [A.buf([4, 2], F32) for _ in range(2)]
        brow_f = A.buf([512], F32)
        brow_b = A.buf([512], BF16)
        V(lambda e: e.memset(Xb, 0.0), [], ["Xb0", "Xb1", "Xb2", "Xb3"])
        V(lambda e: e.memset(ones_b, 0.0), [], ["ones_b"])
        V(lambda e: e.memset(ones_b[0:1, :], 1.0), ["ones_b"], ["ones_b"])
        V(lambda e: e.memset(brow_b, 0.0), [], ["brow_b"])
        DMA(lambda e: e.dma_start(out=brow_f[0:1, :], in_=b_in[0:512].partition_broadcast(1)), [], ["brow_f"])
        V(lambda e: e.tensor_copy(out=brow_b[0:1, :], in_=brow_f[0:1, :]), ["brow_f", "brow_b"], ["brow_b"])
        print("ARENA top (phase A)", A.top, "of", A.n)
        wt_ctr = [0]
        gfm_ctr = [0]

        def load_xb(stile):
            for hh in range(4):
                r0 = stile * 1024 + hh * 256
                DMA((lambda r0: lambda e: e.dma_start(out=xa, in_=x[r0:r0 + 256, :].rearrange("(s p) d -> p s d", p=128)))(r0), [], ["xa"], key="xald")
                for sl in range(2):
                    sub = hh * 2 + sl
                    S((lambda sl, sub: lambda e: e.copy(out=xb[:, sub, :], in_=xa[:, sl, :]))(sl, sub), ["xa"], ["xb%d" % sub])

        def phaseA_front(stile):
            P.tag = 'A_fe%d' % stile
            t0 = stile * 1024
            U8t = U8s[stile % 2]
            ut = "U8_%d_" % (stile % 2)
            if stile == 0:
                load_xb(0)
            for hh in range(2):
                for k in range(8):
                    i, ps, pt = nextps()
                    pb = psbf(i)
                    for sl in range(4):
                        sub = hh * 4 + sl
                        T((lambda pb, sub, sl, k: lambda e: e.transpose(out=pb[:, sl * 128:(sl + 1) * 128], in_=xb[:, sub, k * 128:(k + 1) * 128],
                                                                        identity=ident_b))(pb, sub, sl, k), ["xb%d" % sub, "ident_b"], [pt])
                    S((lambda pb, k, hh: lambda e: e.copy(out=xT[:, k, hh * 512:(hh + 1) * 512], in_=pb[:, 0:512]))(pb, k, hh), [pt], ["xT%d" % k])
            if stile + 1 < 4:
                load_xb(stile + 1)
            if stile == 0:
                convert_chunk(0)
                convert_chunk(1)
            elif stile == 1:
                convert_chunk(2)
                convert_chunk(3)
            elif stile == 2:
                retile_chunk(0)
                retile_chunk(1)
                retile_chunk(2)
            else:
                retile_chunk(3)
            for s8 in range(8):
                i, ps, pt = nextps()
                for k in range(8):
                    T((lambda ps, s8, k: lambda e: e.matmul(ps, lhsT=xT[:, k, :].rearrange("p (j s) -> p s j", s=8)[:, s8, :],
                                                           rhs=winu[:, k, :], start=(k == 0), stop=False))(ps, s8, k),
                      ["xT%d" % k, "winu"], [pt])
                T((lambda ps: lambda e: e.matmul(ps, lhsT=ones_b, rhs=brow_b, start=False, stop=True))(ps), ["ones_b", "brow_b"], [pt])
                S((lambda ps, s8: lambda e: e.copy(out=u_oct[:, :, s8, :], in_=ps.rearrange("p (g c) -> p g c", c=16)))(ps, s8),
                  [pt], ["u_oct%d" % s8])
            for gq in range(4):
                i, ps, pt = nextps()
                pb = psbf(i)
                for l in range(8):
                    g = gq * 8 + l
                    T((lambda pb, g, l: lambda e: e.transpose(out=pb[:, l * 128:(l + 1) * 128], in_=u_oct[:, g].rearrange("p s c -> p (s c)"),
                                                              identity=ident_b))(pb, g, l), ["u_oct%d" % s for s in range(8)] + ["ident_b"], [pt])
                S((lambda pb, gq, U8t: lambda e: e.copy(out=U8t[:, gq * 8:(gq + 1) * 8, :].rearrange("p g j -> p (g j)"), in_=pb))(pb, gq, U8t),
                  [pt], [ut + "%d" % gq])

        def phaseA_scan(stile):
            P.tag = 'A_V%d' % stile
            U8t = U8s[stile % 2]
            ut = "U8_%d_" % (stile % 2)
            for pair in range(2):
                qs = (2 * pair, 2 * pair + 1)
                ctx = []
                for si, q in enumerate(qs):
                    ir, psr, ptr = nextps()
                    ii, psi_, pti = nextps()
                    for l in range(4):
                        gh = q * 4 + l
                        for reim, ps in ((0, psr), (1, psi_)):
                            for g2 in range(2):
                                g = 2 * gh + g2
                                T((lambda ps, g, reim, l, g2: lambda e: e.matmul(ps[:, l * 128:(l + 1) * 128], lhsT=Hpad[:, g, reim, :], rhs=U8t[:, g, :],
                                                                                 start=(g2 == 0), stop=(g2 == 1)))(ps, g, reim, l, g2),
                                  ["Hpad", ut + "%d" % (g // 8)], [ptr if reim == 0 else pti])
                    gsl = slice(q * 4, (q + 1) * 4)
                    ctx.append(dict(q=q, si=si, gsl=gsl, ptr=ptr, pti=pti,
                                    vr=psr.rearrange("p (g j) -> p g j", g=4), vi=psi_.rearrange("p (g j) -> p g j", g=4),
                                    ec=Ec[:, gsl, :], es=Es[:, gsl, :], win=wins[si], Wt=Wts[si], r=rts[si],
                                    wint="win%d" % si, wtok="Wt%d" % si, rt=["rt%d_%d" % (si, i_) for i_ in range(4)],
                                    cta=ctas[si], ctb=ctbs[si], ctt=["cta%d" % si, "ctb%d" % si], xtk="Xb%d" % q))
                for c in ctx:
                    V((lambda c: lambda e: e.tensor_tensor(out=c["r"][0], in0=c["vr"], in1=c["ec"], op=ALU.mult))(c), [c["ptr"], "Ec"], [c["rt"][0]])
                for c in ctx:
                    V((lambda c: lambda e: e.tensor_tensor(out=c["r"][1], in0=c["vi"], in1=c["es"], op=ALU.mult))(c), [c["pti"], "Es"], [c["rt"][1]])
                for c in ctx:
                    V((lambda c: lambda e: e.tensor_tensor(out=c["r"][2], in0=c["vi"], in1=c["ec"], op=ALU.mult))(c), [c["pti"], "Ec"], [c["rt"][2]])
                for c in ctx:
                    V((lambda c: lambda e: e.tensor_tensor(out=c["r"][3], in0=c["vr"], in1=c["es"], op=ALU.mult))(c), [c["ptr"], "Es"], [c["rt"][3]])
                for c in ctx:
                    V((lambda c: lambda e: e.tensor_tensor(out=c["win"][:, :, 0, :], in0=c["r"][0], in1=c["r"][1], op=ALU.add))(c),
                      [c["rt"][0], c["rt"][1]], [c["wint"]])
                for c in ctx:
                    V((lambda c: lambda e: e.tensor_tensor(out=c["win"][:, :, 1, :], in0=c["r"][2], in1=c["r"][3], op=ALU.subtract))(c),
                      [c["rt"][2], c["rt"][3], c["wint"]], [c["wint"]])
                for l in range(4):
                    for reim in range(2):
                        for c in ctx:
                            gh = c["q"] * 4 + l
                            V((lambda c, gh, l, reim: lambda e: e.tensor_tensor_scan(
                                out=c["Wt"][:, l, reim, :], data0=bc(r8tab[:, gh:gh + 1], [128, 128]), data1=c["win"][:, l, reim, :],
                                initial=wcar[:, gh, reim:reim + 1], op0=ALU.mult, op1=ALU.add))(c, gh, l, reim),
                              [c["wint"], "r8tab", "wcar%d" % c["q"]], [c["wtok"]])
                for c in ctx:
                    Wt = c["Wt"]
                    wl = Wt[:, :, :, 127]
                    V((lambda c, wl: lambda e: e.tensor_tensor(out=c["cta"], in0=C1t[:, c["gsl"], :], in1=wl, op=ALU.mult))(c, wl), ["C1t", c["wtok"]], [c["ctt"][0]])
                for c in ctx:
                    Wt = c["Wt"]
                    wl_sw = bass.AP(tensor=Wt.tensor, offset=Wt[:, :, 1, 127].offset, ap=[list(Wt.ap[0]), list(Wt.ap[1]), [-Wt.ap[2][0], 2]])
                    V((lambda c, wl_sw: lambda e: e.tensor_tensor(out=c["ctb"], in0=C2t[:, c["gsl"], :], in1=wl_sw, op=ALU.mult))(c, wl_sw),
                      ["C2t", c["wtok"]], [c["ctt"][1]])
                for c in ctx:
                    V((lambda c: lambda e: e.tensor_tensor(out=wcar[:, c["gsl"], :], in0=c["cta"], in1=c["ctb"], op=ALU.add))(c), c["ctt"], ["wcar%d" % c["q"]])
                for c in ctx:
                    V((lambda c: lambda e: e.tensor_copy(out=Xb[:, c["gsl"], :, 0], in_=Xb[:, c["gsl"], :, 128]))(c), [c["xtk"]], [c["xtk"]])
                for c in ctx:
                    V((lambda c: lambda e: e.tensor_tensor(out=c["r"][0], in0=c["Wt"][:, :, 0, :], in1=c["ec"], op=ALU.mult))(c), [c["wtok"], "Ec"], [c["rt"][0]])
                for c in ctx:
                    V((lambda c: lambda e: e.tensor_tensor(out=c["r"][1], in0=c["Wt"][:, :, 1, :], in1=c["es"], op=ALU.mult))(c), [c["wtok"], "Es"], [c["rt"][1]])
                for c in ctx:
                    V((lambda c: lambda e: e.tensor_tensor(out=c["r"][2], in0=c["Wt"][:, :, 0, :], in1=c["es"], op=ALU.mult))(c), [c["wtok"], "Es"], [c["rt"][2]])
                for c in ctx:
                    V((lambda c: lambda e: e.tensor_tensor(out=c["r"][3], in0=c["Wt"][:, :, 1, :], in1=c["ec"], op=ALU.mult))(c), [c["wtok"], "Ec"], [c["rt"][3]])
                for c in ctx:
                    V((lambda c: lambda e: e.tensor_tensor(out=Xb[:, c["gsl"], 0, 1:129], in0=c["r"][0], in1=c["r"][1], op=ALU.subtract))(c),
                      [c["rt"][0], c["rt"][1], c["xtk"]], [c["xtk"]])
                for c in ctx:
                    V((lambda c: lambda e: e.tensor_tensor(out=Xb[:, c["gsl"], 1, 1:129], in0=c["r"][2], in1=c["r"][3], op=ALU.add))(c),
                      [c["rt"][2], c["rt"][3], c["xtk"]], [c["xtk"]])

        def phaseA_out(stile):
            P.tag = 'A_Y%d' % stile
            t0 = stile * 1024
            U8t = U8s[stile % 2]
            ut = "U8_%d_" % (stile % 2)
            for gq in range(8):
                i, ps, pt = nextps()
                for l in range(4):
                    g = gq * 4 + l
                    gh = g // 2
                    o_ = ps[:, l * 128:(l + 1) * 128]
                    xtk = "Xb%d" % (gh // 4)
                    T((lambda o_, g, U8t: lambda e: e.matmul(o_, lhsT=U8t[:, g, :], rhs=M0[:, g, :], start=True, stop=False))(o_, g, U8t),
                      [ut + "%d" % (g // 8), "M0"], [pt])
                    T((lambda o_, g, gh: lambda e: e.matmul(o_, lhsT=Xb[:, gh, 0, 0:128], rhs=Gpad[:, g, 0, :], start=False, stop=False))(o_, g, gh),
                      [xtk, "Gpad"], [pt])
                    T((lambda o_, g, gh: lambda e: e.matmul(o_, lhsT=Xb[:, gh, 1, 0:128], rhs=Gpad[:, g, 1, :], start=False, stop=True))(o_, g, gh),
                      [xtk, "Gpad"], [pt])
                S((lambda ps, gq: lambda e: e.activation(
                    out=g_oct[:, :, gq * 64:(gq + 1) * 64].rearrange("p t (g c) -> p t g c", g=4),
                    in_=ps.rearrange("p (g t c) -> p t g c", g=4, t=8), func=AF.Gelu_apprx_tanh))(ps, gq),
                  [pt], ["g_oct"])
            P.tag = 'A_gT%d' % stile
            for cb in range(4):
                i, ps, pt = nextps()
                pb = psbf(i)
                for t8 in range(8):
                    T((lambda pb, t8, cb: lambda e: e.transpose(out=pb[:, t8 * 128:(t8 + 1) * 128], in_=g_oct[:, t8, cb * 128:(cb + 1) * 128],
                                                                identity=ident_b))(pb, t8, cb), ["g_oct", "ident_b"], [pt])
                gs_ = gfm_ctr[0] % 2
                gfm_ctr[0] += 1
                gfm = g_fms[gs_]
                S((lambda pb, gfm: lambda e: e.copy(out=gfm.rearrange("p (j t) -> p t j", t=8),
                                                    in_=pb.rearrange("p (t j) -> p t j", t=8)))(pb, gfm), [pt], ["g_fm%d" % gs_])
                DMA((lambda cb, t0, gfm: lambda e: e.dma_start(out=g_s[cb * 128:(cb + 1) * 128, t0:t0 + 1024], in_=gfm))(cb, t0, gfm),
                    ["g_fm%d" % gs_], ["g_s%d" % stile], key="gst%d" % gs_, eng="gpsimd")

        phaseA_front(0)
        for stile in range(4):
            phaseA_scan(stile)
            if stile + 1 < 4:
                phaseA_front(stile + 1)
            phaseA_out(stile)

        P.fence(fence_fns)
        A.release(mA)
        g1bc = A.buf([D], F32)
        b1bc = None
        g2bc = A.buf([D], F32)
        b2bc = A.buf([D], F32)
        glu_sb = A.buf([4, 512], BF16)
        wso_sb = A.buf([4, D], BF16)
        wco_sb = A.buf([4, D], BF16)
        wo_sb = A.buf([8, D], BF16)
        for dst, src, tok in ((g1bc, ln1_g, "g1bc"), (g2bc, ln2_g, "g2bc"), (b2bc, ln2_b, "b2bc")):
            DMA((lambda dst, src: lambda e: e.dma_start(out=dst, in_=src.partition_broadcast(128)))(dst, src), [], [tok])
        V(lambda e: e.tensor_scalar(out=g1bc, in0=g1bc, scalar1=ALPHA, scalar2=None, op0=ALU.mult), ["g1bc"], ["g1bc"])
        brow2 = A.buf([D], BF16)
        NT = 8
        g_tb = A.buf([4, 512], BF16)
        xb2 = A.buf([4, D], BF16)
        xT2 = A.buf([8, 512], BF16)
        xbx = A.buf([4, D], BF16)
        xTx = A.buf([8, 512], BF16)
        NWS = 3
        wgrp = [A.buf([8, 384], BF16) for _ in range(NWS)]
        NTMP = 8
        tmps = [A.buf([512], F32) for _ in range(NTMP)]
        bz = A.buf([4, 512], BF16)
        merged = A.buf([8, 512], BF16)
        x1 = A.buf([4, D], F32)
        stats = A.buf([4, 2, 6], F32)
        mv = A.buf([4, 2], F32)
        rstd4 = A.buf([4], F32)
        nmr4 = A.buf([4], F32)
        NFS = 3
        ffw = [A.buf([2, 8, 128], BF16) for _ in range(NFS)]
        NDS = 3
        wdb = [A.buf([D], BF16) for _ in range(NDS)]
        hid = A.buf([NFB, 512], BF16)
        brow2f = hid[:, 0:4, :].rearrange("p a b -> p (a b)").bitcast(F32)
        V(lambda e: e.memset(brow2, 0.0), [], ["brow2"])
        DMA(lambda e: e.dma_start(out=brow2f[0:1, :], in_=ln1_b.partition_broadcast(1)), [], ["hid0", "hid1", "hid2", "hid3"])
        V(lambda e: e.tensor_scalar(out=brow2[0:1, :], in0=brow2f[0:1, :], scalar1=ALPHA, scalar2=None, op0=ALU.mult),
          ["hid0", "hid1", "hid2", "hid3", "brow2"], ["brow2"])
        V(lambda e: e.memset(eps_t, LN_EPS), [], ["eps_t"])
        print("ARENA top (phase B)", A.top, "of", A.n)

        tmp_ctr = [0]

        def tmp():
            i = tmp_ctr[0] % NTMP
            tmp_ctr[0] += 1
            return tmps[i], "tmp%d" % i

        wg_ctr = [0]

        def load_wgrp(kind, idx):
            slot = wg_ctr[0] % NWS
            wg_ctr[0] += 1
            if kind == "cv":
                DMA((lambda slot, idx: lambda e: e.dma_start(out=wgrp[slot], in_=wcv_s[idx]))(slot, idx),
                    ["wcv_s"], ["wgrp%d" % slot], key="wgrp%d" % slot)
            else:
                DMA((lambda slot, idx: lambda e: e.dma_start(out=wgrp[slot][:, :, 0:256], in_=wgt_s[idx]))(slot, idx),
                    ["wgt_s"], ["wgrp%d" % slot], key="wgrp%d" % slot)
            return slot

        def ln_stage(gbc, bbc, gtok, btok):
            xt = ["x1_%d" % s_ for s_ in range(4)]
            for sub in range(4):
                for half in range(2):
                    V((lambda sub, half: lambda e: e.bn_stats(out=stats[:, sub, half, :], in_=x1[:, sub, half * 512:(half + 1) * 512]))(sub, half),
                      [xt[sub]], ["stats%d" % sub])
                V((lambda sub: lambda e: e.bn_aggr(out=mv[:, sub, :], in_=stats[:, sub].rearrange("p a b -> p (a b)")))(sub), ["stats%d" % sub], ["mv"])
            def part_b():
                S(lambda e: e.activation(out=rstd4, in_=mv[:, :, 1], func=AF.Sqrt, bias=eps_t, scale=1.0), ["mv", "eps_t"], ["rstd4"])
                V(lambda e: e.reciprocal(out=rstd4, in_=rstd4), ["rstd4"], ["rstd4"])
                for sub in range(4):
                    V((lambda sub: lambda e: e.tensor_scalar(out=x1[:, sub, :], in0=x1[:, sub, :], scalar1=mv[:, sub, 0:1], scalar2=rstd4[:, sub:sub + 1],
                                                             op0=ALU.subtract, op1=ALU.mult))(sub), [xt[sub], "mv", "rstd4"], [xt[sub]])
            deferred = []
            for sub in range(4):
                deferred.append((lambda sub: lambda: V((lambda sub: lambda e: e.tensor_tensor(out=x1[:, sub, :], in0=x1[:, sub, :], in1=gbc, op=ALU.mult))(sub),
                                                       [xt[sub], gtok], [xt[sub]]))(sub))
                deferred.append((lambda sub: lambda: V((lambda sub: lambda e: e.tensor_tensor(out=x1[:, sub, :], in0=x1[:, sub, :], in1=bbc, op=ALU.add))(sub),
                                                       [xt[sub], btok], [xt[sub]]))(sub))
            return [part_b] + deferred

        def load_x_bf16(t):
            for sub in range(4):
                r0 = t * 512 + sub * 128
                DMA((lambda sub, r0: lambda e: e.dma_start(out=xbx[:, sub, :], in_=x[r0:r0 + 128, :]))(sub, r0),
                    [], ["xbx_%d" % sub], key="xld%d" % sub, eng="gpsimd")

        def emit_xT():
            for k in range(8):
                i, ps, pt = nextps()
                pb = psbf(i)
                for sub in range(4):
                    T((lambda pb, sub, k: lambda e: e.transpose(out=pb[:, sub * 128:(sub + 1) * 128], in_=xbx[:, sub, k * 128:(k + 1) * 128],
                                                                identity=ident_b))(pb, sub, k), ["xbx_%d" % sub, "ident_b"], [pt])
                S((lambda pb, k: lambda e: e.copy(out=xTx[:, k, :], in_=pb[:, 0:512]))(pb, k), [pt], ["xTx_%d" % k])

        wq = [("cv", 0, c) for c in range(4)]
        for t_ in range(NT):
            wq += [("gt", t_, d) for d in range(8)]
            if t_ + 1 < NT:
                wq += [("cv", t_ + 1, c) for c in range(4)]
        gslot = {}

        def pump(n):
            for _ in range(n):
                if wq:
                    kd = wq.pop(0)
                    gslot[kd] = load_wgrp(kd[0], kd[2])

        def proj_block(kd, cbl):
            slot_ = gslot[kd]
            i, ps, pt = nextps()
            for k in range(8):
                T((lambda ps, slot_, cbl, k: lambda e: e.matmul(ps, lhsT=wgrp[slot_][:, k, cbl * 128:(cbl + 1) * 128], rhs=xTx[:, k, :],
                                                               start=(k == 0), stop=(k == 7)))(ps, slot_, cbl, k),
                  ["wgrp%d" % slot_, "xTx_%d" % k], [pt])
            return ps, pt

        def glu_stage(t):
            P.tag = 'B%d_glu' % t
            glu_ps = []
            for eb in range(4):
                i, ps, pt = nextps()
                for k in range(4):
                    T((lambda ps, eb, k: lambda e: e.matmul(ps, lhsT=glu_sb[:, k, eb * 128:(eb + 1) * 128], rhs=g_tb[:, k, :],
                                                           start=(k == 0), stop=(k == 3)))(ps, eb, k), ["glu_sb", "g_tb"], [pt])
                glu_ps.append((ps, pt))
            for eb in range(4):
                ps, pt = glu_ps[eb]
                sg, sgt = tmp()
                S((lambda ps, eb, sg: lambda e: e.activation(out=sg, in_=ps, func=AF.Sigmoid, bias=glub_fm[:, eb:eb + 1], scale=1.0))(ps, eb, sg),
                  [pt, "glub_fm"], [sgt])
                V((lambda eb, sg: lambda e: e.tensor_tensor(out=g_tb[:, eb, :], in0=g_tb[:, eb, :], in1=sg, op=ALU.mult))(eb, sg), [sgt, "g_tb"], ["g_tb"])

        def conv_stage(t, cbs):
            P.tag = 'B%d_conv' % t
            for cb in cbs:
                hp, hpt = proj_block(("cv", t, cb), 0)
                cp, cpt = proj_block(("cv", t, cb), 1)
                bp, bpt = proj_block(("cv", t, cb), 2)
                pump(1)
                hsb, hsbt = tmp()
                zt, ztt = tmp()
                S((lambda hp, cb, hsb: lambda e: e.activation(out=hsb, in_=hp, func=AF.Identity, bias=bias_fm[:, 4 + cb:5 + cb], scale=1.0))(hp, cb, hsb),
                  [hpt, "bias_fm"], [hsbt])
                V((lambda cp, cb, hsb: lambda e: e.scalar_tensor_tensor(out=vbuf[:, cb, 2:514], in0=cp, scalar=bias_fm[:, 8 + cb:9 + cb], in1=hsb,
                                                                        op0=ALU.add, op1=ALU.mult))(cp, cb, hsb), [cpt, "bias_fm", hsbt], ["vbuf%d" % cb])
                V((lambda cb, zt: lambda e: e.tensor_scalar(out=zt, in0=vbuf[:, cb, 0:512], scalar1=convw_fm[:, cb:cb + 1], scalar2=None, op0=ALU.mult))(cb, zt),
                  ["vbuf%d" % cb, "convw_fm"], [ztt])
                V((lambda cb, zt: lambda e: e.scalar_tensor_tensor(out=zt, in0=vbuf[:, cb, 1:513], scalar=convw_fm[:, 4 + cb:5 + cb], in1=zt,
                                                                   op0=ALU.mult, op1=ALU.add))(cb, zt), ["vbuf%d" % cb, "convw_fm", ztt], [ztt])
                V((lambda cb, zt: lambda e: e.scalar_tensor_tensor(out=zt, in0=vbuf[:, cb, 2:514], scalar=convw_fm[:, 8 + cb:9 + cb], in1=zt,
                                                                   op0=ALU.mult, op1=ALU.add))(cb, zt), ["vbuf%d" % cb, "convw_fm", ztt], [ztt])
                V((lambda bp, cb, zt: lambda e: e.scalar_tensor_tensor(out=bz[:, cb, :], in0=bp, scalar=bias_fm[:, 12 + cb:13 + cb], in1=zt,
                                                                       op0=ALU.add, op1=ALU.mult))(bp, cb, zt), [bpt, "bias_fm", ztt], ["bz%d" % cb])
                V((lambda cb: lambda e: e.tensor_copy(out=vbuf[:, cb, 0:2], in_=vbuf[:, cb, 512:514]))(cb), ["vbuf%d" % cb], ["vbuf%d" % cb])

        ff_ctr = [0]
        wd_ctr = [0]
        P.tag = 'B0_xT'
        load_x_bf16(0)
        DMA(lambda e: e.dma_start(out=glu_sb, in_=glu_w.rearrange("(k p) n -> p k n", p=128)), [], ["glu_sb"], eng="gpsimd")
        DMA(lambda e: e.dma_start(out=wso_sb, in_=w_ssm_out.rearrange("(k p) n -> p k n", p=128)), [], ["wso_sb"], eng="gpsimd")
        DMA(lambda e: e.dma_start(out=wco_sb, in_=w_conv_out.rearrange("(k p) n -> p k n", p=128)), [], ["wco_sb"], eng="gpsimd")
        DMA(lambda e: e.dma_start(out=wo_sb, in_=w_o.rearrange("(k p) n -> p k n", p=128)), [], ["wo_sb"], eng="gpsimd")
        pump(3)
        emit_xT()
        load_x_bf16(1)
        conv_stage(0, range(4))
        ln2_q = []
        def load_g(t):
            DMA((lambda t: lambda e: e.dma_start(out=g_tb, in_=g_s[:, t * 512:(t + 1) * 512].rearrange("(k p) n -> p k n", p=128)))(t),
                ["g_s%d" % (t // 2)], ["g_tb"], key="g_tb")

        load_g(0)
        glu_stage(0)
        for t in range(NT):
            P.tag = 'B%d_merged' % t
            for db in range(8):
                gap, gapt = proj_block(("gt", t, db), 0)
                gbp, gbpt = proj_block(("gt", t, db), 1)
                pump(1)
                i, yap, yapt = nextps()
                for k in range(4):
                    T((lambda yap, db, k: lambda e: e.matmul(yap, lhsT=wso_sb[:, k, db * 128:(db + 1) * 128], rhs=g_tb[:, k, :],
                                                            start=(k == 0), stop=(k == 3)))(yap, db, k), ["wso_sb", "g_tb"], [yapt])
                i, ybp, ybpt = nextps()
                for k in range(4):
                    T((lambda ybp, db, k: lambda e: e.matmul(ybp, lhsT=wco_sb[:, k, db * 128:(db + 1) * 128], rhs=bz[:, k, :],
                                                            start=(k == 0), stop=(k == 3)))(ybp, db, k), ["wco_sb", "bz%d" % k], [ybpt])
                sa, sat = tmp()
                sb_, sbt = tmp()
                S((lambda gap, db, sa: lambda e: e.activation(out=sa, in_=gap, func=AF.Sigmoid, bias=bias_fm[:, 16 + db:17 + db], scale=1.0))(gap, db, sa),
                  [gapt, "bias_fm"], [sat])
                S((lambda gbp, db, sb_: lambda e: e.activation(out=sb_, in_=gbp, func=AF.Sigmoid, bias=bias_fm[:, 24 + db:25 + db], scale=1.0))(gbp, db, sb_),
                  [gbpt, "bias_fm"], [sbt])
                V((lambda yap, sa: lambda e: e.tensor_tensor(out=sa, in0=yap, in1=sa, op=ALU.mult))(yap, sa), [yapt, sat], [sat])
                V((lambda ybp, sb_: lambda e: e.tensor_tensor(out=sb_, in0=ybp, in1=sb_, op=ALU.mult))(ybp, sb_), [ybpt, sbt], [sbt])
                V((lambda db, sa, sb_: lambda e: e.tensor_tensor(out=merged[:, db, :], in0=sa, in1=sb_, op=ALU.add))(db, sa, sb_), [sat, sbt], ["merged%d" % db])
                for _ in range({1: 1, 2: 2, 3: 2, 4: 2, 5: 2}.get(db, 0)):
                    if ln2_q:
                        ln2_q.pop(0)()
            ffq = list(range(NFB))
            ffslot = {}

            def ff_store(fb):
                fs_ = ffslot[fb]
                DMA((lambda fs_, fb: lambda e: e.dma_start(out=wgu_s[fb], in_=ffw[fs_]))(fs_, fb), ["ffw%d" % fs_], ["wgu%d" % fb], key="ffst%d" % fs_)

            def ffpump(n):
                for _ in range(n):
                    if ffq:
                        fb = ffq.pop(0)
                        fs = ff_ctr[0] % NFS
                        ff_ctr[0] += 1
                        ffslot[fb] = fs
                        if t == 0:
                            if fb >= 1:
                                ff_store(fb - 1)
                            for wh, wsrc in enumerate((w_gate, w_up)):
                                DMA((lambda fs, fb, wh, wsrc: lambda e: e.dma_start(
                                    out=ffw[fs][:, wh], in_=wsrc[:, fb * 128:(fb + 1) * 128].rearrange("(k p) n -> p k n", p=128)))(fs, fb, wh, wsrc),
                                    [], ["ffw%d" % fs], key="ffwc%d" % fs, eng="gpsimd")
                        else:
                            DMA((lambda fs, fb: lambda e: e.dma_start(out=ffw[fs], in_=wgu_s[fb]))(fs, fb), ["wgu%d" % fb], ["ffw%d" % fs],
                                key="ffw%d" % fs)

            ffpump(NFS)
            P.tag = 'B%d_wo' % t
            for sub in range(4):
                r0 = t * 512 + sub * 128
                DMA((lambda sub, r0: lambda e: e.dma_start(out=x1[:, sub, :], in_=x[r0:r0 + 128, :]))(sub, r0), [], ["x1_%d" % sub], key="xres%d" % sub)
            for sub in range(4):
                for half in range(2):
                    i, ps, pt = nextps()
                    for k in range(8):
                        T((lambda ps, sub, half, k: lambda e: e.matmul(ps, lhsT=merged[:, k, sub * 128:(sub + 1) * 128],
                                                                      rhs=wo_sb[:, k, half * 512:(half + 1) * 512],
                                                                      start=(k == 0), stop=(k == 7)))(ps, sub, half, k), ["merged%d" % k, "wo_sb"], [pt])
                    V((lambda ps, sub, half: lambda e: e.scalar_tensor_tensor(
                        out=x1[:, sub, half * 512:(half + 1) * 512], in0=x1[:, sub, half * 512:(half + 1) * 512], scalar=ALPHA, in1=ps,
                        op0=ALU.mult, op1=ALU.add))(ps, sub, half), [pt, "x1_%d" % sub], ["x1_%d" % sub])
            xt_ = ["x1_%d" % s_ for s_ in range(4)]
            for sub in range(4):
                for half in range(2):
                    V((lambda sub, half: lambda e: e.bn_stats(out=stats[:, sub, half, :], in_=x1[:, sub, half * 512:(half + 1) * 512]))(sub, half),
                      [xt_[sub]], ["stats%d" % sub])
                V((lambda sub: lambda e: e.bn_aggr(out=mv[:, sub, :], in_=stats[:, sub].rearrange("p a b -> p (a b)")))(sub), ["stats%d" % sub], ["mv"])
            S(lambda e: e.activation(out=rstd4, in_=mv[:, :, 1], func=AF.Sqrt, bias=eps_t, scale=1.0), ["mv", "eps_t"], ["rstd4"])
            V(lambda e: e.reciprocal(out=rstd4, in_=rstd4), ["rstd4"], ["rstd4"])
            V(lambda e: e.scalar_tensor_tensor(out=nmr4, in0=mv[:, :, 0], scalar=-1.0, in1=rstd4, op0=ALU.mult, op1=ALU.mult), ["mv", "rstd4"], ["nmr4"])
            for sub in range(4):
                S((lambda sub: lambda e: e.activation(out=xb2[:, sub, :], in_=x1[:, sub, :], func=AF.Identity,
                                                      bias=nmr4[:, sub:sub + 1], scale=rstd4[:, sub:sub + 1]))(sub), [xt_[sub], "rstd4", "nmr4"], ["xb2_%d" % sub])
            if t + 1 < NT:
                P.tag = 'B%d_xT' % (t + 1)
                emit_xT()
                if t + 2 < NT:
                    load_x_bf16(t + 2)
                conv_stage(t + 1, (0, 1))
            P.tag = 'B%d_x1T' % t
            for k in range(8):
                i, ps, pt = nextps()
                pb = psbf(i)
                for sub in range(4):
                    T((lambda pb, sub, k: lambda e: e.transpose(out=pb[:, sub * 128:(sub + 1) * 128], in_=xb2[:, sub, k * 128:(k + 1) * 128],
                                                                identity=ident_b))(pb, sub, k), ["xb2_%d" % sub, "ident_b"], [pt])
                S((lambda pb, k: lambda e: e.activation(out=xT2[:, k, :], in_=pb[:, 0:512], func=AF.Identity,
                                                        bias=b1_fm[:, k:k + 1], scale=g1_fm[:, k:k + 1]))(pb, k), [pt, "g1_fm", "b1_fm"], ["xT2_%d" % k])
            if t + 1 < NT:
                conv_stage(t + 1, (2, 3))
            ln1_q = []
            for sub in range(4):
                ln1_q.append((lambda sub: lambda: V((lambda sub: lambda e: e.tensor_scalar(
                    out=x1[:, sub, :], in0=x1[:, sub, :], scalar1=mv[:, sub, 0:1], scalar2=rstd4[:, sub:sub + 1],
                    op0=ALU.subtract, op1=ALU.mult))(sub), [xt_[sub], "mv", "rstd4"], [xt_[sub]]))(sub))
            for sub in range(4):
                ln1_q.append((lambda sub: lambda: V((lambda sub: lambda e: e.tensor_tensor(
                    out=x1[:, sub, :], in0=x1[:, sub, :], in1=g1bc, op=ALU.mult))(sub), [xt_[sub], "g1bc"], [xt_[sub]]))(sub))
            if t + 1 < NT:
                load_g(t + 1)
            P.tag = 'B%d_ffn' % t
            wdq = list(range(NFB))
            wdslot = {}

            def wdpump(n):
                for _ in range(n):
                    if wdq:
                        fb = wdq.pop(0)
                        ds_ = wd_ctr[0] % NDS
                        wd_ctr[0] += 1
                        wdslot[fb] = ds_
                        DMA((lambda ds_, fb: lambda e: e.dma_start(out=wdb[ds_], in_=wd_s[fb * 128:(fb + 1) * 128, :]))(ds_, fb),
                            ["wd_s"], ["wdb%d" % ds_], key="wdb%d" % ds_)

            for fb in range(NFB):
                fs = ffslot[fb]
                i, gp, gpt = nextps()
                for k in range(8):
                    T((lambda gp, fs, k: lambda e: e.matmul(gp, lhsT=ffw[fs][:, 0, k, :], rhs=xT2[:, k, :],
                                                           start=(k == 0), stop=(k == 7)))(gp, fs, k), ["ffw%d" % fs, "xT2_%d" % k], [gpt])
                i, up, upt = nextps()
                for k in range(8):
                    T((lambda up, fs, k: lambda e: e.matmul(up, lhsT=ffw[fs][:, 1, k, :], rhs=xT2[:, k, :],
                                                           start=(k == 0), stop=(k == 7)))(up, fs, k), ["ffw%d" % fs, "xT2_%d" % k], [upt])
                ffpump(1)
                if fb == NFB - 3:
                    wdpump(NDS)
                sgl, sglt = tmp()
                S((lambda gp, sgl: lambda e: e.activation(out=sgl, in_=gp, func=AF.Silu))(gp, sgl), [gpt], [sglt])
                V((lambda up, fb, sgl: lambda e: e.tensor_tensor(out=hid[:, fb, :], in0=up, in1=sgl, op=ALU.mult))(up, fb, sgl), [upt, sglt], ["hid%d" % fb])
                if ln1_q:
                    ln1_q.pop(0)()
            if t == 0:
                ff_store(NFB - 1)
            if t + 1 < NT:
                glu_stage(t + 1)
            P.tag = 'B%d_down' % t
            banks = {}
            for sub in range(4):
                for half in range(2):
                    banks[(sub, half)] = nextps()
            for sub in range(4):
                for half in range(2):
                    i, ps, pt = banks[(sub, half)]
                    T((lambda ps, half: lambda e: e.matmul(ps, lhsT=ones_b, rhs=brow2[:, half * 512:(half + 1) * 512], start=True, stop=False))(ps, half),
                      ["ones_b", "brow2"], [pt])
            for fb in range(NFB):
                ds_ = wdslot[fb]
                for sub in range(4):
                    for half in range(2):
                        i, ps, pt = banks[(sub, half)]
                        T((lambda ps, ds_, fb, sub, half: lambda e: e.matmul(ps, lhsT=hid[:, fb, sub * 128:(sub + 1) * 128],
                                                                            rhs=wdb[ds_][:, half * 512:(half + 1) * 512],
                                                                            start=False, stop=(fb == NFB - 1)))(ps, ds_, fb, sub, half),
                          ["hid%d" % fb, "wdb%d" % ds_], [pt])
                wdpump(1)
            for sub in range(4):
                for half in range(2):
                    i, ps, pt = banks[(sub, half)]
                    V((lambda ps, sub, half: lambda e: e.tensor_tensor(
                        out=x1[:, sub, half * 512:(half + 1) * 512], in0=ps, in1=x1[:, sub, half * 512:(half + 1) * 512],
                        op=ALU.add))(ps, sub, half), [pt, "x1_%d" % sub], ["x1_%d" % sub])
            dfr = ln_stage(g2bc, b2bc, "g2bc", "b2bc")
            ln2_q = [dfr[0]]
            for sub in range(4):
                r0 = t * 512 + sub * 128
                ln2_q.append(dfr[1 + 2 * sub])
                ln2_q.append((lambda sub, r0, f: lambda: (f(), DMA((lambda sub, r0: lambda e: e.dma_start(out=out[r0:r0 + 128, :], in_=x1[:, sub, :]))(sub, r0),
                                                                     ["x1_%d" % sub], [], key="ost%d" % sub, eng="gpsimd")))(sub, r0, dfr[2 + 2 * sub]))
            if t == NT - 1:
                for f in ln2_q:
                    f()
                ln2_q = []

        P.emit()
        import os
        if os.environ.get('KDUMP_TAGS'):
            import json
            json.dump({e: [o.tag for o in P.ops[e] if o.fn is not None] for e in ENGS}, open(os.environ['KDUMP_TAGS'], 'w'))
    return nc


_NC_CACHE = {}


def kernel(**inputs):
    if "nc" not in _NC_CACHE:
        _NC_CACHE["nc"] = build_nc()
    nc = _NC_CACHE["nc"]
    x = np.ascontiguousarray(inputs["x"], dtype=np.float32)
    shared = {}
    for k, v in inputs.items():
        if k == "x":
            continue
        shared[k] = np.ascontiguousarray(np.asarray(v, dtype=np.float32)[0])
    in_maps = []
    for c in range(NCORES):
        m = dict(shared)
        m["x"] = x[c]
        in_maps.append(m)
    res = run_bass_kernel_spmd(nc, in_maps, core_ids=list(range(NCORES)))
    outs = [np.asarray(res.results[c]["out"], dtype=np.float32) for c in range(NCORES)]
    return np.stack(outs, axis=0)
```

```python
import math
import contextlib
import numpy as np
import concourse.bass as bass
import concourse.mybir as mybir
from concourse.bass_utils import run_bass_kernel_spmd

F32 = mybir.dt.float32
BF16 = mybir.dt.bfloat16
I32 = mybir.dt.int32
U8 = mybir.dt.uint8
ALU = mybir.AluOpType
AF = mybir.ActivationFunctionType

D = 1024
SEQ = 4096
NCORES = 8
FFN = 2816
NFB = FFN // 128
ALPHA = 2.0 ** 0.25
LN_EPS = 1e-5
TWO_PI = 2.0 * math.pi
MAGIC = 12582912.0
SIN_SCALE = 1.0 - 2e-6

ENGS = ("tensor", "vector", "scalar", "gpsimd", "sync")


class _Op:
    __slots__ = ("eng", "fn", "deps", "is_dma", "dma_key", "dma_target", "signal", "seq", "tag")

    def __init__(self, eng, fn, is_dma=False):
        self.eng = eng
        self.fn = fn
        self.deps = []
        self.is_dma = is_dma
        self.dma_key = None
        self.dma_target = 0
        self.signal = False
        self.seq = 0
        self.tag = ''


class Prog:
    def __init__(self, nc):
        self.nc = nc
        self.ops = {e: [] for e in ENGS}
        self.last_writer = {}
        self.readers = {}
        self.dma_counts = {}
        self.last_dma = {}
        self.all_ops = []
        self.tag = ''

    def _add_dep(self, op, dep):
        if dep is None or dep is op:
            return
        if (dep.eng == op.eng and not op.is_dma and not dep.is_dma
                and op.eng in ("tensor",)):
            return
        op.deps.append(dep)
        if not dep.is_dma:
            dep.signal = True

    def op(self, eng, fn, reads=(), writes=(), dma_key=None):
        is_dma = dma_key is not None
        o = _Op(eng, fn, is_dma)
        o.tag = self.tag
        for t in reads:
            self._add_dep(o, self.last_writer.get(t))
        for t in writes:
            self._add_dep(o, self.last_writer.get(t))
            for r in self.readers.get(t, ()):
                self._add_dep(o, r)
        for t in reads:
            self.readers.setdefault(t, []).append(o)
        for t in writes:
            self.last_writer[t] = o
            self.readers[t] = []
        if is_dma:
            c = self.dma_counts.get(dma_key, 0) + 16
            self.dma_counts[dma_key] = c
            o.dma_key = dma_key
            o.dma_target = c
            self.last_dma[dma_key] = o
        self.ops[eng].append(o)
        self.all_ops.append(o)
        return o

    def fence(self, fence_fns):
        fs = []
        for e, fn in fence_fns.items():
            o = _Op(e, fn)
            o.signal = True
            self.ops[e].append(o)
            self.all_ops.append(o)
            fs.append(o)
        dm = [o for k, o in self.last_dma.items() if not str(k).startswith('cv_')]
        for e in ENGS:
            g = _Op(e, None)
            g.deps = [f for f in fs] + dm
            self.ops[e].append(g)
            self.all_ops.append(g)

    def emit(self, final_wait_eng="sync"):
        nc = self.nc
        fin = _Op(final_wait_eng, None)
        fin.deps = list(self.last_dma.values())
        self.ops[final_wait_eng].append(fin)
        for e in ENGS:
            c = 0
            for o in self.ops[e]:
                if o.signal and not o.is_dma:
                    c += 1
                    o.seq = c
        with contextlib.ExitStack() as st:
            esem = {e: st.enter_context(nc.semaphore("es_" + e)) for e in ENGS}
            dsem = {}
            for i, k in enumerate(self.dma_counts):
                dsem[k] = st.enter_context(nc.semaphore("ds_%d" % i))
            block = st.enter_context(nc.Block())
            ops = self.ops

            def make(e):
                def body(eng):
                    waited = {}
                    for o in ops[e]:
                        for d in o.deps:
                            if d.is_dma:
                                s, v, k = dsem[d.dma_key], d.dma_target, ("d", d.dma_key)
                            else:
                                s, v, k = esem[d.eng], d.seq, ("e", d.eng)
                            if waited.get(k, 0) >= v:
                                continue
                            waited[k] = v
                            eng.wait_ge(s, v)
                        if o.fn is None:
                            continue
                        ins = o.fn(eng)
                        if o.is_dma:
                            ins.then_inc(dsem[o.dma_key], 16)
                        elif o.signal:
                            ins.then_inc(esem[e], 1)
                return body

            block.tensor(make("tensor"))
            block.vector(make("vector"))
            block.scalar(make("scalar"))
            block.gpsimd(make("gpsimd"))
            block.sync(make("sync"))


class Arena:
    def __init__(self, big, nbytes):
        self.big = big
        self.n = nbytes
        self.top = 0

    def buf(self, shape, dt, parts=128):
        esz = {F32: 4, BF16: 2, I32: 4}[dt]
        n = int(np.prod(shape)) * esz
        n_al = (n + 63) // 64 * 64
        off = self.top
        assert off + n_al <= self.n, ("arena overflow", off, n_al, self.n)
        self.top += n_al
        ap = self.big[0:parts, off:off + n].bitcast(dt)
        if len(shape) > 1:
            names = " ".join("d%d" % i for i in range(len(shape)))
            kw = {"d%d" % i: int(s) for i, s in enumerate(shape)}
            ap = ap.rearrange("p (%s) -> p %s" % (names, names), **kw)
        return ap

    def mark(self):
        return self.top

    def release(self, m):
        self.top = m


def bc(ap, shape):
    return ap.broadcast_to(list(shape))


def build_nc(debug=False):
    nc = bass.Bass("TRN2", target_bir_lowering=False)

    def din(name, shape):
        return nc.dram_tensor(name, list(shape), F32, kind="ExternalInput").ap()

    x = din("x", [SEQ, D])
    w_in = din("w_in", [D, 4096])
    b_in = din("b_in", [4096])
    lam_re = din("ssm_lambda_re", [32, 64])
    lam_im = din("ssm_lambda_im", [32, 64])
    log_dt = din("ssm_log_dt", [32])
    b_re = din("ssm_b_re", [32, 64, 16])
    b_im = din("ssm_b_im", [32, 64, 16])
    c_re = din("ssm_c_re", [32, 16, 64])
    c_im = din("ssm_c_im", [32, 16, 64])
    ssm_d = din("ssm_d", [512])
    glu_w = din("glu_w", [512, 512])
    glu_b = din("glu_b", [512])
    w_ssm_out = din("w_ssm_out", [512, D])
    conv_w = din("conv_w", [3, 512])
    w_conv_out = din("w_conv_out", [512, D])
    w_o = din("w_o", [D, D])
    ln1_g = din("ln1_g", [D])
    ln1_b = din("ln1_b", [D])
    w_gate = din("w_gate", [D, FFN])
    w_up = din("w_up", [D, FFN])
    w_down = din("w_down", [FFN, D])
    ln2_g = din("ln2_g", [D])
    ln2_b = din("ln2_b", [D])
    out = nc.dram_tensor("out", [SEQ, D], F32, kind="ExternalOutput").ap()

    win_r = nc.dram_tensor("win_r", [D, 3584], BF16, kind="Internal").ap()
    wcv_s = nc.dram_tensor("wcv_s", [4, 128, 8, 384], BF16, kind="Internal").ap()
    wgt_s = nc.dram_tensor("wgt_s", [8, 128, 8, 256], BF16, kind="Internal").ap()
    wgu_s = nc.dram_tensor("wgu_s", [NFB, 128, 2, 8, 128], BF16, kind="Internal").ap()
    wd_s = nc.dram_tensor("wd_s", [FFN, D], BF16, kind="Internal").ap()
    g_s = nc.dram_tensor("g_s", [512, SEQ], BF16, kind="ExternalOutput" if debug else "Internal").ap()

    ARENA_BYTES = 206 * 1024
    with contextlib.ExitStack() as st:
        big = st.enter_context(nc.sbuf_tensor("arena", [128, ARENA_BYTES], U8))
        psb = [st.enter_context(nc.psum_tensor("ps%d" % i, [128, 512], F32)) for i in range(8)]
        A = Arena(big, ARENA_BYTES)
        P = Prog(nc)

        ps_rr = [0]

        def nextps():
            i = ps_rr[0]
            ps_rr[0] = (i + 1) % 8
            return i, psb[i][:], "ps%d" % i

        def psbf(i):
            return psb[i][:].bitcast(BF16)

        def V(fn, reads, writes):
            return P.op("vector", fn, reads, writes)

        def S(fn, reads, writes):
            return P.op("scalar", fn, reads, writes)

        def G(fn, reads, writes):
            return P.op("gpsimd", fn, reads, writes)

        def T(fn, reads, writes):
            return P.op("tensor", fn, reads, writes)

        dma_ctr = [0]

        def DMA(fn, reads, writes, key=None, eng="sync"):
            if key is None:
                key = "dma%d" % dma_ctr[0]
                dma_ctr[0] += 1
            return P.op(eng, fn, reads, writes, dma_key=key)

        ident_f = A.buf([128], F32)
        ident_b = A.buf([128], BF16)
        iota_i = A.buf([128], I32)
        fence_v = A.buf([1], F32)
        fence_s = A.buf([1], F32)
        fence_g = A.buf([1], F32)
        bias_fm = A.buf([32], F32)
        bias_u = A.buf([512], F32)
        glub_fm = A.buf([4], F32)
        g1_fm = A.buf([8], F32)
        b1_fm = A.buf([8], F32)
        convw_fm = A.buf([12], F32)
        vbuf = A.buf([4, 514], F32)
        r8tab = A.buf([16], F32)
        C1t = A.buf([16, 2], F32)
        C2t = A.buf([16, 2], F32)
        wcar = A.buf([16, 2], F32)
        eps_t = A.buf([1], F32)
        ones_b = A.buf([128], BF16)

        fence_fns = {
            "vector": lambda e: e.memset(fence_v, 0.0),
            "gpsimd": lambda e: e.memset(fence_g, 0.0),
            "scalar": lambda e: e.activation(out=fence_s, in_=ident_f[:, 0:1], func=AF.Copy),
        }

        mA = A.mark()
        winu = A.buf([8, 512], BF16)
        DMA(lambda e: e.dma_start(out=winu, in_=w_in[:, 0:512].rearrange("(k p) n -> p k n", p=128)),
            [], ["winu"], eng="gpsimd")
        def convert_chunk(c):
            for r in (2 * c, 2 * c + 1):
                rs = slice(r * 128, (r + 1) * 128)
                DMA((lambda rs: lambda e: e.dma_start(out=win_r[rs, :].rearrange("r (c e) -> r c e", e=896),
                                                      in_=w_in[rs, 512:4096].rearrange("r (c e) -> r c e", e=896)))(rs),
                    [], ["win_r%d" % c], key="cv_win%d" % c, eng="gpsimd")
            for r in range(c * 6, min(NFB, (c + 1) * 6)):
                rs = slice(r * 128, (r + 1) * 128)
                DMA((lambda rs: lambda e: e.dma_start(out=wd_s[rs, :], in_=w_down[rs, :]))(rs),
                    [], ["wd_s"], key="cv_wd", eng="gpsimd")

        def retile_chunk(c):
            for k in (2 * c, 2 * c + 1):
                rs = slice(k * 128, (k + 1) * 128)
                for wh in range(3):
                    DMA((lambda rs, k, wh: lambda e: e.dma_start(
                        out=wcv_s[:, :, k, wh * 128:(wh + 1) * 128].rearrange("c p n -> p c n"),
                        in_=win_r[rs, wh * 512:(wh + 1) * 512].rearrange("p (c n) -> p c n", n=128)))(rs, k, wh),
                        ["win_r%d" % c], ["wcv_s"], key="cv2_win")
                for wh in range(2):
                    DMA((lambda rs, k, wh: lambda e: e.dma_start(
                        out=wgt_s[:, :, k, wh * 128:(wh + 1) * 128].rearrange("c p n -> p c n"),
                        in_=win_r[rs, 1536 + wh * 1024: 1536 + (wh + 1) * 1024].rearrange("p (c n) -> p c n", n=128)))(rs, k, wh),
                        ["win_r%d" % c], ["wgt_s"], key="cv2_wgt")

        P.tag = 'P0'
        G(lambda e: e.iota(iota_i, pattern=[[1, 128]], base=0, channel_multiplier=-1), [], ["iota_i"])
        V(lambda e: e.tensor_scalar(out=ident_f, in0=iota_i, scalar1=0.0, scalar2=None, op0=ALU.is_equal), ["iota_i"], ["ident_f"])
        V(lambda e: e.tensor_copy(out=ident_b, in_=ident_f), ["ident_f"], ["ident_b"])
        V(lambda e: e.memset(vbuf, 0.0), [], ["vbuf0", "vbuf1", "vbuf2", "vbuf3"])

        DMA(lambda e: e.dma_start(out=bias_u, in_=b_in[0:512].partition_broadcast(128)), [], ["bias_u"])

        Ec = A.buf([16, 128], F32)
        Es = A.buf([16, 128], F32)
        M0 = A.buf([32, 128], BF16)
        Hpad = A.buf([32, 2, 128], BF16)
        Gpad = A.buf([32, 2, 128], BF16)
        m0 = A.mark()
        nat = A.buf([128], F32)
        DMA(lambda e: e.dma_start(out=nat[0:32, :], in_=b_in.rearrange("(c p) -> c p", p=128)), [], ["nat_bin"])
        nat2 = A.buf([128], F32)
        DMA(lambda e: e.dma_start(out=nat2[0:4, :], in_=glu_b.rearrange("(c p) -> c p", p=128)), [], ["nat_glub"])
        nat3 = A.buf([128], F32)
        DMA(lambda e: e.dma_start(out=nat3[0:12, :], in_=conv_w.rearrange("k (c p) -> (k c) p", p=128)), [], ["nat_convw"])

        def small_T(dst, src_nat, K, rtok, wtok):
            i, ps, pt = nextps()
            T(lambda e: e.transpose(out=ps[:, 0:K], in_=src_nat[0:K, :], identity=ident_f[0:K, 0:K]), [rtok, "ident_f"], [pt])
            V(lambda e: e.tensor_copy(out=dst, in_=ps[:, 0:K]), [pt], [wtok])

        small_T(bias_fm, nat, 32, "nat_bin", "bias_fm")
        nat4 = A.buf([128], F32)
        DMA(lambda e: e.dma_start(out=nat4[0:8, :], in_=ln1_g.rearrange("(c p) -> c p", p=128)), [], ["nat_g1"])
        nat5 = A.buf([128], F32)
        DMA(lambda e: e.dma_start(out=nat5[0:8, :], in_=ln1_b.rearrange("(c p) -> c p", p=128)), [], ["nat_b1"])
        small_T(g1_fm, nat4, 8, "nat_g1", "g1_fm")
        small_T(b1_fm, nat5, 8, "nat_b1", "b1_fm")
        small_T(glub_fm, nat2, 4, "nat_glub", "glub_fm")
        small_T(convw_fm, nat3, 12, "nat_convw", "convw_fm")

        lr = A.buf([16], F32)
        li = A.buf([16], F32)
        ldt = A.buf([16], F32)
        Br = A.buf([16, 16], F32)
        Bi = A.buf([16, 16], F32)
        Cr = A.buf([16, 16], F32)
        Ci = A.buf([16, 16], F32)
        cnat_r = A.buf([4, 64], F32)
        cnat_i = A.buf([4, 64], F32)
        C64r = A.buf([512], F32)
        C64i = A.buf([512], F32)
        dT = A.buf([32], F32)
        for g2 in range(2):
            ps_ = slice(g2 * 64, (g2 + 1) * 64)
            DMA((lambda ps_, g2: lambda e: e.dma_start(out=lr[ps_, :], in_=lam_re.rearrange("(gh g2) p -> g2 p gh", g2=2)[g2],
                                                      allow_slow_non_contiguous=True))(ps_, g2), [], ["lr"], key="ld_lr")
            DMA((lambda ps_, g2: lambda e: e.dma_start(out=li[ps_, :], in_=lam_im.rearrange("(gh g2) p -> g2 p gh", g2=2)[g2],
                                                      allow_slow_non_contiguous=True))(ps_, g2), [], ["li"], key="ld_li")
            DMA((lambda ps_, g2: lambda e: e.dma_start(out=ldt[ps_, :], in_=bass.AP(tensor=log_dt.tensor, offset=g2, ap=[[0, 64], [2, 16]]),
                                                      allow_slow_non_contiguous=True))(ps_, g2), [], ["ldt"], key="ld_ldt")
            DMA((lambda ps_, g2: lambda e: e.dma_start(out=Br[ps_], in_=b_re.rearrange("(gh g2) p c -> g2 p gh c", g2=2)[g2]))(ps_, g2),
                [], ["Br"], key="ld_Br")
            DMA((lambda ps_, g2: lambda e: e.dma_start(out=Bi[ps_], in_=b_im.rearrange("(gh g2) p c -> g2 p gh c", g2=2)[g2]))(ps_, g2),
                [], ["Bi"], key="ld_Bi")
        DMA(lambda e: e.dma_start(out=cnat_r, in_=c_re.rearrange("g c p -> (g c) p").rearrange("(i r) p -> r i p", r=128)), [], ["cnat_r"])
        DMA(lambda e: e.dma_start(out=cnat_i, in_=c_im.rearrange("g c p -> (g c) p").rearrange("(i r) p -> r i p", r=128)), [], ["cnat_i"])
        DMA(lambda e: e.dma_start(out=dT[0:16, :], in_=ssm_d.rearrange("(g c) -> c g", c=16), allow_slow_non_contiguous=True), [], ["dT"])

        for cnat, C64, ctok, C128 in ((cnat_r, C64r, "cnat_r", Cr), (cnat_i, C64i, "cnat_i", Ci)):
            i, ps, pt = nextps()
            for t4 in range(4):
                T((lambda ps, cnat, t4: lambda e: e.transpose(out=ps[0:64, t4 * 128:(t4 + 1) * 128], in_=cnat[:, t4, :], identity=ident_f))(ps, cnat, t4),
                  [ctok, "ident_f"], [pt])
            S((lambda ps, C64: lambda e: e.copy(out=C64[0:64, :], in_=ps[0:64, :]))(ps, C64), [pt], [ctok + "64"])
            for g2 in range(2):
                DMA((lambda C64, C128, g2: lambda e: e.dma_start(
                    out=C128[g2 * 64:(g2 + 1) * 64],
                    in_=C64[0:64, :].rearrange("p (gh g2 c) -> p gh g2 c", g2=2, c=16)[:, :, g2, :]))(C64, C128, g2),
                    [ctok + "64"], [ctok + "128"], key="shuf_" + ctok)

        ev9 = A.buf([9], F32)
        evD = A.buf([8], F32)
        jv = A.buf([128], F32)
        ev_i = A.buf([128], I32)
        G(lambda e: e.iota(ev_i, pattern=[[1, 128]], base=0, channel_multiplier=0), [], ["ev_i"])
        V(lambda e: e.tensor_copy(out=jv, in_=ev_i), ["ev_i"], ["jv"])
        V(lambda e: e.tensor_copy(out=ev9, in_=ev_i[:, 0:9]), ["ev_i"], ["ev9"])
        V(lambda e: e.tensor_scalar(out=evD, in0=jv[:, 0:8], scalar1=-1.0, scalar2=7.0, op0=ALU.mult, op1=ALU.add), ["jv"], ["evD"])

        dt_t = A.buf([16], F32)
        a_t = A.buf([16], F32)
        th_t = A.buf([16], F32)
        S(lambda e: e.activation(out=dt_t, in_=ldt, func=AF.Exp), ["ldt"], ["dt_t"])
        V(lambda e: e.tensor_tensor(out=a_t, in0=lr, in1=dt_t, op=ALU.mult), ["lr", "dt_t"], ["a_t"])
        V(lambda e: e.tensor_tensor(out=th_t, in0=li, in1=dt_t, op=ALU.mult), ["li", "dt_t"], ["th_t"])

        def powtab(ev, n, nm, th_src=th_t, th_tok="th_t", lead=16, with_mag=True):
            ang = A.buf([lead, n], F32)
            kk = A.buf([lead, n], F32)
            rr = A.buf([lead, n], F32)
            rc = A.buf([lead, n], F32)
            sn = A.buf([lead, n], F32)
            cs = A.buf([lead, n], F32)
            thb = bc(th_src[:, :, None] if len(th_src.shape) == 2 else th_src, [128, lead, n])
            evb = bc(ev[:, None, :], [128, lead, n])
            V(lambda e: e.tensor_tensor(out=ang, in0=thb, in1=evb, op=ALU.mult), [th_tok, "ev9", "evD", "jv"], [nm + "ang"])
            V(lambda e: e.tensor_scalar(out=kk, in0=ang, scalar1=1.0 / TWO_PI, scalar2=MAGIC, op0=ALU.mult, op1=ALU.add), [nm + "ang"], [nm + "kk"])
            V(lambda e: e.tensor_scalar(out=kk, in0=kk, scalar1=-MAGIC, scalar2=None, op0=ALU.add), [nm + "kk"], [nm + "kk"])
            V(lambda e: e.scalar_tensor_tensor(out=rr, in0=kk, scalar=-TWO_PI, in1=ang, op0=ALU.mult, op1=ALU.add), [nm + "kk", nm + "ang"], [nm + "rr"])
            V(lambda e: e.tensor_scalar(out=rc, in0=rr, scalar1=math.pi / 2, scalar2=None, op0=ALU.add), [nm + "rr"], [nm + "rc"])
            V(lambda e: e.tensor_scalar(out=kk, in0=rc, scalar1=math.pi, scalar2=-TWO_PI, op0=ALU.is_gt, op1=ALU.mult), [nm + "rc", nm + "rr"], [nm + "kk"])
            V(lambda e: e.tensor_tensor(out=rc, in0=rc, in1=kk, op=ALU.add), [nm + "rc", nm + "kk"], [nm + "rc"])
            S(lambda e: e.activation(out=sn, in_=rr, func=AF.Sin, scale=SIN_SCALE), [nm + "rr"], [nm + "sn"])
            S(lambda e: e.activation(out=cs, in_=rc, func=AF.Sin, scale=SIN_SCALE), [nm + "rc"], [nm + "cs"])
            if not with_mag:
                return cs, sn, rr, None
            ea = A.buf([lead, n], F32)
            mag = A.buf([lead, n], F32)
            pre = A.buf([lead, n], F32)
            pim = A.buf([lead, n], F32)
            ab = bc(a_t[:, :, None], [128, lead, n])
            V(lambda e: e.tensor_tensor(out=ea, in0=ab, in1=evb, op=ALU.mult), ["a_t", "ev9", "evD"], [nm + "ea"])
            S(lambda e: e.activation(out=mag, in_=ea, func=AF.Exp), [nm + "ea"], [nm + "mag"])
            V(lambda e: e.tensor_tensor(out=pre, in0=mag, in1=cs, op=ALU.mult), [nm + "mag", nm + "cs"], [nm + "pre"])
            V(lambda e: e.tensor_tensor(out=pim, in0=mag, in1=sn, op=ALU.mult), [nm + "mag", nm + "sn"], [nm + "pim"])
            return pre, pim, rr, mag

        pA_re, pA_im, rA, magA = powtab(ev9, 9, "pA")
        pD_re, pD_im, _, _ = powtab(evD, 8, "pD")

        fr = A.buf([16], F32)
        fi = A.buf([16], F32)
        t1 = A.buf([16], F32)
        t2 = A.buf([16], F32)
        t3 = A.buf([16], F32)
        den = A.buf([16], F32)
        nre = A.buf([16], F32)
        lbre = pA_re[:, :, 1]
        lbim = pA_im[:, :, 1]
        V(lambda e: e.tensor_scalar(out=nre, in0=lbre, scalar1=-1.0, scalar2=None, op0=ALU.add), ["pApre"], ["nre"])
        V(lambda e: e.tensor_tensor(out=t1, in0=lr, in1=lr, op=ALU.mult), ["lr"], ["t1"])
        V(lambda e: e.tensor_tensor(out=t2, in0=li, in1=li, op=ALU.mult), ["li"], ["t2"])
        V(lambda e: e.tensor_tensor(out=den, in0=t1, in1=t2, op=ALU.add), ["t1", "t2"], ["den"])
        V(lambda e: e.reciprocal(out=den, in_=den), ["den"], ["den"])
        V(lambda e: e.tensor_tensor(out=t1, in0=nre, in1=lr, op=ALU.mult), ["nre", "lr", "den"], ["t1"])
        V(lambda e: e.tensor_tensor(out=t2, in0=lbim, in1=li, op=ALU.mult), ["pApim", "li", "den"], ["t2"])
        V(lambda e: e.tensor_tensor(out=t3, in0=t1, in1=t2, op=ALU.add), ["t1", "t2"], ["t3"])
        V(lambda e: e.tensor_tensor(out=fr, in0=t3, in1=den, op=ALU.mult), ["t3", "den"], ["fr"])
        V(lambda e: e.tensor_tensor(out=t1, in0=lbim, in1=lr, op=ALU.mult), ["pApim", "lr", "t3"], ["t1"])
        V(lambda e: e.tensor_tensor(out=t2, in0=nre, in1=li, op=ALU.mult), ["nre", "li", "t3"], ["t2"])
        V(lambda e: e.tensor_tensor(out=t3, in0=t1, in1=t2, op=ALU.subtract), ["t1", "t2", "fr"], ["t3"])
        V(lambda e: e.tensor_tensor(out=fi, in0=t3, in1=den, op=ALU.mult), ["t3", "den"], ["fi"])

        bbr = A.buf([16, 16], F32)
        bbi = A.buf([16, 16], F32)
        u1 = A.buf([16, 16], F32)
        u2 = A.buf([16, 16], F32)
        frb = bc(fr[:, :, None], [128, 16, 16])
        fib = bc(fi[:, :, None], [128, 16, 16])
        V(lambda e: e.tensor_tensor(out=u1, in0=frb, in1=Br, op=ALU.mult), ["fr", "Br"], ["u1"])
        V(lambda e: e.tensor_tensor(out=u2, in0=fib, in1=Bi, op=ALU.mult), ["fi", "Bi"], ["u2"])
        V(lambda e: e.tensor_tensor(out=bbr, in0=u1, in1=u2, op=ALU.subtract), ["u1", "u2"], ["bbr"])
        V(lambda e: e.tensor_tensor(out=u1, in0=frb, in1=Bi, op=ALU.mult), ["fr", "Bi", "bbr"], ["u1"])
        V(lambda e: e.tensor_tensor(out=u2, in0=fib, in1=Br, op=ALU.mult), ["fi", "Br", "bbr"], ["u2"])
        V(lambda e: e.tensor_tensor(out=bbi, in0=u1, in1=u2, op=ALU.add), ["u1", "u2"], ["bbi"])

        HTD_re = A.buf([16, 8, 16], F32)
        HTD_im = A.buf([16, 8, 16], F32)
        GA_re = A.buf([16, 9, 16], F32)
        GA_nim = A.buf([16, 9, 16], F32)
        w1 = A.buf([16, 9, 16], F32)
        w2 = A.buf([16, 9, 16], F32)

        def cplx_tab(eng_a, eng_b, pr, pi_, ptok, xr, xi, xtok, n, o_re, o_second, second_mode, otok):
            prb = bc(pr[:, :, :, None], [128, 16, n, 16])
            pib = bc(pi_[:, :, :, None], [128, 16, n, 16])
            xrb = bc(xr[:, :, None, :], [128, 16, n, 16])
            xib = bc(xi[:, :, None, :], [128, 16, n, 16])
            a1 = w1[:, :, 0:n, :]
            a2 = w2[:, :, 0:n, :]
            a3 = a1
            a4 = a2
            P.op(eng_a, lambda e: e.tensor_tensor(out=a1, in0=prb, in1=xrb, op=ALU.mult), [ptok + "pre", xtok[0]], ["w1"])
            P.op(eng_a, lambda e: e.tensor_tensor(out=a2, in0=pib, in1=xib, op=ALU.mult), [ptok + "pim", xtok[1]], ["w2"])
            P.op(eng_a, lambda e: e.tensor_tensor(out=o_re, in0=a1, in1=a2, op=ALU.subtract), ["w1", "w2"], [otok + "_re"])
            P.op(eng_b, lambda e: e.tensor_tensor(out=a3, in0=prb, in1=xib, op=ALU.mult), [ptok + "pre", xtok[1]], ["w1"])
            P.op(eng_b, lambda e: e.tensor_tensor(out=a4, in0=pib, in1=xrb, op=ALU.mult), [ptok + "pim", xtok[0]], ["w2"])
            if second_mode > 0:
                P.op(eng_b, lambda e: e.tensor_tensor(out=o_second, in0=a3, in1=a4, op=ALU.add), ["w1", "w2"], [otok + "_2"])
            else:
                P.op(eng_b, lambda e: e.tensor_scalar(out=a3, in0=a3, scalar1=-1.0, scalar2=None, op0=ALU.mult), ["w1"], ["w1"])
                P.op(eng_b, lambda e: e.tensor_tensor(out=o_second, in0=a3, in1=a4, op=ALU.subtract), ["w1", "w2"], [otok + "_2"])

        cplx_tab("vector", "gpsimd", pD_re, pD_im, "pD", bbr, bbi, ("bbr", "bbi"), 8, HTD_re, HTD_im, +1, "HTD")
        cplx_tab("vector", "gpsimd", pA_re, pA_im, "pA", Cr, Ci, ("cnat_r128", "cnat_i128"), 9, GA_re, GA_nim, -1, "GA")

        V(lambda e: e.tensor_copy(out=r8tab, in_=magA[:, :, 8]), ["pAmag"], ["r8tab"])
        psi = rA[:, :, 8]
        ang128 = A.buf([16], F32)
        k128 = A.buf([16], F32)
        r128 = A.buf([16], F32)
        rc128 = A.buf([16], F32)
        sn128 = A.buf([16], F32)
        cs128 = A.buf([16], F32)
        V(lambda e: e.tensor_scalar(out=ang128, in0=psi, scalar1=128.0, scalar2=None, op0=ALU.mult), ["pArr"], ["ang128"])
        V(lambda e: e.tensor_scalar(out=k128, in0=ang128, scalar1=1.0 / TWO_PI, scalar2=MAGIC, op0=ALU.mult, op1=ALU.add), ["ang128"], ["k128"])
        V(lambda e: e.tensor_scalar(out=k128, in0=k128, scalar1=-MAGIC, scalar2=None, op0=ALU.add), ["k128"], ["k128"])
        V(lambda e: e.scalar_tensor_tensor(out=r128, in0=k128, scalar=-TWO_PI, in1=ang128, op0=ALU.mult, op1=ALU.add), ["k128", "ang128"], ["r128"])
        V(lambda e: e.tensor_scalar(out=rc128, in0=r128, scalar1=math.pi / 2, scalar2=None, op0=ALU.add), ["r128"], ["rc128"])
        V(lambda e: e.tensor_scalar(out=k128, in0=rc128, scalar1=math.pi, scalar2=-TWO_PI, op0=ALU.is_gt, op1=ALU.mult), ["rc128", "r128"], ["k128"])
        V(lambda e: e.tensor_tensor(out=rc128, in0=rc128, in1=k128, op=ALU.add), ["rc128", "k128"], ["rc128"])
        S(lambda e: e.activation(out=sn128, in_=r128, func=AF.Sin, scale=SIN_SCALE), ["r128"], ["sn128"])
        S(lambda e: e.activation(out=cs128, in_=rc128, func=AF.Sin, scale=SIN_SCALE), ["rc128"], ["cs128"])
        V(lambda e: e.tensor_copy(out=C1t, in_=bc(cs128[:, :, None], [128, 16, 2])), ["cs128"], ["C1t"])
        V(lambda e: e.tensor_scalar(out=C2t[:, :, 0], in0=sn128, scalar1=-1.0, scalar2=None, op0=ALU.mult), ["sn128"], ["C2t"])
        V(lambda e: e.tensor_copy(out=C2t[:, :, 1], in_=sn128), ["sn128", "C2t"], ["C2t"])
        V(lambda e: e.memset(wcar, 0.0), [], ["wcar"])

        angj = w1.rearrange('p a b c -> p (a b c)')[:, 0:2048].rearrange('p (a b) -> p a b', a=16)
        kj = w2.rearrange('p a b c -> p (a b c)')[:, 0:2048].rearrange('p (a b) -> p a b', a=16)
        rj = angj
        rcj = kj
        psib = bc(psi[:, :, None], [128, 16, 128])
        jvb = bc(jv[:, None, :], [128, 16, 128])
        V(lambda e: e.tensor_tensor(out=angj, in0=psib, in1=jvb, op=ALU.mult), ["pArr", "jv", "GA_re", "GA_2", "HTD_re", "HTD_2"], ["w1"])
        V(lambda e: e.tensor_scalar(out=kj, in0=angj, scalar1=1.0 / TWO_PI, scalar2=MAGIC, op0=ALU.mult, op1=ALU.add), ["w1", "GA_re", "GA_2", "HTD_re", "HTD_2"], ["w2"])
        V(lambda e: e.tensor_scalar(out=kj, in0=kj, scalar1=-MAGIC, scalar2=None, op0=ALU.add), ["w2"], ["w2"])
        V(lambda e: e.scalar_tensor_tensor(out=rj, in0=kj, scalar=-TWO_PI, in1=angj, op0=ALU.mult, op1=ALU.add), ["w2", "w1"], ["w1"])
        V(lambda e: e.tensor_scalar(out=Ec, in0=rj, scalar1=math.pi / 2, scalar2=None, op0=ALU.add), ["w1"], ["Ec"])
        V(lambda e: e.tensor_scalar(out=kj, in0=Ec, scalar1=math.pi, scalar2=-TWO_PI, op0=ALU.is_gt, op1=ALU.mult), ["Ec", "w1"], ["w2"])
        V(lambda e: e.tensor_tensor(out=Ec, in0=Ec, in1=kj, op=ALU.add), ["Ec", "w2"], ["Ec"])
        S(lambda e: e.activation(out=Es, in_=rj, func=AF.Sin, scale=SIN_SCALE), ["w1"], ["Es"])
        S(lambda e: e.activation(out=Ec, in_=Ec, func=AF.Sin, scale=SIN_SCALE), ["Ec"], ["Ec"])

        V(lambda e: e.memset(Hpad, 0.0), [], ["Hpad"])
        V(lambda e: e.memset(Gpad, 0.0), [], ["Gpad"])
        V(lambda e: e.memset(M0, 0.0), [], ["M0"])
        cnt = 0
        for reim, (HT, htok) in enumerate(((HTD_re, "HTD_re"), (HTD_im, "HTD_2"))):
            for ghq in range(4):
                i, ps, pt = nextps()
                for l in range(4):
                    gh = ghq * 4 + l
                    T((lambda ps, HT, gh, l: lambda e: e.transpose(out=ps[:, l * 128:(l + 1) * 128],
                                                                   in_=HT[:, gh].rearrange("p s c -> p (s c)"), identity=ident_f))(ps, HT, gh, l),
                      [htok, "ident_f"], [pt])
                for l in range(4):
                    gh = ghq * 4 + l
                    for g2 in range(2):
                        fn = (lambda ps, gh, g2, l, reim: lambda e: e.tensor_copy(
                            out=Hpad[:, 2 * gh + g2, reim, g2 * 64:(g2 + 1) * 64],
                            in_=ps[:, l * 128 + g2 * 64: l * 128 + (g2 + 1) * 64]))(ps, gh, g2, l, reim)
                        if cnt % 2 == 0:
                            V(fn, [pt, "Hpad"], ["Hpad"])
                        else:
                            S((lambda ps, gh, g2, l, reim: lambda e: e.copy(
                                out=Hpad[:, 2 * gh + g2, reim, g2 * 64:(g2 + 1) * 64],
                                in_=ps[:, l * 128 + g2 * 64: l * 128 + (g2 + 1) * 64]))(ps, gh, g2, l, reim), [pt, "Hpad"], ["Hpad"])
                        cnt += 1
        for reim, (GT, gtok) in enumerate(((GA_re, "GA_re"), (GA_nim, "GA_2"))):
            for g2 in range(2):
                pp = slice(g2 * 64, (g2 + 1) * 64)
                V((lambda GT, pp, g2, reim: lambda e: e.tensor_copy(
                    out=Gpad[pp].rearrange("p (gh g2) r n -> p gh g2 r n", g2=2)[:, :, g2, reim, :],
                    in_=GT[pp, :, 1:9, :].rearrange("p g t c -> p g (t c)")))(GT, pp, g2, reim), [gtok, "Gpad"], ["Gpad"])
        BBr = A.buf([32, 16], F32)
        BBi = A.buf([32, 16], F32)
        Kcb = A.buf([32, 128], BF16)
        dd = A.buf([32, 16], F32)
        V(lambda e: e.memset(BBr, 0.0), [], ["BBr"])
        V(lambda e: e.memset(BBi, 0.0), [], ["BBi"])
        for BB, src, stok, btok in ((BBr, bbr, "bbr", "BBr"), (BBi, bbi, "bbi", "BBi")):
            for g2 in range(2):
                pp = slice(g2 * 64, (g2 + 1) * 64)
                V((lambda BB, src, pp, g2: lambda e: e.tensor_copy(
                    out=BB[pp].rearrange("p (gh g2) c -> p gh g2 c", g2=2)[:, :, g2, :], in_=src[pp]))(BB, src, pp, g2),
                  [stok, btok], [btok])
        V(lambda e: e.tensor_tensor(out=dd[0:16], in0=bc(ident_f[0:16, None, 0:16], [16, 32, 16]), in1=bc(dT[0:16, :, None], [16, 32, 16]), op=ALU.mult),
          ["ident_f", "dT"], ["dd"])
        for gq in range(8):
            i, ps, pt = nextps()
            for l in range(4):
                g = gq * 4 + l
                gh = g // 2
                T((lambda ps, g, gh, l: lambda e: e.matmul(ps[0:16, l * 128:(l + 1) * 128], lhsT=BBr[:, g, :],
                                                          rhs=GA_re[:, gh, 0:8, :].rearrange("p t c -> p (t c)"), start=True, stop=False))(ps, g, gh, l),
                  ["BBr", "GA_re"], [pt])
                T((lambda ps, g, gh, l: lambda e: e.matmul(ps[0:16, l * 128:(l + 1) * 128], lhsT=BBi[:, g, :],
                                                          rhs=GA_nim[:, gh, 0:8, :].rearrange("p t c -> p (t c)"), start=False, stop=True))(ps, g, gh, l),
                  ["BBi", "GA_2"], [pt])
            V((lambda ps, gq: lambda e: e.tensor_copy(out=Kcb[0:16, gq * 4:(gq + 1) * 4, :].rearrange("p g n -> p (g n)"), in_=ps[0:16, :]))(ps, gq),
              [pt], ["Kcb"])
            V((lambda ps, gq: lambda e: e.tensor_tensor(out=Kcb[0:16, gq * 4:(gq + 1) * 4, 0:16],
                                                        in0=ps[0:16, :].rearrange("p (g n) -> p g n", g=4)[:, :, 0:16],
                                                        in1=dd[0:16, gq * 4:(gq + 1) * 4, :], op=ALU.add))(ps, gq), [pt, "dd", "Kcb"], ["Kcb"])
        for s8 in range(8):
            DMA((lambda s8: lambda e: e.dma_start(out=M0[16 * s8:16 * s8 + 16, :, 16 * s8:128], in_=Kcb[0:16, :, 0:128 - 16 * s8]))(s8),
                ["Kcb", "M0"], ["M0"], key="m0asm")

        P.fence(fence_fns)
        A.release(m0)
        xa = A.buf([2, D], F32)
        xb = A.buf([8, D], BF16)
        xT = A.buf([8, 1024], BF16)
        u_oct = A.buf([32, 8, 16], BF16)
        U8s = [A.buf([32, 128], BF16) for _ in range(2)]
        wins = [A.buf([4, 2, 128], F32) for _ in range(2)]
        Wts = [A.buf([4, 2, 128], F32) for _ in range(2)]
        Xb = A.buf([16, 2, 129], BF16)
        g_oct = A.buf([8, 512], BF16)
        g_fms = [A.buf([1024], BF16) for _ in range(2)]
        rts = [[A.buf([4, 128], F32) for _ in range(4)] for _ in range(2)]
        ctas = [A.buf([4, 2], F32) for _ in range(2)]
        ctbs = [A.buf([4, 2], F32) for _ in range(2)]
        brow_f = A.buf([512], F32)
        brow_b = A.buf([512], BF16)
        V(lambda e: e.memset(Xb, 0.0), [], ["Xb0", "Xb1", "Xb2", "Xb3"])
        V(lambda e: e.memset(ones_b, 0.0), [], ["ones_b"])
        V(lambda e: e.memset(ones_b[0:1, :], 1.0), ["ones_b"], ["ones_b"])
        V(lambda e: e.memset(brow_b, 0.0), [], ["brow_b"])
        DMA(lambda e: e.dma_start(out=brow_f[0:1, :], in_=b_in[0:512].partition_broadcast(1)), [], ["brow_f"])
        V(lambda e: e.tensor_copy(out=brow_b[0:1, :], in_=brow_f[0:1, :]), ["brow_f", "brow_b"], ["brow_b"])
        print("ARENA top (phase A)", A.top, "of", A.n)
        wt_ctr = [0]
        gfm_ctr = [0]

        def load_xb(stile):
            for hh in range(4):
                r0 = stile * 1024 + hh * 256
                DMA((lambda r0: lambda e: e.dma_start(out=xa, in_=x[r0:r0 + 256, :].rearrange("(s p) d -> p s d", p=128)))(r0), [], ["xa"], key="xald")
                for sl in range(2):
                    sub = hh * 2 + sl
                    S((lambda sl, sub: lambda e: e.copy(out=xb[:, sub, :], in_=xa[:, sl, :]))(sl, sub), ["xa"], ["xb%d" % sub])

        def phaseA_front(stile):
            P.tag = 'A_fe%d' % stile
            t0 = stile * 1024
            U8t = U8s[stile % 2]
            ut = "U8_%d_" % (stile % 2)
            if stile == 0:
                load_xb(0)
            for hh in range(2):
                for k in range(8):
                    i, ps, pt = nextps()
                    pb = psbf(i)
                    for sl in range(4):
                        sub = hh * 4 + sl
                        T((lambda pb, sub, sl, k: lambda e: e.transpose(out=pb[:, sl * 128:(sl + 1) * 128], in_=xb[:, sub, k * 128:(k + 1) * 128],
                                                                        identity=ident_b))(pb, sub, sl, k), ["xb%d" % sub, "ident_b"], [pt])
                    S((lambda pb, k, hh: lambda e: e.copy(out=xT[:, k, hh * 512:(hh + 1) * 512], in_=pb[:, 0:512]))(pb, k, hh), [pt], ["xT%d" % k])
            if stile + 1 < 4:
                load_xb(stile + 1)
            if stile == 0:
                convert_chunk(0)
                convert_chunk(1)
            elif stile == 1:
                convert_chunk(2)
                convert_chunk(3)
            elif stile == 2:
                retile_chunk(0)
                retile_chunk(1)
                retile_chunk(2)
            else:
                retile_chunk(3)
            for s8 in range(8):
                i, ps, pt = nextps()
                for k in range(8):
                    T((lambda ps, s8, k: lambda e: e.matmul(ps, lhsT=xT[:, k, :].rearrange("p (j s) -> p s j", s=8)[:, s8, :],
                                                           rhs=winu[:, k, :], start=(k == 0), stop=False))(ps, s8, k),
                      ["xT%d" % k, "winu"], [pt])
                T((lambda ps: lambda e: e.matmul(ps, lhsT=ones_b, rhs=brow_b, start=False, stop=True))(ps), ["ones_b", "brow_b"], [pt])
                S((lambda ps, s8: lambda e: e.copy(out=u_oct[:, :, s8, :], in_=ps.rearrange("p (g c) -> p g c", c=16)))(ps, s8),
                  [pt], ["u_oct%d" % s8])
            for gq in range(4):
                i, ps, pt = nextps()
                pb = psbf(i)
                for l in range(8):
                    g = gq * 8 + l
                    T((lambda pb, g, l: lambda e: e.transpose(out=pb[:, l * 128:(l + 1) * 128], in_=u_oct[:, g].rearrange("p s c -> p (s c)"),
                                                              identity=ident_b))(pb, g, l), ["u_oct%d" % s for s in range(8)] + ["ident_b"], [pt])
                S((lambda pb, gq, U8t: lambda e: e.copy(out=U8t[:, gq * 8:(gq + 1) * 8, :].rearrange("p g j -> p (g j)"), in_=pb))(pb, gq, U8t),
                  [pt], [ut + "%d" % gq])

        def phaseA_scan(stile):
            P.tag = 'A_V%d' % stile
            U8t = U8s[stile % 2]
            ut = "U8_%d_" % (stile % 2)
            for pair in range(2):
                qs = (2 * pair, 2 * pair + 1)
                ctx = []
                for si, q in enumerate(qs):
                    ir, psr, ptr = nextps()
                    ii, psi_, pti = nextps()
                    for l in range(4):
                        gh = q * 4 + l
                        for reim, ps in ((0, psr), (1, psi_)):
                            for g2 in range(2):
                                g = 2 * gh + g2
                                T((lambda ps, g, reim, l, g2: lambda e: e.matmul(ps[:, l * 128:(l + 1) * 128], lhsT=Hpad[:, g, reim, :], rhs=U8t[:, g, :],
                                                                                 start=(g2 == 0), stop=(g2 == 1)))(ps, g, reim, l, g2),
                                  ["Hpad", ut + "%d" % (g // 8)], [ptr if reim == 0 else pti])
                    gsl = slice(q * 4, (q + 1) * 4)
                    ctx.append(dict(q=q, si=si, gsl=gsl, ptr=ptr, pti=pti,
                                    vr=psr.rearrange("p (g j) -> p g j", g=4), vi=psi_.rearrange("p (g j) -> p g j", g=4),
                                    ec=Ec[:, gsl, :], es=Es[:, gsl, :], win=wins[si], Wt=Wts[si], r=rts[si],
                                    wint="win%d" % si, wtok="Wt%d" % si, rt=["rt%d_%d" % (si, i_) for i_ in range(4)],
                                    cta=ctas[si], ctb=ctbs[si], ctt=["cta%d" % si, "ctb%d" % si], xtk="Xb%d" % q))
                for c in ctx:
                    V((lambda c: lambda e: e.tensor_tensor(out=c["r"][0], in0=c["vr"], in1=c["ec"], op=ALU.mult))(c), [c["ptr"], "Ec"], [c["rt"][0]])
                for c in ctx:
                    V((lambda c: lambda e: e.tensor_tensor(out=c["r"][1], in0=c["vi"], in1=c["es"], op=ALU.mult))(c), [c["pti"], "Es"], [c["rt"][1]])
                for c in ctx:
                    V((lambda c: lambda e: e.tensor_tensor(out=c["r"][2], in0=c["vi"], in1=c["ec"], op=ALU.mult))(c), [c["pti"], "Ec"], [c["rt"][2]])
                for c in ctx:
                    V((lambda c: lambda e: e.tensor_tensor(out=c["r"][3], in0=c["vr"], in1=c["es"], op=ALU.mult))(c), [c["ptr"], "Es"], [c["rt"][3]])
                for c in ctx:
                    V((lambda c: lambda e: e.tensor_tensor(out=c["win"][:, :, 0, :], in0=c["r"][0], in1=c["r"][1], op=ALU.add))(c),
                      [c["rt"][0], c["rt"][1]], [c["wint"]])
                for c in ctx:
                    V((lambda c: lambda e: e.tensor_tensor(out=c["win"][:, :, 1, :], in0=c["r"][2], in1=c["r"][3], op=ALU.subtract))(c),
                      [c["rt"][2], c["rt"][3], c["wint"]], [c["wint"]])
                for l in range(4):
                    for reim in range(2):
                        for c in ctx:
                            gh = c["q"] * 4 + l
                            V((lambda c, gh, l, reim: lambda e: e.tensor_tensor_scan(
                                out=c["Wt"][:, l, reim, :], data0=bc(r8tab[:, gh:gh + 1], [128, 128]), data1=c["win"][:, l, reim, :],
                                initial=wcar[:, gh, reim:reim + 1], op0=ALU.mult, op1=ALU.add))(c, gh, l, reim),
                              [c["wint"], "r8tab", "wcar%d" % c["q"]], [c["wtok"]])
                for c in ctx:
                    Wt = c["Wt"]
                    wl = Wt[:, :, :, 127]
                    V((lambda c, wl: lambda e: e.tensor_tensor(out=c["cta"], in0=C1t[:, c["gsl"], :], in1=wl, op=ALU.mult))(c, wl), ["C1t", c["wtok"]], [c["ctt"][0]])
                for c in ctx:
                    Wt = c["Wt"]
                    wl_sw = bass.AP(tensor=Wt.tensor, offset=Wt[:, :, 1, 127].offset, ap=[list(Wt.ap[0]), list(Wt.ap[1]), [-Wt.ap[2][0], 2]])
                    V((lambda c, wl_sw: lambda e: e.tensor_tensor(out=c["ctb"], in0=C2t[:, c["gsl"], :], in1=wl_sw, op=ALU.mult))(c, wl_sw),
                      ["C2t", c["wtok"]], [c["ctt"][1]])
                for c in ctx:
                    V((lambda c: lambda e: e.tensor_tensor(out=wcar[:, c["gsl"], :], in0=c["cta"], in1=c["ctb"], op=ALU.add))(c), c["ctt"], ["wcar%d" % c["q"]])
                for c in ctx:
                    V((lambda c: lambda e: e.tensor_copy(out=Xb[:, c["gsl"], :, 0], in_=Xb[:, c["gsl"], :, 128]))(c), [c["xtk"]], [c["xtk"]])
                for c in ctx:
                    V((lambda c: lambda e: e.tensor_tensor(out=c["r"][0], in0=c["Wt"][:, :, 0, :], in1=c["ec"], op=ALU.mult))(c), [c["wtok"], "Ec"], [c["rt"][0]])
                for c in ctx:
                    V((lambda c: lambda e: e.tensor_tensor(out=c["r"][1], in0=c["Wt"][:, :, 1, :], in1=c["es"], op=ALU.mult))(c), [c["wtok"], "Es"], [c["rt"][1]])
                for c in ctx:
                    V((lambda c: lambda e: e.tensor_tensor(out=c["r"][2], in0=c["Wt"][:, :, 0, :], in1=c["es"], op=ALU.mult))(c), [c["wtok"], "Es"], [c["rt"][2]])
                for c in ctx:
                    V((lambda c: lambda e: e.tensor_tensor(out=c["r"][3], in0=c["Wt"][:, :, 1, :], in1=c["ec"], op=ALU.mult))(c), [c["wtok"], "Ec"], [c["rt"][3]])
                for c in ctx:
                    V((lambda c: lambda e: e.tensor_tensor(out=Xb[:, c["gsl"], 0, 1:129], in0=c["r"][0], in1=c["r"][1], op=ALU.subtract))(c),
                      [c["rt"][0], c["rt"][1], c["xtk"]], [c["xtk"]])
                for c in ctx:
                    V((lambda c: lambda e: e.tensor_tensor(out=Xb[:, c["gsl"], 1, 1:129], in0=c["r"][2], in1=c["r"][3], op=ALU.add))(c),
                      [c["rt"][2], c["rt"][3], c["xtk"]], [c["xtk"]])

        def phaseA_out(stile):
            P.tag = 'A_Y%d' % stile
            t0 = stile * 1024
            U8t = U8s[stile % 2]
            ut = "U8_%d_" % (stile % 2)
            for gq in range(8):
                i, ps, pt = nextps()
                for l in range(4):
                    g = gq * 4 + l
                    gh = g // 2
                    o_ = ps[:, l * 128:(l + 1) * 128]
                    xtk = "Xb%d" % (gh // 4)
                    T((lambda o_, g, U8t: lambda e: e.matmul(o_, lhsT=U8t[:, g, :], rhs=M0[:, g, :], start=True, stop=False))(o_, g, U8t),
                      [ut + "%d" % (g // 8), "M0"], [pt])
                    T((lambda o_, g, gh: lambda e: e.matmul(o_, lhsT=Xb[:, gh, 0, 0:128], rhs=Gpad[:, g, 0, :], start=False, stop=False))(o_, g, gh),
                      [xtk, "Gpad"], [pt])
                    T((lambda o_, g, gh: lambda e: e.matmul(o_, lhsT=Xb[:, gh, 1, 0:128], rhs=Gpad[:, g, 1, :], start=False, stop=True))(o_, g, gh),
                      [xtk, "Gpad"], [pt])
                S((lambda ps, gq: lambda e: e.activation(
                    out=g_oct[:, :, gq * 64:(gq + 1) * 64].rearrange("p t (g c) -> p t g c", g=4),
                    in_=ps.rearrange("p (g t c) -> p t g c", g=4, t=8), func=AF.Gelu_apprx_tanh))(ps, gq),
                  [pt], ["g_oct"])
            P.tag = 'A_gT%d' % stile
            for cb in range(4):
                i, ps, pt = nextps()
                pb = psbf(i)
                for t8 in range(8):
                    T((lambda pb, t8, cb: lambda e: e.transpose(out=pb[:, t8 * 128:(t8 + 1) * 128], in_=g_oct[:, t8, cb * 128:(cb + 1) * 128],
                                                                identity=ident_b))(pb, t8, cb), ["g_oct", "ident_b"], [pt])
                gs_ = gfm_ctr[0] % 2
                gfm_ctr[0] += 1
                gfm = g_fms[gs_]
                S((lambda pb, gfm: lambda e: e.copy(out=gfm.rearrange("p (j t) -> p t j", t=8),
                                                    in_=pb.rearrange("p (t j) -> p t j", t=8)))(pb, gfm), [pt], ["g_fm%d" % gs_])
                DMA((lambda cb, t0, gfm: lambda e: e.dma_start(out=g_s[cb * 128:(cb + 1) * 128, t0:t0 + 1024], in_=gfm))(cb, t0, gfm),
                    ["g_fm%d" % gs_], ["g_s%d" % stile], key="gst%d" % gs_, eng="gpsimd")

        phaseA_front(0)
        for stile in range(4):
            phaseA_scan(stile)
            if stile + 1 < 4:
                phaseA_front(stile + 1)
            phaseA_out(stile)

        P.fence(fence_fns)
        A.release(mA)
        g1bc = A.buf([D], F32)
        b1bc = None
        g2bc = A.buf([D], F32)
        b2bc = A.buf([D], F32)
        glu_sb = A.buf([4, 512], BF16)
        wso_sb = A.buf([4, D], BF16)
        wco_sb = A.buf([4, D], BF16)
        wo_sb = A.buf([8, D], BF16)
        for dst, src, tok in ((g1bc, ln1_g, "g1bc"), (g2bc, ln2_g, "g2bc"), (b2bc, ln2_b, "b2bc")):
            DMA((lambda dst, src: lambda e: e.dma_start(out=dst, in_=src.partition_broadcast(128)))(dst, src), [], [tok])
        V(lambda e: e.tensor_scalar(out=g1bc, in0=g1bc, scalar1=ALPHA, scalar2=None, op0=ALU.mult), ["g1bc"], ["g1bc"])
        brow2 = A.buf([D], BF16)
        NT = 8
        g_tb = A.buf([4, 512], BF16)
        xb2 = A.buf([4, D], BF16)
        xT2 = A.buf([8, 512], BF16)
        xbx = A.buf([4, D], BF16)
        xTx = A.buf([8, 512], BF16)
        NWS = 3
        wgrp = [A.buf([8, 384], BF16) for _ in range(NWS)]
        NTMP = 8
        tmps = [A.buf([512], F32) for _ in range(NTMP)]
        bz = A.buf([4, 512], BF16)
        merged = A.buf([8, 512], BF16)
        x1 = A.buf([4, D], F32)
        stats = A.buf([4, 2, 6], F32)
        mv = A.buf([4, 2], F32)
        rstd4 = A.buf([4], F32)
        nmr4 = A.buf([4], F32)
        NFS = 3
        ffw = [A.buf([2, 8, 128], BF16) for _ in range(NFS)]
        NDS = 3
        wdb = [A.buf([D], BF16) for _ in range(NDS)]
        hid = A.buf([NFB, 512], BF16)
        brow2f = hid[:, 0:4, :].rearrange("p a b -> p (a b)").bitcast(F32)
        V(lambda e: e.memset(brow2, 0.0), [], ["brow2"])
        DMA(lambda e: e.dma_start(out=brow2f[0:1, :], in_=ln1_b.partition_broadcast(1)), [], ["hid0", "hid1", "hid2", "hid3"])
        V(lambda e: e.tensor_scalar(out=brow2[0:1, :], in0=brow2f[0:1, :], scalar1=ALPHA, scalar2=None, op0=ALU.mult),
          ["hid0", "hid1", "hid2", "hid3", "brow2"], ["brow2"])
        V(lambda e: e.memset(eps_t, LN_EPS), [], ["eps_t"])
        print("ARENA top (phase B)", A.top, "of", A.n)

        tmp_ctr = [0]

        def tmp():
            i = tmp_ctr[0] % NTMP
            tmp_ctr[0] += 1
            return tmps[i], "tmp%d" % i

        wg_ctr = [0]

        def load_wgrp(kind, idx):
            slot = wg_ctr[0] % NWS
            wg_ctr[0] += 1
            if kind == "cv":
                DMA((lambda slot, idx: lambda e: e.dma_start(out=wgrp[slot], in_=wcv_s[idx]))(slot, idx),
                    ["wcv_s"], ["wgrp%d" % slot], key="wgrp%d" % slot)
            else:
                DMA((lambda slot, idx: lambda e: e.dma_start(out=wgrp[slot][:, :, 0:256], in_=wgt_s[idx]))(slot, idx),
                    ["wgt_s"], ["wgrp%d" % slot], key="wgrp%d" % slot)
            return slot

        def ln_stage(gbc, bbc, gtok, btok):
            xt = ["x1_%d" % s_ for s_ in range(4)]
            for sub in range(4):
                for half in range(2):
                    V((lambda sub, half: lambda e: e.bn_stats(out=stats[:, sub, half, :], in_=x1[:, sub, half * 512:(half + 1) * 512]))(sub, half),
                      [xt[sub]], ["stats%d" % sub])
                V((lambda sub: lambda e: e.bn_aggr(out=mv[:, sub, :], in_=stats[:, sub].rearrange("p a b -> p (a b)")))(sub), ["stats%d" % sub], ["mv"])
            def part_b():
                S(lambda e: e.activation(out=rstd4, in_=mv[:, :, 1], func=AF.Sqrt, bias=eps_t, scale=1.0), ["mv", "eps_t"], ["rstd4"])
                V(lambda e: e.reciprocal(out=rstd4, in_=rstd4), ["rstd4"], ["rstd4"])
                for sub in range(4):
                    V((lambda sub: lambda e: e.tensor_scalar(out=x1[:, sub, :], in0=x1[:, sub, :], scalar1=mv[:, sub, 0:1], scalar2=rstd4[:, sub:sub + 1],
                                                             op0=ALU.subtract, op1=ALU.mult))(sub), [xt[sub], "mv", "rstd4"], [xt[sub]])
            deferred = []
            for sub in range(4):
                deferred.append((lambda sub: lambda: V((lambda sub: lambda e: e.tensor_tensor(out=x1[:, sub, :], in0=x1[:, sub, :], in1=gbc, op=ALU.mult))(sub),
                                                       [xt[sub], gtok], [xt[sub]]))(sub))
                deferred.append((lambda sub: lambda: V((lambda sub: lambda e: e.tensor_tensor(out=x1[:, sub, :], in0=x1[:, sub, :], in1=bbc, op=ALU.add))(sub),
                                                       [xt[sub], btok], [xt[sub]]))(sub))
            return [part_b] + deferred

        def load_x_bf16(t):
            for sub in range(4):
                r0 = t * 512 + sub * 128
                DMA((lambda sub, r0: lambda e: e.dma_start(out=xbx[:, sub, :], in_=x[r0:r0 + 128, :]))(sub, r0),
                    [], ["xbx_%d" % sub], key="xld%d" % sub, eng="gpsimd")

        def emit_xT():
            for k in range(8):
                i, ps, pt = nextps()
                pb = psbf(i)
                for sub in range(4):
                    T((lambda pb, sub, k: lambda e: e.transpose(out=pb[:, sub * 128:(sub + 1) * 128], in_=xbx[:, sub, k * 128:(k + 1) * 128],
                                                                identity=ident_b))(pb, sub, k), ["xbx_%d" % sub, "ident_b"], [pt])
                S((lambda pb, k: lambda e: e.copy(out=xTx[:, k, :], in_=pb[:, 0:512]))(pb, k), [pt], ["xTx_%d" % k])

        wq = [("cv", 0, c) for c in range(4)]
        for t_ in range(NT):
            wq += [("gt", t_, d) for d in range(8)]
            if t_ + 1 < NT:
                wq += [("cv", t_ + 1, c) for c in range(4)]
        gslot = {}

        def pump(n):
            for _ in range(n):
                if wq:
                    kd = wq.pop(0)
                    gslot[kd] = load_wgrp(kd[0], kd[2])

        def proj_block(kd, cbl):
            slot_ = gslot[kd]
            i, ps, pt = nextps()
            for k in range(8):
                T((lambda ps, slot_, cbl, k: lambda e: e.matmul(ps, lhsT=wgrp[slot_][:, k, cbl * 128:(cbl + 1) * 128], rhs=xTx[:, k, :],
                                                               start=(k == 0), stop=(k == 7)))(ps, slot_, cbl, k),
                  ["wgrp%d" % slot_, "xTx_%d" % k], [pt])
            return ps, pt

        def glu_stage(t):
            P.tag = 'B%d_glu' % t
            glu_ps = []
            for eb in range(4):
                i, ps, pt = nextps()
                for k in range(4):
                    T((lambda ps, eb, k: lambda e: e.matmul(ps, lhsT=glu_sb[:, k, eb * 128:(eb + 1) * 128], rhs=g_tb[:, k, :],
                                                           start=(k == 0), stop=(k == 3)))(ps, eb, k), ["glu_sb", "g_tb"], [pt])
                glu_ps.append((ps, pt))
            for eb in range(4):
                ps, pt = glu_ps[eb]
                sg, sgt = tmp()
                S((lambda ps, eb, sg: lambda e: e.activation(out=sg, in_=ps, func=AF.Sigmoid, bias=glub_fm[:, eb:eb + 1], scale=1.0))(ps, eb, sg),
                  [pt, "glub_fm"], [sgt])
                V((lambda eb, sg: lambda e: e.tensor_tensor(out=g_tb[:, eb, :], in0=g_tb[:, eb, :], in1=sg, op=ALU.mult))(eb, sg), [sgt, "g_tb"], ["g_tb"])

        def conv_stage(t, cbs):
            P.tag = 'B%d_conv' % t
            for cb in cbs:
                hp, hpt = proj_block(("cv", t, cb), 0)
                cp, cpt = proj_block(("cv", t, cb), 1)
                bp, bpt = proj_block(("cv", t, cb), 2)
                pump(1)
                hsb, hsbt = tmp()
                zt, ztt = tmp()
                S((lambda hp, cb, hsb: lambda e: e.activation(out=hsb, in_=hp, func=AF.Identity, bias=bias_fm[:, 4 + cb:5 + cb], scale=1.0))(hp, cb, hsb),
                  [hpt, "bias_fm"], [hsbt])
                V((lambda cp, cb, hsb: lambda e: e.scalar_tensor_tensor(out=vbuf[:, cb, 2:514], in0=cp, scalar=bias_fm[:, 8 + cb:9 + cb], in1=hsb,
                                                                        op0=ALU.add, op1=ALU.mult))(cp, cb, hsb), [cpt, "bias_fm", hsbt], ["vbuf%d" % cb])
                V((lambda cb, zt: lambda e: e.tensor_scalar(out=zt, in0=vbuf[:, cb, 0:512], scalar1=convw_fm[:, cb:cb + 1], scalar2=None, op0=ALU.mult))(cb, zt),
                  ["vbuf%d" % cb, "convw_fm"], [ztt])
                V((lambda cb, zt: lambda e: e.scalar_tensor_tensor(out=zt, in0=vbuf[:, cb, 1:513], scalar=convw_fm[:, 4 + cb:5 + cb], in1=zt,
                                                                   op0=ALU.mult, op1=ALU.add))(cb, zt), ["vbuf%d" % cb, "convw_fm", ztt], [ztt])
                V((lambda cb, zt: lambda e: e.scalar_tensor_tensor(out=zt, in0=vbuf[:, cb, 2:514], scalar=convw_fm[:, 8 + cb:9 + cb], in1=zt,
                                                                   op0=ALU.mult, op1=ALU.add))(cb, zt), ["vbuf%d" % cb, "convw_fm", ztt], [ztt])
                V((lambda bp, cb, zt: lambda e: e.scalar_tensor_tensor(out=bz[:, cb, :], in0=bp, scalar=bias_fm[:, 12 + cb:13 + cb], in1=zt,
                                                                       op0=ALU.add, op1=ALU.mult))(bp, cb, zt), [bpt, "bias_fm", ztt], ["bz%d" % cb])
                V((lambda cb: lambda e: e.tensor_copy(out=vbuf[:, cb, 0:2], in_=vbuf[:, cb, 512:514]))(cb), ["vbuf%d" % cb], ["vbuf%d" % cb])

        ff_ctr = [0]
        wd_ctr = [0]
        P.tag = 'B0_xT'
        load_x_bf16(0)
        DMA(lambda e: e.dma_start(out=glu_sb, in_=glu_w.rearrange("(k p) n -> p k n", p=128)), [], ["glu_sb"], eng="gpsimd")
        DMA(lambda e: e.dma_start(out=wso_sb, in_=w_ssm_out.rearrange("(k p) n -> p k n", p=128)), [], ["wso_sb"], eng="gpsimd")
        DMA(lambda e: e.dma_start(out=wco_sb, in_=w_conv_out.rearrange("(k p) n -> p k n", p=128)), [], ["wco_sb"], eng="gpsimd")
        DMA(lambda e: e.dma_start(out=wo_sb, in_=w_o.rearrange("(k p) n -> p k n", p=128)), [], ["wo_sb"], eng="gpsimd")
        pump(3)
        emit_xT()
        load_x_bf16(1)
        conv_stage(0, range(4))
        ln2_q = []
        def load_g(t):
            DMA((lambda t: lambda e: e.dma_start(out=g_tb, in_=g_s[:, t * 512:(t + 1) * 512].rearrange("(k p) n -> p k n", p=128)))(t),
                ["g_s%d" % (t // 2)], ["g_tb"], key="g_tb")

        load_g(0)
        glu_stage(0)
        for t in range(NT):
            P.tag = 'B%d_merged' % t
            for db in range(8):
                gap, gapt = proj_block(("gt", t, db), 0)
                gbp, gbpt = proj_block(("gt", t, db), 1)
                pump(1)
                i, yap, yapt = nextps()
                for k in range(4):
                    T((lambda yap, db, k: lambda e: e.matmul(yap, lhsT=wso_sb[:, k, db * 128:(db + 1) * 128], rhs=g_tb[:, k, :],
                                                            start=(k == 0), stop=(k == 3)))(yap, db, k), ["wso_sb", "g_tb"], [yapt])
                i, ybp, ybpt = nextps()
                for k in range(4):
                    T((lambda ybp, db, k: lambda e: e.matmul(ybp, lhsT=wco_sb[:, k, db * 128:(db + 1) * 128], rhs=bz[:, k, :],
                                                            start=(k == 0), stop=(k == 3)))(ybp, db, k), ["wco_sb", "bz%d" % k], [ybpt])
                sa, sat = tmp()
                sb_, sbt = tmp()
                S((lambda gap, db, sa: lambda e: e.activation(out=sa, in_=gap, func=AF.Sigmoid, bias=bias_fm[:, 16 + db:17 + db], scale=1.0))(gap, db, sa),
                  [gapt, "bias_fm"], [sat])
                S((lambda gbp, db, sb_: lambda e: e.activation(out=sb_, in_=gbp, func=AF.Sigmoid, bias=bias_fm[:, 24 + db:25 + db], scale=1.0))(gbp, db, sb_),
                  [gbpt, "bias_fm"], [sbt])
                V((lambda yap, sa: lambda e: e.tensor_tensor(out=sa, in0=yap, in1=sa, op=ALU.mult))(yap, sa), [yapt, sat], [sat])
                V((lambda ybp, sb_: lambda e: e.tensor_tensor(out=sb_, in0=ybp, in1=sb_, op=ALU.mult))(ybp, sb_), [ybpt, sbt], [sbt])
                V((lambda db, sa, sb_: lambda e: e.tensor_tensor(out=merged[:, db, :], in0=sa, in1=sb_, op=ALU.add))(db, sa, sb_), [sat, sbt], ["merged%d" % db])
                for _ in range({1: 1, 2: 2, 3: 2, 4: 2, 5: 2}.get(db, 0)):
                    if ln2_q:
                        ln2_q.pop(0)()
            ffq = list(range(NFB))
            ffslot = {}

            def ff_store(fb):
                fs_ = ffslot[fb]
                DMA((lambda fs_, fb: lambda e: e.dma_start(out=wgu_s[fb], in_=ffw[fs_]))(fs_, fb), ["ffw%d" % fs_], ["wgu%d" % fb], key="ffst%d" % fs_)

            def ffpump(n):
                for _ in range(n):
                    if ffq:
                        fb = ffq.pop(0)
                        fs = ff_ctr[0] % NFS
                        ff_ctr[0] += 1
                        ffslot[fb] = fs
                        if t == 0:
                            if fb >= 1:
                                ff_store(fb - 1)
                            for wh, wsrc in enumerate((w_gate, w_up)):
                                DMA((lambda fs, fb, wh, wsrc: lambda e: e.dma_start(
                                    out=ffw[fs][:, wh], in_=wsrc[:, fb * 128:(fb + 1) * 128].rearrange("(k p) n -> p k n", p=128)))(fs, fb, wh, wsrc),
                                    [], ["ffw%d" % fs], key="ffwc%d" % fs, eng="gpsimd")
                        else:
                            DMA((lambda fs, fb: lambda e: e.dma_start(out=ffw[fs], in_=wgu_s[fb]))(fs, fb), ["wgu%d" % fb], ["ffw%d" % fs],
                                key="ffw%d" % fs)

            ffpump(NFS)
            P.tag = 'B%d_wo' % t
            for sub in range(4):
                r0 = t * 512 + sub * 128
                DMA((lambda sub, r0: lambda e: e.dma_start(out=x1[:, sub, :], in_=x[r0:r0 + 128, :]))(sub, r0), [], ["x1_%d" % sub], key="xres%d" % sub)
            for sub in range(4):
                for half in range(2):
                    i, ps, pt = nextps()
                    for k in range(8):
                        T((lambda ps, sub, half, k: lambda e: e.matmul(ps, lhsT=merged[:, k, sub * 128:(sub + 1) * 128],
                                                                      rhs=wo_sb[:, k, half * 512:(half + 1) * 512],
                                                                      start=(k == 0), stop=(k == 7)))(ps, sub, half, k), ["merged%d" % k, "wo_sb"], [pt])
                    V((lambda ps, sub, half: lambda e: e.scalar_tensor_tensor(
                        out=x1[:, sub, half * 512:(half + 1) * 512], in0=x1[:, sub, half * 512:(half + 1) * 512], scalar=ALPHA, in1=ps,
                        op0=ALU.mult, op1=ALU.add))(ps, sub, half), [pt, "x1_%d" % sub], ["x1_%d" % sub])
            xt_ = ["x1_%d" % s_ for s_ in range(4)]
            for sub in range(4):
                for half in range(2):
                    V((lambda sub, half: lambda e: e.bn_stats(out=stats[:, sub, half, :], in_=x1[:, sub, half * 512:(half + 1) * 512]))(sub, half),
                      [xt_[sub]], ["stats%d" % sub])
                V((lambda sub: lambda e: e.bn_aggr(out=mv[:, sub, :], in_=stats[:, sub].rearrange("p a b -> p (a b)")))(sub), ["stats%d" % sub], ["mv"])
            S(lambda e: e.activation(out=rstd4, in_=mv[:, :, 1], func=AF.Sqrt, bias=eps_t, scale=1.0), ["mv", "eps_t"], ["rstd4"])
            V(lambda e: e.reciprocal(out=rstd4, in_=rstd4), ["rstd4"], ["rstd4"])
            V(lambda e: e.scalar_tensor_tensor(out=nmr4, in0=mv[:, :, 0], scalar=-1.0, in1=rstd4, op0=ALU.mult, op1=ALU.mult), ["mv", "rstd4"], ["nmr4"])
            for sub in range(4):
                S((lambda sub: lambda e: e.activation(out=xb2[:, sub, :], in_=x1[:, sub, :], func=AF.Identity,
                                                      bias=nmr4[:, sub:sub + 1], scale=rstd4[:, sub:sub + 1]))(sub), [xt_[sub], "rstd4", "nmr4"], ["xb2_%d" % sub])
            if t + 1 < NT:
                P.tag = 'B%d_xT' % (t + 1)
                emit_xT()
                if t + 2 < NT:
                    load_x_bf16(t + 2)
                conv_stage(t + 1, (0, 1))
            P.tag = 'B%d_x1T' % t
            for k in range(8):
                i, ps, pt = nextps()
                pb = psbf(i)
                for sub in range(4):
                    T((lambda pb, sub, k: lambda e: e.transpose(out=pb[:, sub * 128:(sub + 1) * 128], in_=xb2[:, sub, k * 128:(k + 1) * 128],
                                                                identity=ident_b))(pb, sub, k), ["xb2_%d" % sub, "ident_b"], [pt])
                S((lambda pb, k: lambda e: e.activation(out=xT2[:, k, :], in_=pb[:, 0:512], func=AF.Identity,
                                                        bias=b1_fm[:, k:k + 1], scale=g1_fm[:, k:k + 1]))(pb, k), [pt, "g1_fm", "b1_fm"], ["xT2_%d" % k])
            if t + 1 < NT:
                conv_stage(t + 1, (2, 3))
            ln1_q = []
            for sub in range(4):
                ln1_q.append((lambda sub: lambda: V((lambda sub: lambda e: e.tensor_scalar(
                    out=x1[:, sub, :], in0=x1[:, sub, :], scalar1=mv[:, sub, 0:1], scalar2=rstd4[:, sub:sub + 1],
                    op0=ALU.subtract, op1=ALU.mult))(sub), [xt_[sub], "mv", "rstd4"], [xt_[sub]]))(sub))
            for sub in range(4):
                ln1_q.append((lambda sub: lambda: V((lambda sub: lambda e: e.tensor_tensor(
                    out=x1[:, sub, :], in0=x1[:, sub, :], in1=g1bc, op=ALU.mult))(sub), [xt_[sub], "g1bc"], [xt_[sub]]))(sub))
            if t + 1 < NT:
                load_g(t + 1)
            P.tag = 'B%d_ffn' % t
            wdq = list(range(NFB))
            wdslot = {}

            def wdpump(n):
                for _ in range(n):
                    if wdq:
                        fb = wdq.pop(0)
                        ds_ = wd_ctr[0] % NDS
                        wd_ctr[0] += 1
                        wdslot[fb] = ds_
                        DMA((lambda ds_, fb: lambda e: e.dma_start(out=wdb[ds_], in_=wd_s[fb * 128:(fb + 1) * 128, :]))(ds_, fb),
                            ["wd_s"], ["wdb%d" % ds_], key="wdb%d" % ds_)

            for fb in range(NFB):
                fs = ffslot[fb]
                i, gp, gpt = nextps()
                for k in range(8):
                    T((lambda gp, fs, k: lambda e: e.matmul(gp, lhsT=ffw[fs][:, 0, k, :], rhs=xT2[:, k, :],
                                                           start=(k == 0), stop=(k == 7)))(gp, fs, k), ["ffw%d" % fs, "xT2_%d" % k], [gpt])
                i, up, upt = nextps()
                for k in range(8):
                    T((lambda up, fs, k: lambda e: e.matmul(up, lhsT=ffw[fs][:, 1, k, :], rhs=xT2[:, k, :],
                                                           start=(k == 0), stop=(k == 7)))(up, fs, k), ["ffw%d" % fs, "xT2_%d" % k], [upt])
                ffpump(1)
                if fb == (0 if t == 0 else NFB - 3):
                    wdpump(NDS)
                sgl, sglt = tmp()
                S((lambda gp, sgl: lambda e: e.activation(out=sgl, in_=gp, func=AF.Silu))(gp, sgl), [gpt], [sglt])
                V((lambda up, fb, sgl: lambda e: e.tensor_tensor(out=hid[:, fb, :], in0=up, in1=sgl, op=ALU.mult))(up, fb, sgl), [upt, sglt], ["hid%d" % fb])
                if ln1_q:
                    ln1_q.pop(0)()
            if t == 0:
                ff_store(NFB - 1)
            if t + 1 < NT:
                glu_stage(t + 1)
            P.tag = 'B%d_down' % t
            banks = {}
            for sub in range(4):
                for half in range(2):
                    banks[(sub, half)] = nextps()
            for sub in range(4):
                for half in range(2):
                    i, ps, pt = banks[(sub, half)]
                    T((lambda ps, half: lambda e: e.matmul(ps, lhsT=ones_b, rhs=brow2[:, half * 512:(half + 1) * 512], start=True, stop=False))(ps, half),
                      ["ones_b", "brow2"], [pt])
            for fb in range(NFB):
                ds_ = wdslot[fb]
                for sub in range(4):
                    for half in range(2):
                        i, ps, pt = banks[(sub, half)]
                        T((lambda ps, ds_, fb, sub, half: lambda e: e.matmul(ps, lhsT=hid[:, fb, sub * 128:(sub + 1) * 128],
                                                                            rhs=wdb[ds_][:, half * 512:(half + 1) * 512],
                                                                            start=False, stop=(fb == NFB - 1)))(ps, ds_, fb, sub, half),
                          ["hid%d" % fb, "wdb%d" % ds_], [pt])
                wdpump(1)
            for sub in range(4):
                for half in range(2):
                    i, ps, pt = banks[(sub, half)]
                    V((lambda ps, sub, half: lambda e: e.tensor_tensor(
                        out=x1[:, sub, half * 512:(half + 1) * 512], in0=ps, in1=x1[:, sub, half * 512:(half + 1) * 512],
                        op=ALU.add))(ps, sub, half), [pt, "x1_%d" % sub], ["x1_%d" % sub])
            dfr = ln_stage(g2bc, b2bc, "g2bc", "b2bc")
            ln2_q = [dfr[0]]
            for sub in range(4):
                r0 = t * 512 + sub * 128
                ln2_q.append(dfr[1 + 2 * sub])
                ln2_q.append((lambda sub, r0, f: lambda: (f(), DMA((lambda sub, r0: lambda e: e.dma_start(out=out[r0:r0 + 128, :], in_=x1[:, sub, :]))(sub, r0),
                                                                     ["x1_%d" % sub], [], key="ost%d" % sub, eng="gpsimd")))(sub, r0, dfr[2 + 2 * sub]))
            if t == NT - 1:
                for f in ln2_q:
                    f()
                ln2_q = []

        P.emit()
        import os
        if os.environ.get('KDUMP_TAGS'):
            import json
            json.dump({e: [o.tag for o in P.ops[e] if o.fn is not None] for e in ENGS}, open(os.environ['KDUMP_TAGS'], 'w'))
    return nc


_NC_CACHE = {}


def kernel(**inputs):
    if "nc" not in _NC_CACHE:
        _NC_CACHE["nc"] = build_nc()
    nc = _NC_CACHE["nc"]
    x = np.ascontiguousarray(inputs["x"], dtype=np.float32)
    shared = {}
    for k, v in inputs.items():
        if k == "x":
            continue
        shared[k] = np.ascontiguousarray(np.asarray(v, dtype=np.float32)[0])
    in_maps = []
    for c in range(NCORES):
        m = dict(shared)
        m["x"] = x[c]
        in_maps.append(m)
    res = run_bass_kernel_spmd(nc, in_maps, core_ids=list(range(NCORES)))
    outs = [np.asarray(res.results[c]["out"], dtype=np.float32) for c in range(NCORES)]
    return np.stack(outs, axis=0)
```
